# Optimizing a Trainium2 kernel written in Bass

```python
import jax, jax.numpy as jnp
from jax import lax
import numpy as np

D_MODEL = 2048
BATCH = 2
SEQ = 4096
DEPTH = 1
DEC_BATCH = 8
DEC_SEQ = 1
PAST_LEN = 16384
PAGE_SIZE = 128

HEAD_DIM = 128
N_MIX_HEADS = D_MODEL // HEAD_DIM
SB_HEADS = N_MIX_HEADS // 2
GDN_HEADS = N_MIX_HEADS - SB_HEADS
SB_WIDTH = SB_HEADS * HEAD_DIM
GDN_WIDTH = GDN_HEADS * HEAD_DIM
MIX_WIDTH = SB_WIDTH + GDN_WIDTH
SB_BLOCK = 128
SB_SCALE = HEAD_DIM ** -0.5
SB_LOGIT_BIAS_INIT = -8.0
GDN_CONV = 4
GDN_CONV_CH = 3 * GDN_WIDTH
GDN_CHUNK = 64
D_FF = ((8 * D_MODEL // 3 + 127) // 128) * 128
FFN_CONV = 3
PLE_DIM = 256
EPS = 1e-6

O_SB_Q = 0
O_SB_K = O_SB_Q + SB_WIDTH
O_SB_V = O_SB_K + SB_WIDTH
O_GDN_QKV = O_SB_V + SB_WIDTH
O_GDN_Z = O_GDN_QKV + GDN_CONV_CH
O_GDN_A = O_GDN_Z + GDN_WIDTH
O_GDN_B = O_GDN_A + GDN_HEADS
IN_COLS = O_GDN_B + GDN_HEADS

kernel_name = 'hymba_stickbreak_gdn_convffn_step'


def rms_norm(x, w):
    x32 = x.astype(jnp.float32)
    y = x32 * lax.rsqrt(jnp.mean(x32 * x32, axis=-1, keepdims=True) + EPS)
    return (y * w.astype(jnp.float32)).astype(x.dtype)


def l2_normalize(x):
    return x * lax.rsqrt(jnp.sum(x * x, axis=-1, keepdims=True) + 1e-6)


def causal_dwconv(xh, w):
    width = w.shape[0]
    t = xh.shape[1] - (width - 1)
    y = xh[:, 0:t] * w[0]
    for j in range(1, width):
        y = y + xh[:, j:j + t] * w[j]
    return y.astype(xh.dtype)


def sb_block(q_blk, k, v, logit_bias, q_start):
    tq, tk = q_blk.shape[1], k.shape[1]
    z = (jnp.einsum('bqhd,bkhd->bhqk', q_blk, k).astype(jnp.float32) * SB_SCALE
         + logit_bias.astype(jnp.float32)[None, :, None, None])
    qpos = q_start + jnp.arange(tq)
    kpos = jnp.arange(tk)
    valid = kpos[None, :] < qpos[:, None]
    log_beta = jax.nn.log_sigmoid(z)
    log_1m_beta = jnp.where(valid, log_beta - z, 0.0)
    between = lax.cumsum(log_1m_beta, axis=3, reverse=True) - log_1m_beta
    a = jnp.where(valid, jnp.exp(log_beta + between), 0.0)
    return jnp.einsum('bhqk,bkhd->bqhd', a.astype(v.dtype), v)


def sb_attention(q, k, v, logit_bias, q_offset):
    tq, tk = q.shape[1], k.shape[1]
    outs = []
    for start in range(0, tq, SB_BLOCK):
        stop = min(start + SB_BLOCK, tq)
        kend = min(tk, q_offset + stop)
        outs.append(sb_block(q[:, start:stop], k[:, :kend], v[:, :kend], logit_bias,
                             q_offset + start))
    return jnp.concatenate(outs, axis=1)


def to_chunks(x):
    b, tp, h = x.shape[:3]
    x = x.reshape((b, tp // GDN_CHUNK, GDN_CHUNK, h) + x.shape[3:])
    return jnp.moveaxis(x, (1, 3), (0, 2))


def gdn_chunked(q, k, v, g, beta, s0):
    b, t, h, dv = v.shape
    pad = (-t) % GDN_CHUNK
    if pad:
        padt = lambda a: jnp.pad(a, [(0, 0), (0, pad)] + [(0, 0)] * (a.ndim - 2))
        q, k, v, g, beta = padt(q), padt(k), padt(v), padt(g), padt(beta)
    q, k, v, g, beta = to_chunks(q), to_chunks(k), to_chunks(v), to_chunks(g), to_chunks(beta)
    gc = jnp.cumsum(g, axis=-1)
    kb = k * beta[..., None]
    vb = v * beta[..., None]
    idx = jnp.arange(GDN_CHUNK)
    tri = idx[:, None] >= idx[None, :]
    strict = idx[:, None] > idx[None, :]
    diff = gc[..., :, None] - gc[..., None, :]
    decay = jnp.where(tri, jnp.exp(jnp.where(tri, diff, 0.0)), 0.0)
    low = jnp.where(strict, jnp.einsum('nbhcd,nbhsd->nbhcs', kb, k) * decay, 0.0)
    unit_low = low + jnp.eye(GDN_CHUNK, dtype=low.dtype)
    rhs = jnp.concatenate([vb, kb * jnp.exp(gc)[..., None]], axis=-1)
    sol = lax.linalg.triangular_solve(unit_low, rhs, left_side=True, lower=True,
                                      unit_diagonal=True)
    u, w = sol[..., :dv], sol[..., dv:]
    qk = jnp.einsum('nbhcd,nbhsd->nbhcs', q, k) * decay

    def step(s, inp):
        q_i, k_i, u_i, w_i, qk_i, gc_i = inp
        v_new = u_i - jnp.einsum('bhcd,bhde->bhce', w_i, s)
        o_i = (jnp.einsum('bhcd,bhde->bhce', q_i * jnp.exp(gc_i)[..., None], s)
               + jnp.einsum('bhcs,bhse->bhce', qk_i, v_new))
        g_last = gc_i[..., -1:]
        s = (s * jnp.exp(g_last)[..., None]
             + jnp.einsum('bhcd,bhce->bhde', k_i * jnp.exp(g_last - gc_i)[..., None], v_new))
        return s, o_i

    s_final, o = lax.scan(step, s0, (q, k, u, w, qk, gc))
    o = jnp.moveaxis(o, (0, 2), (1, 3)).reshape(b, -1, h, dv)[:, :t]
    return o, s_final


def layer_forward(x, p, k_past, v_past, gdn_conv_hist, gdn_s0, ffn_conv_hist, q_offset,
                  attn_norm, w_in, sb_logit_bias, sb_out_norm, gdn_conv_w, gdn_a_log,
                  gdn_dt_bias, gdn_out_norm, w_out, ffn_norm, w_ffn_gate, w_ffn_up,
                  ffn_conv_w, w_ffn_down, ple_norm, w_ple_gate, w_ple_proj):
    b, t, _ = x.shape
    f32 = jnp.float32
    a = rms_norm(x, attn_norm)
    proj = a @ w_in

    sb_q = proj[..., O_SB_Q:O_SB_Q + SB_WIDTH].reshape(b, t, SB_HEADS, HEAD_DIM)
    sb_k = proj[..., O_SB_K:O_SB_K + SB_WIDTH].reshape(b, t, SB_HEADS, HEAD_DIM)
    sb_v = proj[..., O_SB_V:O_SB_V + SB_WIDTH].reshape(b, t, SB_HEADS, HEAD_DIM)
    if k_past is None:
        k_all, v_all = sb_k, sb_v
    else:
        k_all = jnp.concatenate([k_past.astype(sb_k.dtype), sb_k], axis=1)
        v_all = jnp.concatenate([v_past.astype(sb_v.dtype), sb_v], axis=1)
    o_sb = rms_norm(sb_attention(sb_q, k_all, v_all, sb_logit_bias, q_offset), sb_out_norm)

    conv_in = proj[..., O_GDN_QKV:O_GDN_QKV + GDN_CONV_CH]
    hist = jnp.concatenate([gdn_conv_hist.astype(conv_in.dtype), conv_in], axis=1)
    new_gdn_conv = hist[:, hist.shape[1] - (GDN_CONV - 1):]
    c = jax.nn.silu(causal_dwconv(hist, gdn_conv_w).astype(f32))
    gq = c[..., 0:GDN_WIDTH].reshape(b, t, GDN_HEADS, HEAD_DIM)
    gk = c[..., GDN_WIDTH:2 * GDN_WIDTH].reshape(b, t, GDN_HEADS, HEAD_DIM)
    gv = c[..., 2 * GDN_WIDTH:3 * GDN_WIDTH].reshape(b, t, GDN_HEADS, HEAD_DIM)
    gq = l2_normalize(gq) * (HEAD_DIM ** -0.5)
    gk = l2_normalize(gk)
    a_in = proj[..., O_GDN_A:O_GDN_A + GDN_HEADS].astype(f32)
    g = -jnp.exp(gdn_a_log.astype(f32)) * jax.nn.softplus(a_in + gdn_dt_bias.astype(f32))
    beta = jax.nn.sigmoid(proj[..., O_GDN_B:O_GDN_B + GDN_HEADS].astype(f32))
    o_gdn, s_new = gdn_chunked(gq, gk, gv, g, beta, gdn_s0.astype(f32))
    z = proj[..., O_GDN_Z:O_GDN_Z + GDN_WIDTH].reshape(b, t, GDN_HEADS, HEAD_DIM).astype(f32)
    o_gdn = rms_norm(o_gdn, gdn_out_norm) * jax.nn.silu(z)

    mix = jnp.concatenate([o_sb.reshape(b, t, SB_WIDTH).astype(x.dtype),
                           o_gdn.reshape(b, t, GDN_WIDTH).astype(x.dtype)], axis=-1)
    h = x + mix @ w_out

    f = rms_norm(h, ffn_norm)
    gate_pre = f @ w_ffn_gate
    ghist = jnp.concatenate([ffn_conv_hist.astype(gate_pre.dtype), gate_pre], axis=1)
    new_ffn_conv = ghist[:, ghist.shape[1] - (FFN_CONV - 1):]
    gate = causal_dwconv(ghist, ffn_conv_w)
    h = h + (jax.nn.silu(gate) * (f @ w_ffn_up)) @ w_ffn_down

    ple_gate = jax.nn.sigmoid(rms_norm(h, ple_norm) @ w_ple_gate)
    h = h + (p @ w_ple_proj) * ple_gate
    return h, sb_k, sb_v, new_gdn_conv, s_new.astype(gdn_s0.dtype), new_ffn_conv


def setup_inputs(seed: int = 0) -> dict:
    key = jax.random.key(seed)
    ks = jax.random.split(key, 32)
    f32 = jnp.float32
    n_pages = PAST_LEN // PAGE_SIZE
    n_pool = (DEC_BATCH * n_pages * 5) // 4

    def nrm(k, shape, scale=1.0):
        return jax.random.normal(k, shape, f32) * scale

    def gain(k, shape):
        return 1.0 + 0.01 * jax.random.normal(k, shape, f32)

    page_table = jax.random.permutation(ks[0], n_pool)[:DEC_BATCH * n_pages]
    page_table = page_table.reshape(DEC_BATCH, n_pages).astype(jnp.int32)
    dt = jnp.exp(jax.random.uniform(ks[1], (DEPTH, GDN_HEADS), f32,
                                    float(np.log(1e-3)), float(np.log(1e-1))))
    gdn_dt_bias = dt + jnp.log(-jnp.expm1(-dt))
    gdn_a_log = jnp.log(jax.random.uniform(ks[2], (DEPTH, GDN_HEADS), f32, 1.0, 16.0))
    return {
        'x_prompt': nrm(ks[3], (BATCH, SEQ, D_MODEL)),
        'x_sample': nrm(ks[4], (DEC_BATCH, DEC_SEQ, D_MODEL)),
        'cache_sb_k': nrm(ks[5], (DEPTH, n_pool, PAGE_SIZE, SB_HEADS, HEAD_DIM)),
        'cache_sb_v': nrm(ks[6], (DEPTH, n_pool, PAGE_SIZE, SB_HEADS, HEAD_DIM)),
        'page_table': page_table,
        'state_gdn_conv': nrm(ks[7], (DEPTH, DEC_BATCH, GDN_CONV - 1, GDN_CONV_CH)),
        'state_gdn_rec': nrm(ks[8], (DEPTH, DEC_BATCH, GDN_HEADS, HEAD_DIM, HEAD_DIM), 0.1),
        'state_ffn_conv': nrm(ks[9], (DEPTH, DEC_BATCH, FFN_CONV - 1, D_FF)),
        'p_prompt': nrm(ks[10], (DEPTH, BATCH, SEQ, PLE_DIM)),
        'p_sample': nrm(ks[11], (DEPTH, DEC_BATCH, DEC_SEQ, PLE_DIM)),
        'attn_norm': gain(ks[12], (DEPTH, D_MODEL)),
        'w_in': nrm(ks[13], (DEPTH, D_MODEL, IN_COLS), D_MODEL ** -0.5),
        'sb_logit_bias': SB_LOGIT_BIAS_INIT + 0.1 * jax.random.normal(ks[27], (DEPTH, SB_HEADS), f32),
        'sb_out_norm': gain(ks[14], (DEPTH, HEAD_DIM)),
        'gdn_conv_w': nrm(ks[15], (DEPTH, GDN_CONV, GDN_CONV_CH), GDN_CONV ** -0.5),
        'gdn_a_log': gdn_a_log,
        'gdn_dt_bias': gdn_dt_bias,
        'gdn_out_norm': gain(ks[16], (DEPTH, HEAD_DIM)),
        'w_out': nrm(ks[17], (DEPTH, MIX_WIDTH, D_MODEL), MIX_WIDTH ** -0.5),
        'ffn_norm': gain(ks[18], (DEPTH, D_MODEL)),
        'w_ffn_gate': nrm(ks[19], (DEPTH, D_MODEL, D_FF), D_MODEL ** -0.5),
        'w_ffn_up': nrm(ks[20], (DEPTH, D_MODEL, D_FF), D_MODEL ** -0.5),
        'ffn_conv_w': nrm(ks[21], (DEPTH, FFN_CONV, D_FF), FFN_CONV ** -0.5),
        'w_ffn_down': nrm(ks[22], (DEPTH, D_FF, D_MODEL), D_FF ** -0.5),
        'ple_norm': gain(ks[23], (DEPTH, D_MODEL)),
        'w_ple_gate': nrm(ks[24], (DEPTH, D_MODEL, D_MODEL), D_MODEL ** -0.5),
        'w_ple_proj': nrm(ks[25], (DEPTH, PLE_DIM, D_MODEL), PLE_DIM ** -0.5),
        'final_norm': gain(ks[26], (D_MODEL,)),
    }


def reference(x_prompt, x_sample, cache_sb_k, cache_sb_v, page_table, state_gdn_conv,
              state_gdn_rec, state_ffn_conv, p_prompt, p_sample, attn_norm, w_in,
              sb_logit_bias, sb_out_norm, gdn_conv_w, gdn_a_log, gdn_dt_bias, gdn_out_norm,
              w_out, ffn_norm, w_ffn_gate, w_ffn_up, ffn_conv_w, w_ffn_down, ple_norm,
              w_ple_gate, w_ple_proj, final_norm):
    bp = x_prompt.shape[0]
    bs = x_sample.shape[0]
    past_len = page_table.shape[1] * cache_sb_k.shape[2]
    hp, hs = x_prompt, x_sample
    out_p = ([], [], [], [], [])
    out_s = ([], [], [], [], [])
    for i in range(DEPTH):
        lw = (attn_norm[i], w_in[i], sb_logit_bias[i], sb_out_norm[i], gdn_conv_w[i],
              gdn_a_log[i], gdn_dt_bias[i], gdn_out_norm[i], w_out[i], ffn_norm[i],
              w_ffn_gate[i], w_ffn_up[i], ffn_conv_w[i], w_ffn_down[i], ple_norm[i],
              w_ple_gate[i], w_ple_proj[i])
        hp, *st_p = layer_forward(
            hp, p_prompt[i], None, None,
            jnp.zeros((bp, GDN_CONV - 1, GDN_CONV_CH), hp.dtype),
            jnp.zeros((bp, GDN_HEADS, HEAD_DIM, HEAD_DIM), hp.dtype),
            jnp.zeros((bp, FFN_CONV - 1, D_FF), hp.dtype), 0, *lw)
        for lst, s in zip(out_p, st_p):
            lst.append(s)
        k_past = cache_sb_k[i][page_table].reshape(bs, past_len, SB_HEADS, HEAD_DIM)
        v_past = cache_sb_v[i][page_table].reshape(bs, past_len, SB_HEADS, HEAD_DIM)
        hs, *st_s = layer_forward(
            hs, p_sample[i], k_past, v_past, state_gdn_conv[i], state_gdn_rec[i],
            state_ffn_conv[i], past_len, *lw)
        for lst, s in zip(out_s, st_s):
            lst.append(s)
    y_prompt = rms_norm(hp, final_norm)
    y_sample = rms_norm(hs, final_norm)
    sb_k_p, sb_v_p, gdn_conv_p, gdn_rec_p, ffn_conv_p = [jnp.stack(l, axis=0) for l in out_p]
    sb_k_s, sb_v_s, gdn_conv_s, gdn_rec_s, ffn_conv_s = [jnp.stack(l, axis=0) for l in out_s]
    return (y_prompt, y_sample, sb_k_p, sb_v_p, gdn_conv_p, gdn_rec_p, ffn_conv_p,
            sb_k_s, sb_v_s, gdn_conv_s, gdn_rec_s, ffn_conv_s)
```

```python
from concourse.bass_utils import run_bass_kernel_spmd
import contextlib
import numpy as np
import concourse.bass as bass
import concourse.mybir as mybir

F32 = mybir.dt.float32
BF16 = mybir.dt.bfloat16
I32 = mybir.dt.int32
AF = mybir.ActivationFunctionType
ALU = mybir.AluOpType
AX = mybir.AxisListType


class Res:
    __slots__ = ("name", "w", "r", "dsem", "dval", "excl", "dq")

    def __init__(self, name, inherit=None):
        self.name = name
        self.w = {}
        self.r = dict(inherit) if inherit else {}
        self.dsem = None
        self.dval = 0
        self.excl = False
        self.dq = None


class Ctx:
    def __init__(self, nc):
        self.nc = nc
        self.eng = {}
        for name, h in (("pe", nc.tensor), ("act", nc.scalar), ("dve", nc.vector),
                        ("pool", nc.gpsimd), ("sp", nc.sync)):
            sem = nc.alloc_semaphore("sem_" + name)
            self.eng[name] = dict(h=h, sem=sem, cnt=0, seen={})
        self.freed = {}
        self.out_toks = {}
        self.nres = 0
        self.dsem_pool = {}
        self.par = None

    def res(self, name=None):
        self.nres += 1
        return Res(name or f"r{self.nres}", inherit=self.freed)

    def sb(self, stack, name, shape, dt):
        t = stack.enter_context(self.nc.sbuf_tensor("sb_" + name, list(shape), dt))
        return t, self.res(name)

    def ps(self, stack, name, shape, dt):
        t = stack.enter_context(self.nc.psum_tensor("pp_" + name, list(shape), dt))
        r = self.res(name)
        r.excl = True
        return t, r

    def free(self, *ress):
        for r in ress:
            for d in (r.w, r.r):
                for s, v in d.items():
                    if self.freed.get(s, 0) < v:
                        self.freed[s] = v
            if r.dsem is not None:
                self.dsem_pool.setdefault(r.dq, []).append((r.dsem, r.dval))
                r.dsem = None

    def _wait(self, e, toks):
        E = self.eng[e]
        for sem, val in toks.items():
            if E["seen"].get(sem, 0) < val:
                E["h"].wait_ge(sem, val)
                E["seen"][sem] = val

    @staticmethod
    def _merge(dst, src):
        for s, v in src.items():
            if dst.get(s, 0) < v:
                dst[s] = v

    def _deps(self, e, reads, writes, skip_sem=None):
        toks = {}
        own = self.eng[e]["sem"]
        for r in reads:
            self._merge(toks, r.w)
            if r.excl:
                self._merge(toks, {s_: v_ for s_, v_ in r.r.items() if s_ != own})
        for w in writes:
            self._merge(toks, w.w)
            self._merge(toks, w.r)
        if skip_sem is not None:
            toks.pop(skip_sem, None)
        return toks

    def op(self, e, fn, reads=(), writes=(), inc=True):
        E = self.eng[e]
        if self.par is not None:
            inc = True
        toks = self._deps(e, reads, writes, skip_sem=E["sem"] if e == "pe" else None)
        self._wait(e, toks)
        ins = fn(E["h"])
        if inc:
            E["cnt"] += 1
            ins.then_inc(E["sem"], 1)
            val = E["cnt"]
        else:
            val = E["cnt"] + 1
        for r in reads:
            if r.r.get(E["sem"], 0) < val:
                r.r[E["sem"]] = val
        for w in writes:
            w.w = {E["sem"]: val}
            w.r = {}
        if self.par is not None:
            self.par.switch()
        return ins

    def dma(self, q, out, in_, reads=(), writes=(), part=False, is_out=False, owner=None, **kw):
        E = self.eng[q]
        W = owner if owner is not None else writes[0]
        if W.dsem is None:
            W.dq = "pool" if q == "pool" else "hw"
            if self.dsem_pool.get(W.dq):
                W.dsem, W.dval = self.dsem_pool[W.dq].pop()
            else:
                W.dsem = self.nc.alloc_semaphore("d_" + W.name)
                W.dval = 0
        toks = self._deps(q, reads, writes, skip_sem=W.dsem if part else None)
        self._wait(q, toks)
        ins = E["h"].dma_start(out=out, in_=in_, **kw)
        W.dval += 16
        ins.then_inc(W.dsem, 16)
        for r in reads:
            if r.r.get(W.dsem, 0) < W.dval:
                r.r[W.dsem] = W.dval
        for w in writes:
            if part:
                w.w[W.dsem] = W.dval
            else:
                w.w = {W.dsem: W.dval}
                w.r = {}
        if is_out:
            if self.out_toks.get(W.dsem, 0) < W.dval:
                self.out_toks[W.dsem] = W.dval
        return ins

    def indirect(self, out, in_, in_offset, reads=(), writes=(), part=False):
        E = self.eng["pool"]
        W = writes[0]
        if W.dsem is None:
            W.dq = "pool"
            if self.dsem_pool.get("pool"):
                W.dsem, W.dval = self.dsem_pool["pool"].pop()
            else:
                W.dsem = self.nc.alloc_semaphore("d_" + W.name)
                W.dval = 0
        toks = self._deps("pool", reads, writes, skip_sem=W.dsem if part else None)
        self._wait("pool", toks)
        ins = E["h"].indirect_dma_start(out=out, out_offset=None, in_=in_, in_offset=in_offset)
        W.dval += 16
        ins.then_inc(W.dsem, 16)
        for r in reads:
            if r.r.get(W.dsem, 0) < W.dval:
                r.r[W.dsem] = W.dval
        for w in writes:
            if part:
                w.w[W.dsem] = W.dval
            else:
                w.w = {W.dsem: W.dval}
                w.r = {}
        return ins

    def finish(self):
        toks = dict(self.out_toks)
        for name, E in self.eng.items():
            if name != "sp" and E["cnt"] > 0:
                toks[E["sem"]] = E["cnt"]
        self._wait("sp", toks)


class Par:
    def __init__(self, C):
        import threading
        self.C = C
        self.th = threading
        self.cv = threading.Condition()
        self.turn = 0
        self.alive = []
        self.ids = {}
        self.err = None

    def cur(self):
        return self.ids.get(self.th.get_ident(), None)

    def _next(self, i):
        n = len(self.alive)
        for k in range(1, n + 1):
            j = (i + k) % n
            if self.alive[j]:
                return j
        return -1

    def switch(self):
        i = self.cur()
        if i is None:
            return
        with self.cv:
            self.turn = self._next(i)
            self.cv.notify_all()
            while self.turn != i and self.err is None:
                self.cv.wait()
            if self.err is not None and self.turn != i:
                raise RuntimeError("peer failed")

    def idle(self):
        self.switch()

    def run(self, fns):
        self.alive = [True] * len(fns)
        self.turn = 0

        def body(i, fn):
            self.ids[self.th.get_ident()] = i
            try:
                with self.cv:
                    while self.turn != i and self.err is None:
                        self.cv.wait()
                if self.err is None:
                    fn()
            except BaseException as e:
                if self.err is None:
                    self.err = e
            finally:
                with self.cv:
                    self.alive[i] = False
                    self.turn = self._next(i)
                    self.cv.notify_all()
        ts = [self.th.Thread(target=body, args=(i, f)) for i, f in enumerate(fns)]
        self.C.par = self
        for t in ts:
            t.start()
        for t in ts:
            t.join()
        self.C.par = None
        if self.err is not None:
            raise self.err

D = 2048
KC = 16
DFF = 5504
NJ = 43
PLE = 256
EPS = 1e-6


def tiles_of(n, step=512):
    return [(s, min(step, n - s)) for s in range(0, n, step)]


def build_l2(NM=1024):
    NT = NM + 3
    TL = tiles_of(NT)
    nc = bass.Bass("TRN2", target_bir_lowering=False)
    dt_in = lambda name, shape, dt=F32: nc.dram_tensor(name, list(shape), dt, kind="ExternalInput")
    dt_out = lambda name, shape, dt=F32: nc.dram_tensor(name, list(shape), dt, kind="ExternalOutput")
    xT_d = dt_in("xT", [128, KC, NT])
    mixT_d = dt_in("mixT", [128, KC, NT])
    pT_d = dt_in("pT", [128, 2, NT])
    fh_d = dt_in("fhist", [128, NJ, 2])
    w_out_d = dt_in("w_out", [4, 128, KC, 512])
    w_gate_d = dt_in("w_gate", [22, 128, KC, 256])
    w_up_d = dt_in("w_up", [22, 128, KC, 256])
    w_down_d = dt_in("w_down", [KC, 128, NJ, 128])
    w_pg_d = dt_in("w_pg", [4, 128, KC, 512])
    w_pp_d = dt_in("w_pp", [128, 2, D])
    nrm_d = dt_in("nrm", [128, 3, KC])
    fcw_d = dt_in("fcw", [128, NJ, 3])
    yT_d = dt_out("yT", [128, KC, NM + 1])
    fc_d = dt_out("fcnew", [128, NJ, 4])
    hs_d = nc.dram_tensor("h_scratch", [128, KC, NT], F32)
    h2s_d = nc.dram_tensor("h2_scratch", [128, KC, NT], F32)

    C = Ctx(nc)
    hs_res = C.res("hs")
    with contextlib.ExitStack() as S0:
        S0.enter_context(nc.allow_low_precision("bf16 matmul operands, fp32 accumulation"))
        ones_bf, ones_r = C.sb(S0, "ones_bf", [128, 128], BF16)
        C.op("pool", lambda g: g.memset(ones_bf[:, :], 1.0), writes=[ones_r])
        nrm, nrm_r = C.sb(S0, "nrm", [128, 3, KC], F32)
        C.dma("sp", nrm[:, :, :], nrm_d[:, :, :], writes=[nrm_r])
        fcw, fcw_r = C.sb(S0, "fcw", [128, NJ, 3], F32)
        C.dma("sp", fcw[:, :, :], fcw_d[:, :, :], writes=[fcw_r])
        fh, fh_r = C.sb(S0, "fh", [128, NJ, 2], F32)
        C.dma("sp", fh[:, :, :], fh_d[:, :, :], writes=[fh_r])
        fcs, fcs_r = C.sb(S0, "fcs", [128, NJ, 4], F32)
        C.op("pool", lambda p: p.tensor_copy(fcs[:, :, 3], fh[:, :, 1]), reads=[fh_r], writes=[fcs_r])
        rstd, rstd_r = C.sb(S0, "rstd", [128, NT], F32)
        PS = [C.ps(S0, f"ps{i}", [128, 512], F32) for i in range(8)]
        psi = [0]

        def next_ps():
            t = PS[psi[0] % 8]
            psi[0] += 1
            return t

        def rms_stats(src, src_rs, sq_bufs):
            pss = [next_ps() for _ in TL]
            for kc in range(KC):
                sq, sq_r = sq_bufs[kc % len(sq_bufs)]
                C.op("act", lambda a: a.activation(sq[:, :], src[:, kc, :], AF.Square),
                     reads=[src_rs[kc]], writes=[sq_r])
                for ti, (t0, tn) in enumerate(TL):
                    pt, pr = pss[ti]
                    C.op("pe", lambda pe: pe.matmul(pt[:, 0:tn], ones_bf[:, :], sq[:, t0:t0 + tn],
                                                    start=(kc == 0), stop=(kc == KC - 1)),
                         reads=[ones_r, sq_r], writes=[pr], inc=(kc == KC - 1) or True)
            for ti, (t0, tn) in enumerate(TL):
                pt, pr = pss[ti]
                C.op("dve", lambda v: v.tensor_scalar(rstd[:, t0:t0 + tn], pt[:, 0:tn], 1.0 / D, EPS,
                                                      ALU.mult, ALU.add), reads=[pr], writes=[rstd_r])
            C.op("act", lambda a: a.activation(rstd[:, :], rstd[:, :], AF.Sqrt), reads=[rstd_r], writes=[rstd_r])
            C.op("dve", lambda v: v.reciprocal(rstd[:, :], rstd[:, :]), reads=[rstd_r], writes=[rstd_r])

        _sc = nc.named_scope('st1'); _sc.__enter__()
        Sf = contextlib.ExitStack()
        fT, _ = C.sb(Sf, "fT", [128, KC, NT], BF16)
        fT_rs = [C.res(f"fT{k}") for k in range(KC)]
        S1 = contextlib.ExitStack()
        hT, _ = C.sb(S1, "hT", [128, KC, NT], F32)
        hT_rs = [C.res(f"hT{k}") for k in range(KC)]
        for kc in range(KC):
            C.dma("sp", hT[:, kc, :], xT_d[:, kc, :], writes=[hT_rs[kc]])
        Sm = contextlib.ExitStack()
        mixT, _ = C.sb(Sm, "mixT", [128, KC, NT], BF16)
        mix_rs = [C.res(f"mix{k}") for k in range(KC)]
        for kc in range(KC):
            C.dma("pool", mixT[:, kc, :], mixT_d[:, kc, :], writes=[mix_rs[kc]])
        wbufs = [C.sb(Sm, f"wo{i}", [128, KC, 512], BF16) for i in range(2)]
        pass
        for g in range(4):
            wt, wr = wbufs[g % 2]
            C.dma("pool", wt[:, :, :], w_out_d[g, :, :, :], writes=[wr])
            for o in range(4):
                oc = g * 4 + o
                for (t0, tn) in TL:
                    pt, pr = next_ps()
                    for kc in range(KC):
                        C.op("pe", lambda pe: pe.matmul(pt[:, 0:tn], wt[:, kc, o * 128:(o + 1) * 128],
                                                        mixT[:, kc, t0:t0 + tn], start=(kc == 0), stop=(kc == KC - 1)),
                             reads=[wr, mix_rs[kc]], writes=[pr], inc=(kc == KC - 1))
                    C.op("dve", lambda v: v.tensor_tensor(hT[:, oc, t0:t0 + tn], hT[:, oc, t0:t0 + tn], pt[:, 0:tn], ALU.add),
                         reads=[pr, hT_rs[oc]], writes=[hT_rs[oc]])
        C.free(*mix_rs, *[r for _, r in wbufs])
        Sm.close()

        _sc.__exit__(None, None, None); _sc = nc.named_scope('st2'); _sc.__enter__()
        Sq = contextlib.ExitStack()
        sq_bufs = [C.sb(Sq, f"sq{i}", [128, NT], BF16) for i in range(2)]
        rms_stats(hT, hT_rs, sq_bufs)
        for kc in range(KC):
            C.op("dve", lambda v: v.scalar_tensor_tensor(fT[:, kc, :], hT[:, kc, :], nrm[:, 0, kc:kc + 1], rstd[:, :],
                                                         ALU.mult, ALU.mult),
                 reads=[hT_rs[kc], nrm_r, rstd_r], writes=[fT_rs[kc]])
            C.dma("sp", hs_d[:, kc, :], hT[:, kc, :], reads=[hT_rs[kc]], writes=[hs_res], part=True)
        C.free(*hT_rs, *[r for _, r in sq_bufs])
        Sq.close()
        S1.close()

        _sc.__exit__(None, None, None); _sc = nc.named_scope('st3'); _sc.__enter__()
        Sa = contextlib.ExitStack()
        actT, _ = C.sb(Sa, "actT", [128, NJ, NT], BF16)
        act_rs = [C.res(f"act{j}") for j in range(NJ)]
        Sg = contextlib.ExitStack()
        GW = 256
        wg_b = [C.sb(Sg, f"wg{i}", [128, KC, GW], BF16) for i in range(2)]
        wu_b = [C.sb(Sg, f"wu{i}", [128, KC, GW], BF16) for i in range(2)]
        gp_b = [C.sb(Sg, f"gp{i}", [128, NT], F32) for i in range(2)]
        cv_b = [C.sb(Sg, f"cv{i}", [128, NT], F32) for i in range(2)]
        sl_b = [C.sb(Sg, f"sl{i}", [128, NT], F32) for i in range(2)]
        ub_b = [C.sb(Sg, f"ub{i}", [128, NT], BF16) for i in range(2)]
        pass
        pass
        ngrp = (DFF + GW - 1) // GW
        pend = [None]
        stg_b = [C.sb(Sg, f"stg{i}", [128, 4, GW], F32) for i in range(4)]
        wg_rs = [[C.res(f"wg{i}_{k}") for k in range(4)] for i in range(2)]
        wu_rs = [[C.res(f"wu{i}_{k}") for k in range(4)] for i in range(2)]
        stgi = [0]

        def load_group(g):
            for (wb, rs, src) in ((wg_b, wg_rs, w_gate_d), (wu_b, wu_rs, w_up_d)):
                wt, _ = wb[g % 2]
                for k in range(4):
                    stg, stg_r = stg_b[stgi[0] % 4]
                    C.dma("sp", stg[:, :, :], src[g, :, 4 * k:4 * k + 4, :], writes=[stg_r])
                    if stgi[0] % 2 == 0:
                        C.op("act", lambda a: a.copy(wt[:, 4 * k:4 * k + 4, :], stg[:, :, :]), reads=[stg_r], writes=[rs[g % 2][k]])
                    else:
                        C.op("dve", lambda v: v.tensor_copy(wt[:, 4 * k:4 * k + 4, :], stg[:, :, :]), reads=[stg_r], writes=[rs[g % 2][k]])
                    stgi[0] += 1

        load_group(0)
        for g in range(ngrp):
            c0 = g * GW
            cw = min(GW, DFF - c0)
            wg, _ = wg_b[g % 2]
            wu, _ = wu_b[g % 2]
            if g + 1 < ngrp:
                load_group(g + 1)
            for o in range(cw // 128):
                j = (c0 // 128) + o
                gp, gp_r = gp_b[j % 2]
                cv, cv_r = cv_b[j % 2]
                sl, sl_r = sl_b[j % 2]
                pus = []
                for (t0, tn) in TL:
                    pg, pg_r = next_ps()
                    for kc in range(KC):
                        C.op("pe", lambda pe: pe.matmul(pg[:, 0:tn], wg[:, kc, o * 128:(o + 1) * 128],
                                                        fT[:, kc, t0:t0 + tn], start=(kc == 0), stop=(kc == KC - 1)),
                             reads=[wg_rs[g % 2][kc // 4], fT_rs[kc]], writes=[pg_r], inc=(kc == KC - 1))
                    C.op("act", lambda a: a.copy(gp[:, t0:t0 + tn], pg[:, 0:tn]), reads=[pg_r], writes=[gp_r])
                    pu, pu_r = next_ps()
                    for kc in range(KC):
                        C.op("pe", lambda pe: pe.matmul(pu[:, 0:tn], wu[:, kc, o * 128:(o + 1) * 128],
                                                        fT[:, kc, t0:t0 + tn], start=(kc == 0), stop=(kc == KC - 1)),
                             reads=[wu_rs[g % 2][kc // 4], fT_rs[kc]], writes=[pu_r], inc=(kc == KC - 1))
                    ub, ub_r = ub_b[j % 2]
                    C.op("act", lambda a: a.copy(ub[:, t0:t0 + tn], pu[:, 0:tn]), reads=[pu_r], writes=[ub_r])
                    pus.append((ub, ub_r, t0, tn))
                C.op("act", lambda a: a.activation(cv[:, 2:NM + 2], gp[:, 0:NM], AF.Copy, scale=fcw[:, j, 0:1]),
                     reads=[gp_r, fcw_r], writes=[cv_r])
                C.op("dve", lambda p: p.scalar_tensor_tensor(cv[:, 2:NM + 2], gp[:, 1:NM + 1], fcw[:, j, 1:2], cv[:, 2:NM + 2],
                                                              ALU.mult, ALU.add), reads=[gp_r, fcw_r, cv_r], writes=[cv_r])
                C.op("dve", lambda v: v.scalar_tensor_tensor(cv[:, 2:NM + 2], gp[:, 2:NM + 2], fcw[:, j, 2:3], cv[:, 2:NM + 2],
                                                             ALU.mult, ALU.add), reads=[gp_r, fcw_r, cv_r], writes=[cv_r])
                C.op("pool", lambda p: p.memset(cv[:, 0:2], 0.0), writes=[cv_r], reads=[cv_r])
                sc = NM + 2
                C.op("dve", lambda v: v.tensor_scalar(cv[:, sc:sc + 1], fh[:, j, 0:1], fcw[:, j, 0:1], None, ALU.mult),
                     reads=[fh_r, fcw_r, cv_r], writes=[cv_r])
                C.op("dve", lambda v: v.scalar_tensor_tensor(cv[:, sc:sc + 1], fh[:, j, 1:2], fcw[:, j, 1:2], cv[:, sc:sc + 1],
                                                             ALU.mult, ALU.add), reads=[fh_r, fcw_r, cv_r], writes=[cv_r])
                C.op("dve", lambda v: v.scalar_tensor_tensor(cv[:, sc:sc + 1], gp[:, sc:sc + 1], fcw[:, j, 2:3], cv[:, sc:sc + 1],
                                                             ALU.mult, ALU.add), reads=[gp_r, fcw_r, cv_r], writes=[cv_r])
                C.op("pool", lambda p: p.tensor_copy(fcs[:, j, 0:3], gp[:, NM:NM + 3]), reads=[gp_r, fcs_r], writes=[fcs_r])
                def tail(j=j, sl=sl, sl_r=sl_r, cv=cv, cv_r=cv_r, ub=ub, ub_r=ub_r):
                    C.op("act", lambda a: a.activation(sl[:, :], cv[:, :], AF.Silu), reads=[cv_r], writes=[sl_r])
                    C.op("dve", lambda v: v.tensor_tensor(actT[:, j, :], sl[:, :], ub[:, :], ALU.mult),
                         reads=[sl_r, ub_r], writes=[act_rs[j]])
                if pend[0] is not None:
                    pend[0]()
                pend[0] = tail
        pend[0]()
        C.dma("sp", fc_d[:, :, :], fcs[:, :, :], reads=[fcs_r], writes=[C.res("fc_out")], is_out=True)
        C.free(*[r for _, r in wg_b + wu_b + gp_b + cv_b + sl_b + ub_b + stg_b], *[r for l in wg_rs + wu_rs for r in l])
        Sg.close()

        _sc.__exit__(None, None, None); _sc = nc.named_scope('st4'); _sc.__enter__()
        Sd = contextlib.ExitStack()
        wd_b = [C.sb(Sd, f"wd{i}", [128, NJ, 128], BF16) for i in range(2)]
        hc_b = [C.sb(Sd, f"hc{i}", [128, NT], F32) for i in range(2)]
        h2s_res = C.res("h2s")
        pass
        JS = [(0, 11), (11, 11), (22, 11), (33, 10)]
        stg4 = [C.sb(Sd, f"stgd{i}", [128, 11, 128], F32) for i in range(4)]
        wd_rs = [[C.res(f"wd{i}_{k}") for k in range(4)] for i in range(2)]
        s4i = [0]

        def load_wd(oc):
            wd, _ = wd_b[oc % 2]
            for k, (j0, jn) in enumerate(JS):
                stg, stg_r = stg4[s4i[0] % 4]
                C.dma("sp", stg[:, 0:jn, :], w_down_d[oc, :, j0:j0 + jn, :], writes=[stg_r])
                if s4i[0] % 2 == 0:
                    C.op("act", lambda a: a.copy(wd[:, j0:j0 + jn, :], stg[:, 0:jn, :]), reads=[stg_r], writes=[wd_rs[oc % 2][k]])
                else:
                    C.op("dve", lambda v: v.tensor_copy(wd[:, j0:j0 + jn, :], stg[:, 0:jn, :]), reads=[stg_r], writes=[wd_rs[oc % 2][k]])
                s4i[0] += 1

        load_wd(0)
        for oc in range(KC):
            wd, _ = wd_b[oc % 2]
            hc, hc_r = hc_b[oc % 2]
            if oc + 1 < KC:
                load_wd(oc + 1)
            C.dma("sp", hc[:, :], hs_d[:, oc, :], reads=[hs_res], writes=[hc_r])
            for (t0, tn) in TL:
                pt, pr = next_ps()
                for j in range(NJ):
                    C.op("pe", lambda pe: pe.matmul(pt[:, 0:tn], wd[:, j, :], actT[:, j, t0:t0 + tn],
                                                    start=(j == 0), stop=(j == NJ - 1)),
                         reads=[wd_rs[oc % 2][min(j // 11, 3)], act_rs[j]], writes=[pr], inc=(j == NJ - 1))
                C.op("dve", lambda v: v.tensor_tensor(hc[:, t0:t0 + tn], hc[:, t0:t0 + tn], pt[:, 0:tn], ALU.add),
                     reads=[pr, hc_r], writes=[hc_r])
            C.dma("sp", h2s_d[:, oc, :], hc[:, :], reads=[hc_r], writes=[h2s_res], part=True, owner=hc_r)
        C.free(*[r for _, r in wd_b + hc_b + stg4], *[r for l in wd_rs for r in l], *act_rs, *fT_rs)
        Sd.close()
        Sa.close()
        Sf.close()
        Sh = contextlib.ExitStack()
        h2T, _ = C.sb(Sh, "h2T", [128, KC, NT], F32)
        h2_rs = [C.res(f"h2{k}") for k in range(KC)]
        for kc in range(KC):
            C.dma("sp", h2T[:, kc, :], h2s_d[:, kc, :], reads=[h2s_res], writes=[h2_rs[kc]])

        _sc.__exit__(None, None, None); _sc = nc.named_scope('st5'); _sc.__enter__()
        Sp = contextlib.ExitStack()
        gT, _ = C.sb(Sp, "gT", [128, KC, NT], BF16)
        gT_rs = [C.res(f"gT{k}") for k in range(KC)]
        sq_bufs = [C.sb(Sp, f"sqb{i}", [128, NT], BF16) for i in range(2)]
        rms_stats(h2T, h2_rs, sq_bufs)
        for kc in range(KC):
            C.op("dve", lambda v: v.scalar_tensor_tensor(gT[:, kc, :], h2T[:, kc, :], nrm[:, 1, kc:kc + 1], rstd[:, :],
                                                         ALU.mult, ALU.mult),
                 reads=[h2_rs[kc], nrm_r, rstd_r], writes=[gT_rs[kc]])
        pT, pT_r = C.sb(Sp, "pT", [128, 2, NT], BF16)
        C.dma("pool", pT[:, :, :], pT_d[:, :, :], writes=[pT_r])
        wpp, wpp_r = C.sb(Sp, "wpp", [128, 2, D], BF16)
        C.dma("pool", wpp[:, :, :], w_pp_d[:, :, :], writes=[wpp_r])
        wpg_b = [C.sb(Sp, f"wpg{i}", [128, KC, 512], BF16) for i in range(2)]
        sg_b = [C.sb(Sp, f"sg{i}", [128, 512], F32) for i in range(2)]
        pass
        cnt = 0
        for g in range(4):
            wt, wr = wpg_b[g % 2]
            C.dma("pool", wt[:, :, :], w_pg_d[g, :, :, :], writes=[wr])
            for o in range(4):
                oc = g * 4 + o
                for (t0, tn) in TL:
                    pt, pr = next_ps()
                    for kc in range(KC):
                        C.op("pe", lambda pe: pe.matmul(pt[:, 0:tn], wt[:, kc, o * 128:(o + 1) * 128],
                                                        gT[:, kc, t0:t0 + tn], start=(kc == 0), stop=(kc == KC - 1)),
                             reads=[wr, gT_rs[kc]], writes=[pr], inc=(kc == KC - 1))
                    sg, sg_r = sg_b[cnt % 2]
                    cnt += 1
                    C.op("act", lambda a: a.activation(sg[:, 0:tn], pt[:, 0:tn], AF.Sigmoid), reads=[pr], writes=[sg_r])
                    p2, p2r = next_ps()
                    for kc in range(2):
                        C.op("pe", lambda pe: pe.matmul(p2[:, 0:tn], wpp[:, kc, oc * 128:(oc + 1) * 128],
                                                        pT[:, kc, t0:t0 + tn], start=(kc == 0), stop=(kc == 1)),
                             reads=[wpp_r, pT_r], writes=[p2r], inc=(kc == 1))
                    C.op("dve", lambda v: v.tensor_tensor(sg[:, 0:tn], sg[:, 0:tn], p2[:, 0:tn], ALU.mult),
                         reads=[sg_r, p2r], writes=[sg_r])
                    C.op("dve", lambda p: p.tensor_tensor(h2T[:, oc, t0:t0 + tn], h2T[:, oc, t0:t0 + tn], sg[:, 0:tn], ALU.add),
                         reads=[sg_r, h2_rs[oc]], writes=[h2_rs[oc]])
        _sc.__exit__(None, None, None); _sc = nc.named_scope('st6'); _sc.__enter__()
        rms_stats(h2T, h2_rs, sq_bufs)
        yb = [C.sb(Sp, f"yb{i}", [128, NM + 1], F32) for i in range(2)]
        y_res = C.res("y_out")
        for kc in range(KC):
            yt, yr = yb[kc % 2]
            C.op("dve", lambda v: v.scalar_tensor_tensor(yt[:, :], h2T[:, kc, 2:NT], nrm[:, 2, kc:kc + 1], rstd[:, 2:NT],
                                                         ALU.mult, ALU.mult),
                 reads=[h2_rs[kc], nrm_r, rstd_r], writes=[yr])
            C.dma("sp", yT_d[:, kc, :], yt[:, :], reads=[yr], writes=[y_res], part=True, is_out=True, owner=yr)
        _sc.__exit__(None, None, None)
        C.finish()
        Sp.close()
        Sh.close()
    return nc


def l2_weight_layouts(w_out, w_gate, w_up, w_down, w_pg, w_pp):
    f32 = np.float32

    def grp(w, gw):
        n = w.shape[1]
        ng = (n + gw - 1) // gw
        wp = np.zeros((w.shape[0], ng * gw), f32)
        wp[:, :n] = w
        return np.ascontiguousarray(wp.reshape(KC, 128, ng, gw).transpose(2, 1, 0, 3))
    return dict(w_out=grp(w_out, 512), w_gate=grp(w_gate, 256), w_up=grp(w_up, 256),
                w_down=np.ascontiguousarray(w_down.astype(f32).reshape(NJ, 128, KC, 128).transpose(2, 1, 0, 3)),
                w_pg=grp(w_pg, 512),
                w_pp=np.ascontiguousarray(w_pp.astype(f32).reshape(2, 128, D).transpose(1, 0, 2)))


HD = 128
SB_SCALE = HD ** -0.5
NW = 898
C_ID, C_TRIL, C_MST, C_UBD, C_BONES, C_MPOS, C_MNEG, C_SEL0, C_SEL1, C_SU = range(10)
NCST = 10 * 128
P_CB, P_SBN, P_ALOG, P_DTB, P_CW, P_GNC, P_GNR, P_AN = 0, 1, 2, 3, 4, 16, 17, 17 + 128
NPRM = P_AN + KC


def make_consts():
    c = np.zeros((128, 10, 128), np.float32)
    i = np.arange(128)
    same = (i[:, None] // 64) == (i[None, :] // 64)
    c[:, C_ID] = np.eye(128)
    c[:, C_TRIL] = (i[:, None] >= i[None, :])
    c[:, C_MST] = (i[None, :] > i[:, None])
    c[:, C_UBD] = (i[:, None] <= i[None, :]) & same
    c[:, C_BONES] = same
    c[:, C_MPOS] = np.where((i[:, None] > i[None, :]) & same, 0.0, 30000.0)
    c[:, C_MNEG] = np.where((i[None, :] >= i[:, None]) & same, 0.0, -30000.0)
    c[:, C_SEL0] = (i[:, None] < 64) * np.ones((1, 128))
    c[:, C_SEL1] = (i[:, None] >= 64) * np.ones((1, 128))
    c[:, C_SU] = (i[:, None] > i[None, :])
    return c.reshape(128, NCST)


def build_l1(T=4096, NB=2, NS=8, NPOOL=1280, do_sample=True, stages=('attn', 'gdn'), par=True):
    NTOK = NB * T + NS
    NBLK = T // 128
    NTILE = T // 512
    nc = bass.Bass("TRN2", target_bir_lowering=False)
    din = lambda name, shape, dt=F32: nc.dram_tensor(name, list(shape), dt, kind="ExternalInput")
    dout = lambda name, shape, dt=F32: nc.dram_tensor(name, list(shape), dt, kind="ExternalOutput")
    xT_d = din("xT", [128, KC, NTOK])
    w1_d = din("w1", [D, NW])
    prm_d = din("prm", [128, NPRM])
    cst_d = din("cst", [128, NCST])
    poolk_d = din("poolk", [NPOOL, 128 * HD])
    poolv_d = din("poolv", [NPOOL, 128 * HD])
    ptab_d = din("ptab", [128, NS], I32)
    ghist_d = din("ghist", [128, 3, 3, NS])
    grec_d = din("grec", [NS, 128, 128])
    kT_o = dout("kT_o", [128, NTOK])
    vT_o = dout("vT_o", [128, NTOK])
    msb_o = dout("msb_o", [128, NTOK])
    mgd_o = dout("mgd_o", [128, NTOK])
    gcv_o = dout("gcv_o", [128, 3, NB, 3])
    gcs_o = dout("gcs_o", [128, 3, 3, NS])
    grp_o = dout("grp_o", [NB, 128, 128])
    grs_o = dout("grs_o", [NS, 128, 128])

    C = Ctx(nc)
    S0 = contextlib.ExitStack()
    S0.enter_context(nc.allow_low_precision("bf16 matmul operands, fp32 accumulation"))
    sbt = lambda name, shape, dt=F32: C.sb(S0, name, shape, dt)
    op = C.op

    cst, cst_r = sbt("cst", [128, NCST])
    C.dma("sp", cst[:, :], cst_d[:, :], writes=[cst_r])
    prm, prm_r = sbt("prm", [128, NPRM])
    C.dma("sp", prm[:, :], prm_d[:, :], writes=[prm_r])
    cb = lambda k: cst[:, k * 128:(k + 1) * 128]
    ident_f = cb(C_ID)
    w1, w1_r = sbt("w1", [128, KC, NW], BF16)
    C.dma("pool", w1[:, :, :], w1_d.ap().rearrange("(kc p) n -> p kc n", p=128), writes=[w1_r])
    ident_b, idb_r = sbt("ident_b", [128, 128], BF16)
    op("dve", lambda v: v.tensor_copy(ident_b[:, :], ident_f), reads=[cst_r], writes=[idb_r])
    tril_b, trb_r = sbt("tril_b", [128, 128], BF16)
    op("dve", lambda v: v.tensor_copy(tril_b[:, :], cb(C_TRIL)), reads=[cst_r], writes=[trb_r])
    ones_b, onb_r = sbt("ones_b", [128, 128], BF16)
    op("pool", lambda g: g.memset(ones_b[:, :], 1.0), writes=[onb_r])
    ones_f, onf_r = sbt("ones_f", [128, 128])
    op("pool", lambda g: g.memset(ones_f[:, :], 1.0), writes=[onf_r])
    zeros_b, zb_r = sbt("zeros_b", [128, 512], BF16)
    op("pool", lambda g: g.memset(zeros_b[:, :], 0.0), writes=[zb_r])
    nexpA, nea_r = sbt("nexpA", [128, 1])
    op("act", lambda a: a.activation(nexpA[:, :], prm[:, P_ALOG:P_ALOG + 1], AF.Exp), reads=[prm_r], writes=[nea_r])
    op("dve", lambda v: v.tensor_scalar(nexpA[:, :], nexpA[:, :], -1.0, None, ALU.mult), reads=[nea_r], writes=[nea_r])

    PS = [C.ps(S0, f"ps{i}", [128, 512], F32) for i in range(6)]
    PSB = [C.ps(S0, f"psb{i}", [128, 1024], BF16) for i in range(1)]
    PSG = [C.ps(S0, f"psg{i}", [128, 512], F32) for i in range(1)] + PS[1:3]
    psbi = [0]
    psi = [0]

    def next_psb():
        t = PSB[0]
        psbi[0] += 1
        return t

    PAR = Par(C)
    ps_cnt = {}

    def next_ps():
        who = PAR.cur() if C.par is not None else None
        if who == 0:
            lst = PS[3:5]
        elif who == 1:
            lst = PS[1:3]
        elif who == 2:
            lst = [PSG[0], PS[5]]
        else:
            lst = PS[1:6]
        k = ps_cnt.get(who, 0)
        ps_cnt[who] = k + 1
        return lst[k % len(lst)]

    def next_acc():
        return PS[0]

    rot = {}

    def rt(name, shape, dt=F32, n=2):
        if name not in rot:
            rot[name] = [[sbt(f"{name}_{i}", shape, dt) for i in range(n)], 0]
        lst = rot[name]
        t = lst[0][lst[1] % n]
        lst[1] += 1
        return t

    def rsqrt_to(out_ap, out_r, src_ap, src_rs, scale, n):
        tmp, tmp_r = rt(f"rsq{n}", [128, n], F32, 1)
        w = out_ap.shape[-1]
        op("dve", lambda v: v.tensor_scalar(tmp[:, 0:w], src_ap, scale, EPS, ALU.mult, ALU.add), reads=src_rs, writes=[tmp_r])
        op("act", lambda a: a.activation(tmp[:, 0:w], tmp[:, 0:w], AF.Ln), reads=[tmp_r], writes=[tmp_r])
        op("act", lambda a: a.activation(out_ap, tmp[:, 0:w], AF.Exp, scale=-0.5), reads=[tmp_r], writes=[out_r])

    Sm = S0
    BIGW = max(8 * T, 2 * 128 * HD)
    BIG, _ = C.sb(Sm, "BIG", [128, BIGW], BF16)
    bigi = [0]

    def big(name):
        i = bigi[0]
        bigi[0] += 1
        return BIG[:, i * T:(i + 1) * T], C.res(name)
    QT, QT_r = big("QT")
    KT, KT_r = big("KT")
    VTMf, VTM_r = big("VTM")
    VTM = VTMf.rearrange("p (b d) -> p b d", d=128)
    GQ, GQ_r = big("GQ")
    GK, GK_r = big("GK")
    GV, GV_r = big("GV")
    ZS, ZS_r = big("ZS")
    gtm, gtm_r = C.sb(Sm, "gtm", [128, NBLK], F32)
    btm, btm_r = C.sb(Sm, "btm", [128, NBLK], F32)
    xt, _ = C.sb(Sm, "xt", [128, KC, 256], F32)
    xt_rs = [C.res(f"xt{k}") for k in range(KC)]
    xn, _ = C.sb(Sm, "xn", [128, KC, 512], BF16)
    xn_rs = [C.res(f"xn{k}") for k in range(KC)]
    cin = [C.sb(Sm, f"cin{i}", [128, 3 + 512], F32) for i in range(3)]
    gcv_s, gcv_r = sbt("gcv_s", [128, 3, NB, 3])

    def proj_tile(col0, tn, dst):
        for h0 in range(0, tn, 256):
            hn = min(256, tn - h0)
            for kc in range(KC):
                C.dma("sp", xt[:, kc, 0:hn], xT_d[:, kc, col0 + h0:col0 + h0 + hn], writes=[xt_rs[kc]])
            pss, pss_r = next_ps()
            for kc in range(KC):
                sq, sq_r = rt("sq", [128, 512], BF16)
                op("act", lambda a: a.activation(sq[:, 0:hn], xt[:, kc, 0:hn], AF.Square), reads=[xt_rs[kc]], writes=[sq_r])
                op("pe", lambda pe: pe.matmul(pss[:, 0:hn], ones_b[:, :], sq[:, 0:hn], start=(kc == 0), stop=(kc == KC - 1)),
                   reads=[onb_r, sq_r], writes=[pss_r], inc=True)
            rstd, rstd_r = rt("rstd", [128, 512], F32, 1)
            rsqrt_to(rstd[:, 0:hn], rstd_r, pss[:, 0:hn], [pss_r], 1.0 / D, 512)
            for kc in range(KC):
                op("dve", lambda v: v.scalar_tensor_tensor(xn[:, kc, h0:h0 + hn], xt[:, kc, 0:hn], prm[:, P_AN + kc:P_AN + kc + 1],
                                                           rstd[:, 0:hn], ALU.mult, ALU.mult),
                   reads=[xt_rs[kc], prm_r, rstd_r], writes=[xn_rs[kc]])
        for cc in range(7):
            pt, pr = next_ps()
            for kc in range(KC):
                op("pe", lambda pe: pe.matmul(pt[:, 0:tn], w1[:, kc, cc * 128:(cc + 1) * 128], xn[:, kc, 0:tn],
                                              start=(kc == 0), stop=(kc == KC - 1)),
                   reads=[w1_r, xn_rs[kc]], writes=[pr], inc=(kc == KC - 1))
            dst(cc, pt, pr)

    def l2n_to(out_ap, out_r, y, y_r, tn, scale):
        sq, sq_r = rt("sq", [128, 512], BF16)
        op("act", lambda a: a.activation(sq[:, 0:tn], y[:, 0:tn], AF.Square), reads=[y_r], writes=[sq_r])
        pt, pr = next_ps()
        op("pe", lambda pe: pe.matmul(pt[:, 0:tn], ones_b[:, :], sq[:, 0:tn], start=True, stop=True),
           reads=[onb_r, sq_r], writes=[pr])
        rs, rs_r = rt("rstd", [128, 512], F32, 1)
        rsqrt_to(rs[:, 0:tn], rs_r, pt[:, 0:tn], [pr], 1.0, 512)
        op("dve", lambda v: v.scalar_tensor_tensor(out_ap, y[:, 0:tn], scale, rs[:, 0:tn], ALU.mult, ALU.mult),
           reads=[y_r, rs_r], writes=[out_r])

    def softplus_to(out_ap, out_r, src_ap, src_rs, bias_ap, scale, shape):
        tmp, tmp_r = rt("spl%d_%d" % tuple(shape), shape)
        kw = dict(scale=scale)
        if bias_ap is not None:
            kw["bias"] = bias_ap
        op("act", lambda a: a.activation(tmp[:, :], src_ap, AF.Exp, **kw), reads=src_rs + [prm_r], writes=[tmp_r])
        op("act", lambda a: a.activation(out_ap, tmp[:, :], AF.Ln, bias=1.0), reads=[tmp_r], writes=[out_r])

    def headnorm_out(po, po_r, tn, out_d, col0, wcol):
        osb, osb_r = rt("osb", [128, 512], F32, 1)
        op("act", lambda a: a.copy(osb[:, 0:tn], po[:, 0:tn]), reads=[po_r], writes=[osb_r])
        sq, sq_r = rt("sq", [128, 512], BF16)
        op("act", lambda a: a.activation(sq[:, 0:tn], osb[:, 0:tn], AF.Square), reads=[osb_r], writes=[sq_r])
        pt, pr = next_ps()
        op("pe", lambda pe: pe.matmul(pt[:, 0:tn], ones_b[:, :], sq[:, 0:tn], start=True, stop=True),
           reads=[onb_r, sq_r], writes=[pr])
        rs, rs_r = rt("rstd", [128, 512], F32, 1)
        rsqrt_to(rs[:, 0:tn], rs_r, pt[:, 0:tn], [pr], 1.0 / HD, 512)
        op("dve", lambda v: v.scalar_tensor_tensor(osb[:, 0:tn], osb[:, 0:tn], prm[:, wcol:wcol + 1], rs[:, 0:tn],
                                                   ALU.mult, ALU.mult), reads=[osb_r, prm_r, rs_r], writes=[osb_r])
        return osb, osb_r

    out_res = {k: C.res("o_" + k) for k in ("kT", "vT", "msb", "mgd", "gcv", "gcs", "grp", "grs")}

    def store(out_ap, src_ap, src_r, key):
        C.dma("sp", out_ap, src_ap, reads=[src_r], writes=[out_res[key]], part=True, is_out=True, owner=src_r)


    def zt(name, dt):
        t, r = sbt(name, [128, 128], dt)
        op("pool", lambda g: g.memset(t[:, :], 0.0), writes=[r])
        return t, r
    WT0 = [zt("wTm0_%d" % i, BF16) for i in range(2)]
    WT1 = [zt("wTm1_%d" % i, BF16) for i in range(2)]
    QM0 = [zt("qm0_%d" % i, BF16) for i in range(2)]
    QM1 = [zt("qm1_%d" % i, BF16) for i in range(2)]
    KD0 = [zt("kd0_%d" % i, BF16) for i in range(2)]
    KD1 = [zt("kd1_%d" % i, BF16) for i in range(2)]
    gnr = prm[:, P_GNR:P_GNR + 128]

    def mmf(lhsT, rhs, reads, n=128):
        pt, pr = next_ps()
        op("pe", lambda pe: pe.matmul(pt[:, 0:n], lhsT, rhs, start=True, stop=True), reads=reads, writes=[pr])
        return pt, pr

    gstate = {}

    def gdn_wy(b):
        H = gstate[b]
        for nb in range(NBLK):
            while nb - H['scan_done'] > 1:
                PAR.idle()
            cs = slice(nb * 128, (nb + 1) * 128)
            pp_ = nb % 2
            (wTm0, wTm0_r), (wTm1, wTm1_r) = WT0[pp_], WT1[pp_]
            (qm0, qm0_r), (qm1, qm1_r) = QM0[pp_], QM1[pp_]
            (kd0, kd0_r), (kd1, kd1_r) = KD0[pp_], KD1[pp_]
            gcol = gtm[:, nb:nb + 1]
            bcol = btm[:, nb:nb + 1]
            p1, p1_r = next_ps()
            for i, blkc in enumerate((C_UBD, C_BONES, C_SEL0, C_SEL1)):
                op("pe", lambda pe: pe.matmul(p1[:, i:i + 1], cb(blkc), gcol, start=True, stop=True),
                   reads=[cst_r, gtm_r], writes=[p1_r])
            sc, sc_r = rt("sc", [128, 8])
            op("dve", lambda v: v.tensor_copy(sc[:, 0:4], p1[:, 0:4]), reads=[p1_r], writes=[sc_r])
            op("act", lambda a: a.activation(sc[:, 4:5], sc[:, 0:1], AF.Exp), reads=[sc_r], writes=[sc_r])
            op("dve", lambda v: v.tensor_tensor(sc[:, 5:6], sc[:, 1:2], sc[:, 0:1], ALU.subtract), reads=[sc_r], writes=[sc_r])
            op("act", lambda a: a.activation(sc[:, 5:6], sc[:, 5:6], AF.Exp), reads=[sc_r], writes=[sc_r])
            op("dve", lambda v: v.tensor_tensor(sc[:, 6:7], sc[:, 4:5], bcol, ALU.mult), reads=[sc_r, btm_r], writes=[sc_r])
            op("act", lambda a: a.activation(sc[:, 2:4], sc[:, 2:4], AF.Exp), reads=[sc_r], writes=[sc_r])
            Dg, Dg_r = rt("Dg", [128, 128])
            op("dve", lambda v: v.tensor_scalar(Dg[:, :], ident_f, sc[:, 0:1], None, ALU.mult), reads=[cst_r, sc_r], writes=[Dg_r])
            pG, pG_r = mmf(ones_f[:, :], Dg[:, :], [onf_r, Dg_r])
            E1, E1_r = rt("E1", [128, 128])
            op("dve", lambda v: v.scalar_tensor_tensor(E1[:, :], pG[:, 0:128], sc[:, 0:1], cb(C_MPOS), ALU.subtract, ALU.add),
               reads=[pG_r, sc_r, cst_r], writes=[E1_r])
            op("act", lambda a: a.activation(E1[:, :], E1[:, :], AF.Exp, scale=-1.0), reads=[E1_r], writes=[E1_r])
            E2, E2_r = rt("E2", [128, 128])
            op("dve", lambda v: v.scalar_tensor_tensor(E2[:, :], pG[:, 0:128], sc[:, 0:1], cb(C_MNEG), ALU.subtract, ALU.add),
               reads=[pG_r, sc_r, cst_r], writes=[E2_r])
            op("act", lambda a: a.activation(E2[:, :], E2[:, :], AF.Exp), reads=[E2_r], writes=[E2_r])
            pK, pK_r = next_ps()
            op("pe", lambda pe: pe.matmul(pK[:, 0:128], GK[:, cs], GK[:, cs], start=True, stop=True), reads=[GK_r], writes=[pK_r])
            op("pe", lambda pe: pe.matmul(pK[:, 128:256], GK[:, cs], GQ[:, cs], start=True, stop=True), reads=[GK_r, GQ_r], writes=[pK_r])
            Nl, Nl_r = rt("Nl", [128, 128], F32, n=3)
            op("dve", lambda v: v.scalar_tensor_tensor(Nl[:, :], pK[:, 0:128], bcol, E1[:, :], ALU.mult, ALU.mult),
               reads=[pK_r, btm_r, E1_r], writes=[Nl_r])
            QKd, QKd_r = rt("QKd", [128, 128], BF16)
            op("dve", lambda v: v.tensor_tensor(QKd[:, :], pK[:, 128:256], E2[:, :], ALU.mult), reads=[pK_r, E2_r], writes=[QKd_r])
            pM, pM_r = mmf(Nl[:, :], ident_f, [Nl_r, cst_r])
            Ml, Ml_r = rt("Ml", [128, 128], F32, n=3)
            op("act", lambda a: a.copy(Ml[:, :], pM[:, 0:128]), reads=[pM_r], writes=[Ml_r])
            W, W_r = rt("W", [128, 128], F32, n=3)
            op("dve", lambda v: v.tensor_tensor(W[:, :], ident_f, pM[:, 0:128], ALU.subtract), reads=[pM_r, cst_r], writes=[W_r])
            for k in range(1, 6):
                pN, pN_r = mmf(Ml[:, :], Nl[:, :], [Ml_r, Nl_r])
                if k < 5:
                    pM2, pM2_r = mmf(Nl[:, :], Ml[:, :], [Ml_r, Nl_r])
                Nl2, Nl2_r = rt("Nl", [128, 128], F32, n=3)
                op("act", lambda a: a.copy(Nl2[:, :], pN[:, 0:128]), reads=[pN_r], writes=[Nl2_r])
                if k < 5:
                    Ml2, Ml2_r = rt("Ml", [128, 128], F32, n=3)
                    op("dve", lambda v: v.tensor_copy(Ml2[:, :], pM2[:, 0:128]), reads=[pM2_r], writes=[Ml2_r])
                pW, pW_r = mmf(Nl2[:, :], W[:, :], [Nl2_r, W_r])
                W2, W2_r = rt("W", [128, 128], F32, n=3)
                op("dve", lambda v: v.tensor_tensor(W2[:, :], W[:, :], pW[:, 0:128], ALU.add), reads=[W_r, pW_r], writes=[W2_r])
                Nl, Nl_r, W, W_r = Nl2, Nl2_r, W2, W2_r
                if k < 5:
                    Ml, Ml_r = Ml2, Ml2_r
            tpk, tpk_r = next_psb()
            op("pe", lambda pe: pe.transpose(tpk[:, 0:128], GK[:, cs], ident_b[:, :]), reads=[GK_r, idb_r], writes=[tpk_r])
            kbg, kbg_r = rt("kbg", [128, 128])
            op("dve", lambda v: v.tensor_scalar(kbg[:, :], tpk[:, 0:128], sc[:, 6:7], None, ALU.mult), reads=[tpk_r, sc_r], writes=[kbg_r])
            op("act", lambda a: a.activation(kd0[0:64, :], tpk[0:64, 0:128], AF.Copy, scale=sc[0:64, 5:6]), reads=[tpk_r, sc_r, kd0_r], writes=[kd0_r])
            op("act", lambda a: a.activation(kd1[64:128, :], tpk[64:128, 0:128], AF.Copy, scale=sc[64:128, 5:6]), reads=[tpk_r, sc_r, kd1_r], writes=[kd1_r])
            tpv, tpv_r = next_psb()
            op("pe", lambda pe: pe.transpose(tpv[:, 0:128], GV[:, cs], ident_b[:, :]), reads=[GV_r, idb_r], writes=[tpv_r])
            vb_, vb_r = rt("vbt", [128, 128])
            op("dve", lambda v: v.tensor_scalar(vb_[:, :], tpv[:, 0:128], bcol, None, ALU.mult), reads=[tpv_r, btm_r], writes=[vb_r])
            pu, pu_r = mmf(W[:, :], vb_[:, :], [W_r, vb_r])
            u, u_r = rt("u", [128, 128])
            op("act", lambda a: a.copy(u[:, :], pu[:, 0:128]), reads=[pu_r], writes=[u_r])
            pw, pw_r = mmf(kbg[:, :], W[:, :], [kbg_r, W_r])
            op("act", lambda a: a.mul(wTm0[:, 0:64], pw[:, 0:64], -1.0), reads=[pw_r, wTm0_r], writes=[wTm0_r])
            op("dve", lambda v: v.tensor_scalar(wTm1[:, 64:128], pw[:, 64:128], -1.0, None, ALU.mult), reads=[pw_r, wTm1_r], writes=[wTm1_r])
            op("pool", lambda g: g.tensor_copy(qm0[:, 0:64], GQ[:, nb * 128:nb * 128 + 64]), reads=[GQ_r, qm0_r], writes=[qm0_r])
            op("pool", lambda g: g.tensor_copy(qm1[:, 64:128], GQ[:, nb * 128 + 64:nb * 128 + 128]), reads=[GQ_r, qm1_r], writes=[qm1_r])
            H['blk'][nb] = dict(u=(u, u_r), QKd=(QKd, QKd_r), sc=(sc, sc_r), par=nb % 2)
            H['wy_done'] = nb + 1

    def gdn_scan(b):
        H = gstate[b]
        S, S_r = rt("S", [128, 128], F32, n=3)
        Sb, Sb_r = rt("Sb", [128, 128], BF16, n=3)
        op("pool", lambda g: g.memset(S[:, :], 0.0), reads=[S_r], writes=[S_r])
        op("pool", lambda g: g.memset(Sb[:, :], 0.0), reads=[Sb_r], writes=[Sb_r])
        mg, mg_r = None, None
        for nb in range(NBLK):
            while H['wy_done'] <= nb:
                PAR.idle()
            B_ = H['blk'].pop(nb)
            (u, u_r), (QKd, QKd_r), (sc, sc_r) = B_['u'], B_['QKd'], B_['sc']
            pp_ = B_['par']
            (wTm0, wTm0_r), (wTm1, wTm1_r) = WT0[pp_], WT1[pp_]
            (qm0, qm0_r), (qm1, qm1_r) = QM0[pp_], QM1[pp_]
            (kd0, kd0_r), (kd1, kd1_r) = KD0[pp_], KD1[pp_]
            cs = slice(nb * 128, (nb + 1) * 128)
            pa, pa_r = mmf(wTm0[:, :], Sb[:, :], [wTm0_r, Sb_r])
            vn, vn_r = rt("vn", [128, 128])
            op("dve", lambda v: v.tensor_tensor(vn[:, :], u[:, :], pa[:, 0:128], ALU.add), reads=[u_r, pa_r], writes=[vn_r])
            vnb, vnb_r = rt("vnb", [128, 128], BF16)
            op("act", lambda a: a.copy(vnb[:, :], vn[:, :]), reads=[vn_r], writes=[vnb_r])
            pq, pq_r = mmf(qm0[:, :], Sb[:, :], [qm0_r, Sb_r])
            o1, o1_r = rt("o1", [128, 128])
            op("act", lambda a: a.activation(o1[:, :], pq[:, 0:128], AF.Copy, scale=sc[:, 4:5]), reads=[pq_r, sc_r], writes=[o1_r])
            pS1, pS1_r = mmf(kd0[:, :], vnb[:, :], [kd0_r, vnb_r])
            S1, S1_r = rt("S", [128, 128], F32, n=3)
            op("dve", lambda v: v.scalar_tensor_tensor(S1[:, :], S[:, :], sc[:, 2:3], pS1[:, 0:128], ALU.mult, ALU.add),
               reads=[S_r, sc_r, pS1_r], writes=[S1_r])
            S1b, S1b_r = rt("Sb", [128, 128], BF16, n=3)
            op("act", lambda a: a.copy(S1b[:, :], S1[:, :]), reads=[S1_r], writes=[S1b_r])
            pb, pb_r = mmf(wTm1[:, :], S1b[:, :], [wTm1_r, S1b_r])
            vn2, vn2_r = rt("vn", [128, 128])
            op("dve", lambda v: v.tensor_tensor(vn2[:, :], vn[:, :], pb[:, 0:128], ALU.add), reads=[vn_r, pb_r], writes=[vn2_r])
            vn2b, vn2b_r = rt("vnb", [128, 128], BF16)
            op("act", lambda a: a.copy(vn2b[:, :], vn2[:, :]), reads=[vn2_r], writes=[vn2b_r])
            pq2, pq2_r = mmf(qm1[:, :], S1b[:, :], [qm1_r, S1b_r])
            op("act", lambda a: a.activation(o1[64:128, :], pq2[64:128, 0:128], AF.Copy, scale=sc[64:128, 4:5]), reads=[pq2_r, sc_r, o1_r], writes=[o1_r])
            pS2, pS2_r = mmf(kd1[:, :], vn2b[:, :], [kd1_r, vn2b_r])
            S2, S2_r = rt("S", [128, 128], F32, n=3)
            op("dve", lambda v: v.scalar_tensor_tensor(S2[:, :], S1[:, :], sc[:, 3:4], pS2[:, 0:128], ALU.mult, ALU.add),
               reads=[S1_r, sc_r, pS2_r], writes=[S2_r])
            S2b, S2b_r = rt("Sb", [128, 128], BF16, n=3)
            op("act", lambda a: a.copy(S2b[:, :], S2[:, :]), reads=[S2_r], writes=[S2b_r])
            pqk, pqk_r = mmf(QKd[:, :], vn2b[:, :], [QKd_r, vn2b_r])
            op("dve", lambda v: v.tensor_tensor(o1[:, :], o1[:, :], pqk[:, 0:128], ALU.add), reads=[o1_r, pqk_r], writes=[o1_r])
            S, S_r, Sb, Sb_r = S2, S2_r, S2b, S2b_r
            jk, jk_r = rt("jk", [128, 128])
            ssc, ssc_r = rt("ssc", [128, 1])
            op("act", lambda a: a.activation(jk[:, :], o1[:, :], AF.Square, accum_out=ssc[:, :]), reads=[o1_r], writes=[jk_r, ssc_r])
            rs1, rs1_r = rt("rs1", [128, 1])
            rsqrt_to(rs1[:, :], rs1_r, ssc[:, :], [ssc_r], 1.0 / HD, 1)
            on, on_r = rt("on", [128, 128], F32)
            op("dve", lambda v: v.scalar_tensor_tensor(on[:, :], o1[:, :], rs1[:, 0:1], gnr, ALU.mult, ALU.mult),
               reads=[o1_r, rs1_r, prm_r], writes=[on_r])
            tpo, tpo_r = mmf(on[:, :], ident_f, [on_r, cst_r])
            k4 = nb % 4
            if k4 == 0:
                mg, mg_r = rt("mg", [128, 512], F32, 1)
            op("dve", lambda v: v.tensor_tensor(mg[:, k4 * 128:(k4 + 1) * 128], tpo[:, 0:128], ZS[:, cs], ALU.mult),
               reads=[tpo_r, ZS_r, mg_r], writes=[mg_r])
            if k4 == 3:
                store(mgd_o[:, b * T + (nb - 3) * 128:b * T + (nb + 1) * 128], mg[:, :], mg_r, "mgd")
            H['scan_done'] = nb + 1
        store(grp_o[b, :, :], S[:, :], S_r, "grp")


    def sample_part():
        col0 = NB * T
        C.free(QT_r, KT_r, VTM_r, GQ_r, GK_r, GV_r, ZS_r)
        Kg = BIG[:, 0:128 * HD]
        Vg = BIG[:, 128 * HD:2 * 128 * HD]
        Kg_r, Vg_r = C.res("Kg"), C.res("Vg")
        idx, idx_r = sbt("idx", [128, NS], I32)
        C.dma("sp", idx[:, :], ptab_d[:, :], writes=[idx_r])
        gh, gh_r = sbt("gh", [128, 3, 3, NS])
        C.dma("sp", gh[:, :, :, :], ghist_d[:, :, :, :], writes=[gh_r])
        Sall, Sall_r = sbt("Sall", [128, NS, 128])
        for s_ in range(NS):
            C.dma("sp", Sall[:, s_, :], grec_d[s_, :, :], writes=[Sall_r], part=True)
        sm = {}
        t8 = lambda name: sbt(name, [128, NS])

        def dst(cc, pt, pr):
            if cc == 0:
                t_, r_ = t8("s_q")
                op("act", lambda a: a.copy(t_[:, :], pt[:, 0:NS]), reads=[pr], writes=[r_])
                sm["q"] = (t_, r_)
            elif cc in (1, 2):
                t_, r_ = t8("s_kv%d" % cc)
                op("act", lambda a: a.copy(t_[:, :], pt[:, 0:NS]), reads=[pr], writes=[r_])
                store((kT_o if cc == 1 else vT_o)[:, col0:col0 + NS], t_[:, :], r_, "kT" if cc == 1 else "vT")
            elif cc in (3, 4, 5):
                q = cc - 3
                cn, cn_r = t8("s_cn%d" % q)
                op("act", lambda a: a.copy(cn[:, :], pt[:, 0:NS]), reads=[pr], writes=[cn_r])
                store(gcs_o[:, q, 2, :], cn[:, :], cn_r, "gcs")
                C.dma("sp", gcs_o[:, q, 0:2, :], gh[:, q, 1:3, :], reads=[gh_r], writes=[out_res["gcs"]], part=True, is_out=True, owner=C.res("ghst%d" % q))
                y, y_r = t8("s_y%d" % q)
                cw = lambda j: prm[:, P_CW + q * 4 + j:P_CW + q * 4 + j + 1]
                op("dve", lambda v: v.tensor_scalar(y[:, :], gh[:, q, 0, :], cw(0), None, ALU.mult), reads=[gh_r, prm_r], writes=[y_r])
                for j in (1, 2):
                    op("dve", lambda v: v.scalar_tensor_tensor(y[:, :], gh[:, q, j, :], cw(j), y[:, :], ALU.mult, ALU.add),
                       reads=[gh_r, prm_r, y_r], writes=[y_r])
                op("dve", lambda v: v.scalar_tensor_tensor(y[:, :], cn[:, :], cw(3), y[:, :], ALU.mult, ALU.add),
                   reads=[cn_r, prm_r, y_r], writes=[y_r])
                op("act", lambda a: a.activation(y[:, :], y[:, :], AF.Silu), reads=[y_r], writes=[y_r])
                if q == 2:
                    sm["gv"] = (y, y_r)
                else:
                    o_, r_ = t8("s_g%d" % q)
                    l2n_to(o_[:, :], r_, y, y_r, NS, HD ** -0.5 if q == 0 else 1.0)
                    sm["gq" if q == 0 else "gk"] = (o_, r_)
            else:
                t_, r_ = t8("s_zs")
                op("act", lambda a: a.activation(t_[:, :], pt[:, 0:NS], AF.Silu), reads=[pr], writes=[r_])
                sm["zs"] = (t_, r_)

        proj_tile(col0, NS, dst)
        pA, pA_r = next_ps()
        for kc in range(KC):
            op("pe", lambda pe: pe.matmul(pA[0:1, 0:NS], w1[:, kc, 896:897], xn[:, kc, 0:NS], start=(kc == 0), stop=(kc == KC - 1)),
               reads=[w1_r, xn_rs[kc]], writes=[pA_r], inc=(kc == KC - 1))
        row, row_r = sbt("s_row", [1, 2, NS])
        tmp1, tmp1_r = sbt("s_tmp1", [1, NS])
        op("act", lambda a: a.activation(tmp1[:, :], pA[0:1, 0:NS], AF.Exp, bias=prm[0:1, P_DTB:P_DTB + 1]), reads=[pA_r, prm_r], writes=[tmp1_r])
        op("act", lambda a: a.activation(tmp1[:, :], tmp1[:, :], AF.Ln, bias=1.0), reads=[tmp1_r], writes=[tmp1_r])
        op("dve", lambda v: v.tensor_scalar(row[:, 0, :], tmp1[:, :], nexpA[0:1, 0:1], None, ALU.mult), reads=[tmp1_r, nea_r], writes=[row_r])
        pB, pB_r = next_ps()
        for kc in range(KC):
            op("pe", lambda pe: pe.matmul(pB[0:1, 0:NS], w1[:, kc, 897:898], xn[:, kc, 0:NS], start=(kc == 0), stop=(kc == KC - 1)),
               reads=[w1_r, xn_rs[kc]], writes=[pB_r], inc=(kc == KC - 1))
        op("act", lambda a: a.activation(tmp1[:, :], pB[0:1, 0:NS], AF.Exp, scale=-1.0), reads=[pB_r, tmp1_r], writes=[tmp1_r])
        op("dve", lambda v: v.tensor_scalar(tmp1[:, :], tmp1[:, :], 1.0, None, ALU.add), reads=[tmp1_r], writes=[tmp1_r])
        op("dve", lambda v: v.reciprocal(row[:, 1, :], tmp1[:, :]), reads=[tmp1_r, row_r], writes=[row_r])
        pbc, pbc_r = mmf(ones_f[0:1, :], row[0:1, :, :].rearrange("p a n -> p (a n)"), [onf_r, row_r], n=2 * NS)
        bc, bc_r = sbt("s_bc", [128, 3, NS])
        op("dve", lambda v: v.tensor_copy(bc[:, 0:2, :].rearrange("p a n -> p (a n)"), pbc[:, 0:2 * NS]), reads=[pbc_r], writes=[bc_r])
        op("act", lambda a: a.activation(bc[:, 2, :], bc[:, 0, :], AF.Exp), reads=[bc_r], writes=[bc_r])
        egb, beb = bc[:, 2, :], bc[:, 1, :]

        pos, pos_r = next_acc()
        q_, q_r = sm["q"]
        for s_ in range(NS):
            C.indirect(Kg, poolk_d[:, :], bass.IndirectOffsetOnAxis(ap=idx[:, s_:s_ + 1], axis=0), reads=[idx_r], writes=[Kg_r])
            C.indirect(Vg, poolv_d[:, :], bass.IndirectOffsetOnAxis(ap=idx[:, s_:s_ + 1], axis=0), reads=[idx_r], writes=[Vg_r])
            Qrep, Qrep_r = rt("Qrep", [128, 128])
            op("dve", lambda v: v.tensor_scalar(Qrep[:, :], ones_f[:, :], q_[:, s_:s_ + 1], None, ALU.mult), reads=[onf_r, q_r], writes=[Qrep_r])
            pqb, pqb_r = mmf(Qrep[:, :], ident_f, [Qrep_r, cst_r])
            qbc, qbc_r = rt("qbc", [128, 128], BF16)
            op("act", lambda a: a.copy(qbc[:, :], pqb[:, 0:128]), reads=[pqb_r], writes=[qbc_r])
            z, z_r = rt("zs_", [128, 128])
            for r0 in range(0, 128, 8):
                prod, prod_r = rt("prod", [128, 8, 128], F32, 1)
                op("dve", lambda v: v.tensor_tensor(prod[:, :, :], Kg[:, r0 * 128:(r0 + 8) * 128].rearrange("p (r d) -> p r d", d=128),
                                                    qbc[:, :].unsqueeze(1).to_broadcast([128, 8, 128]), ALU.mult),
                   reads=[Kg_r, qbc_r], writes=[prod_r])
                op("dve", lambda v: v.reduce_sum(z[:, r0:r0 + 8], prod[:, :, :], AX.X), reads=[prod_r, z_r], writes=[z_r])
            pzT, pzT_r = mmf(z[:, :], ident_f, [z_r, cst_r])
            e, e_r = rt("se", [128, 128])
            op("act", lambda a: a.activation(e[:, :], pzT[:, 0:128], AF.Exp, bias=prm[:, P_CB:P_CB + 1], scale=SB_SCALE),
               reads=[pzT_r, prm_r], writes=[e_r])
            L, L_r = rt("sL", [128, 128])
            op("act", lambda a: a.activation(L[:, :], e[:, :], AF.Ln, bias=1.0), reads=[e_r], writes=[L_r])
            pS, pS_r = next_ps()
            op("pe", lambda pe: pe.matmul(pS[:, 0:128], cb(C_TRIL), L[:, :], start=True, stop=False), reads=[cst_r, L_r], writes=[pS_r], inc=False)
            pT, pT_r = mmf(L[:, :], ones_f[:, :], [L_r, onf_r])
            Tbt, Tbt_r = rt("Tbt", [128, 128])
            op("act", lambda a: a.copy(Tbt[:, :], pT[:, 0:128]), reads=[pT_r], writes=[Tbt_r])
            op("pe", lambda pe: pe.matmul(pS[:, 0:128], Tbt[:, :], cb(C_SU), start=False, stop=True), reads=[Tbt_r, cst_r], writes=[pS_r])
            w_, w_r = rt("sw", [128, 128])
            op("act", lambda a: a.activation(w_[:, :], pS[:, 0:128], AF.Exp, scale=-1.0), reads=[pS_r], writes=[w_r])
            op("dve", lambda v: v.tensor_tensor(w_[:, :], w_[:, :], e[:, :], ALU.mult), reads=[w_r, e_r], writes=[w_r])
            paT, paT_r = mmf(w_[:, :], ident_f, [w_r, cst_r])
            aT, aT_r = rt("aT", [128, 128], BF16)
            op("act", lambda a: a.copy(aT[:, :], paT[:, 0:128]), reads=[paT_r], writes=[aT_r])
            for r in range(128):
                op("pe", lambda pe: pe.matmul(pos[:, s_:s_ + 1], Vg[:, r * 128:(r + 1) * 128], aT[:, r:r + 1], start=(r == 0), stop=(r == 127)),
                   reads=[Vg_r, aT_r], writes=[pos_r], inc=(r == 127))
        osb, osb_r = headnorm_out(pos, pos_r, NS, msb_o, col0, P_SBN)
        store(msb_o[:, col0:col0 + NS], osb[:, 0:NS], osb_r, "msb")

        gq_, gq_r = sm["gq"]
        gk_, gk_r = sm["gk"]
        gv_, gv_r = sm["gv"]
        zs_, zs_r = sm["zs"]
        pks, pks_r = next_ps()
        for s_ in range(NS):
            op("pe", lambda pe: pe.matmul(pks[:, s_:s_ + 1], Sall[:, s_, :], gk_[:, s_:s_ + 1], start=True, stop=True), reads=[Sall_r, gk_r], writes=[pks_r])
            op("pe", lambda pe: pe.matmul(pks[:, NS + s_:NS + s_ + 1], Sall[:, s_, :], gq_[:, s_:s_ + 1], start=True, stop=True), reads=[Sall_r, gq_r], writes=[pks_r])
        vn, vn_r = t8("s_vn")
        op("dve", lambda v: v.tensor_tensor(vn[:, :], pks[:, 0:NS], egb, ALU.mult), reads=[pks_r, bc_r], writes=[vn_r])
        op("dve", lambda v: v.tensor_tensor(vn[:, :], gv_[:, :], vn[:, :], ALU.subtract), reads=[gv_r, vn_r], writes=[vn_r])
        op("dve", lambda v: v.tensor_tensor(vn[:, :], vn[:, :], beb, ALU.mult), reads=[vn_r, bc_r], writes=[vn_r])
        so, so_r = t8("s_o")
        op("dve", lambda v: v.tensor_tensor(so[:, :], pks[:, NS:2 * NS], egb, ALU.mult), reads=[pks_r, bc_r], writes=[so_r])
        pr_, pr_r = t8("s_pr")
        op("dve", lambda v: v.tensor_tensor(pr_[:, :], gq_[:, :], gk_[:, :], ALU.mult), reads=[gq_r, gk_r], writes=[pr_r])
        pqk, pqk_r = mmf(ones_f[:, :], pr_[:, :], [onf_r, pr_r], n=NS)
        op("dve", lambda v: v.tensor_tensor(pr_[:, :], pqk[:, 0:NS], vn[:, :], ALU.mult), reads=[pqk_r, vn_r, pr_r], writes=[pr_r])
        op("dve", lambda v: v.tensor_tensor(so[:, :], so[:, :], pr_[:, :], ALU.add), reads=[so_r, pr_r], writes=[so_r])
        og, og_r = headnorm_out(so, so_r, NS, mgd_o, col0, P_GNC)
        op("dve", lambda v: v.tensor_tensor(og[:, 0:NS], og[:, 0:NS], zs_[:, :], ALU.mult), reads=[og_r, zs_r], writes=[og_r])
        store(mgd_o[:, col0:col0 + NS], og[:, 0:NS], og_r, "mgd")
        pkr, pkr_r = next_ps()
        op("pe", lambda pe: pe.matmul(pkr[0:NS, 0:128], gk_[:, :], ident_f, start=True, stop=True), reads=[gk_r, cst_r], writes=[pkr_r])
        op("pe", lambda pe: pe.matmul(pkr[0:NS, 128:256], vn[:, :], ident_f, start=True, stop=True), reads=[vn_r, cst_r], writes=[pkr_r])
        kvr, kvr_r = sbt("s_kvr", [NS, 256])
        op("act", lambda a: a.copy(kvr[:, :], pkr[0:NS, 0:256]), reads=[pkr_r], writes=[kvr_r])
        for s_ in range(NS):
            vm, vm_r = rt("s_vm", [NS, 128])
            op("dve", lambda v: v.tensor_scalar(vm[:, :], kvr[:, 128:256], cst[0:NS, C_ID * 128 + s_:C_ID * 128 + s_ + 1], None, ALU.mult),
               reads=[kvr_r, cst_r], writes=[vm_r])
            pO, pO_r = mmf(kvr[:, 0:128], vm[:, :], [kvr_r, vm_r])
            Sn, Sn_r = rt("s_Sn", [128, 128])
            op("dve", lambda v: v.scalar_tensor_tensor(Sn[:, :], Sall[:, s_, :], bc[:, 2, s_:s_ + 1], pO[:, 0:128], ALU.mult, ALU.add),
               reads=[Sall_r, bc_r, pO_r], writes=[Sn_r])
            store(grs_o[s_, :, :], Sn[:, :], Sn_r, "grs")

    def attn_batch(b):
        for G in (range(NTILE) if 'attn' in stages else []):
            racc, racc_r = rt("racc", [128, 512], F32, 1)
            op("pool", lambda g: g.memset(racc[:, :], 0.0), reads=[racc_r], writes=[racc_r])
            po, po_r = next_acc()
            op("pe", lambda pe: pe.matmul(po[:, 0:512], zeros_b[:, 0:128], zeros_b[:, 0:512], start=True, stop=False),
               reads=[zb_r], writes=[po_r], inc=False)
            for jb in range(4 * G + 3, -1, -1):
                c0 = max(0, jb - 4 * G) * 128
                wd = 512 - c0
                pz, pz_r = next_ps()
                op("pe", lambda pe: pe.matmul(pz[:, c0:512], KT[:, jb * 128:(jb + 1) * 128], QT[:, G * 512 + c0:(G + 1) * 512],
                                              start=True, stop=True), reads=[KT_r, QT_r], writes=[pz_r])
                e, e_r = rt("e", [128, 512])
                op("act", lambda a: a.activation(e[:, c0:512], pz[:, c0:512], AF.Exp, bias=prm[:, P_CB:P_CB + 1], scale=SB_SCALE),
                   reads=[pz_r, prm_r], writes=[e_r])
                if jb >= 4 * G:
                    op("dve", lambda g: g.tensor_tensor(e[:, c0:c0 + 128], e[:, c0:c0 + 128], cb(C_MST), ALU.mult),
                       reads=[e_r, cst_r], writes=[e_r])
                L, L_r = rt("L", [128, 512], BF16)
                op("act", lambda a: a.activation(L[:, c0:512], e[:, c0:512], AF.Ln, bias=1.0), reads=[e_r], writes=[L_r])
                pS, pS_r = next_ps()
                op("pe", lambda pe: pe.matmul(pS[:, c0:512], tril_b[:, :], L[:, c0:512], start=True, stop=True),
                   reads=[trb_r, L_r], writes=[pS_r])
                if jb > 0:
                    pR, pR_r = next_ps()
                    op("pe", lambda pe: pe.matmul(pR[:, c0:512], ones_b[:, :], L[:, c0:512], start=True, stop=True),
                       reads=[onb_r, L_r], writes=[pR_r])
                t, t_r = rt("t", [128, 512], F32, 1)
                op("dve", lambda v: v.tensor_tensor(t[:, c0:512], pS[:, c0:512], racc[:, c0:512], ALU.add),
                   reads=[pS_r, racc_r], writes=[t_r])
                op("act", lambda a: a.activation(t[:, c0:512], t[:, c0:512], AF.Exp, scale=-1.0), reads=[t_r], writes=[t_r])
                av, av_r = rt("a", [128, 512], BF16)
                op("dve", lambda g: g.tensor_tensor(av[:, c0:512], e[:, c0:512], t[:, c0:512], ALU.mult),
                   reads=[e_r, t_r], writes=[av_r])
                if jb > 0:
                    op("dve", lambda v: v.tensor_tensor(racc[:, c0:512], racc[:, c0:512], pR[:, c0:512], ALU.add),
                       reads=[pR_r, racc_r], writes=[racc_r])
                op("pe", lambda pe: pe.matmul(po[:, c0:512], VTM[:, jb, :], av[:, c0:512], start=False, stop=(jb == 0)),
                   reads=[VTM_r, av_r], writes=[po_r], inc=True)
            osb, osb_r = headnorm_out(po, po_r, 512, msb_o, 0, P_SBN)
            store(msb_o[:, b * T + G * 512:b * T + (G + 1) * 512], osb[:, :], osb_r, "msb")


    for b in range(NB):
        _sc = nc.named_scope('proj%d' % b); _sc.__enter__()
        for ti in range(NTILE):
            t0 = ti * 512
            gcol = b * T + t0

            def dst(cc, pt, pr, t0=t0, gcol=gcol, ti=ti):
                tn = 512
                if cc == 0:
                    op("act", lambda a: a.copy(QT[:, t0:t0 + tn], pt[:, 0:tn]), reads=[pr], writes=[QT_r])
                elif cc in (1, 2):
                    st, st_r = rt("kvst", [128, 512], F32, 1)
                    op("act", lambda a: a.copy(st[:, :], pt[:, 0:tn]), reads=[pr], writes=[st_r])
                    store((kT_o if cc == 1 else vT_o)[:, gcol:gcol + tn], st[:, :], st_r, "kT" if cc == 1 else "vT")
                    if cc == 1:
                        op("dve", lambda v: v.tensor_copy(KT[:, t0:t0 + tn], pt[:, 0:tn]), reads=[pr], writes=[KT_r])
                    else:
                        vb, vb_r = rt("vbf", [128, 512], BF16, 1)
                        op("dve", lambda v: v.tensor_copy(vb[:, :], pt[:, 0:tn]), reads=[pr], writes=[vb_r])
                        for k in range(4):
                            tp, tp_r = next_psb()
                            op("pe", lambda pe: pe.transpose(tp[:, 0:128], vb[:, k * 128:(k + 1) * 128], ident_b[:, :]),
                               reads=[vb_r, idb_r], writes=[tp_r])
                            op("act", lambda a: a.copy(VTM[:, ti * 4 + k, :], tp[:, 0:128]), reads=[tp_r], writes=[VTM_r])
                elif cc in (3, 4, 5):
                    ci, ci_r = cin[cc - 3]
                    q = cc - 3
                    if ti == 0:
                        op("pool", lambda g: g.memset(ci[:, 0:3], 0.0), reads=[ci_r], writes=[ci_r])
                    else:
                        op("pool", lambda g: g.tensor_copy(ci[:, 0:3], ci[:, 512:515]), reads=[ci_r], writes=[ci_r])
                    op("act", lambda a: a.copy(ci[:, 3:515], pt[:, 0:tn]), reads=[pr], writes=[ci_r])
                    if ti == NTILE - 1:
                        op("pool", lambda g: g.tensor_copy(gcv_s[:, q, b, :], ci[:, 512:515]), reads=[ci_r], writes=[gcv_r])
                    y, y_r = rt("cvy", [128, 512], F32, 1)
                    cw = lambda j: prm[:, P_CW + q * 4 + j:P_CW + q * 4 + j + 1]
                    op("dve", lambda v: v.tensor_scalar(y[:, :], ci[:, 0:512], cw(0), None, ALU.mult), reads=[ci_r, prm_r], writes=[y_r])
                    for j in (1, 2, 3):
                        op("dve", lambda v: v.scalar_tensor_tensor(y[:, :], ci[:, j:j + 512], cw(j), y[:, :], ALU.mult, ALU.add),
                           reads=[ci_r, prm_r, y_r], writes=[y_r])
                    op("act", lambda a: a.activation(y[:, :], y[:, :], AF.Silu), reads=[y_r], writes=[y_r])
                    if q == 0:
                        l2n_to(GQ[:, t0:t0 + tn], GQ_r, y, y_r, tn, HD ** -0.5)
                    elif q == 1:
                        l2n_to(GK[:, t0:t0 + tn], GK_r, y, y_r, tn, 1.0)
                    else:
                        op("dve", lambda v: v.tensor_copy(GV[:, t0:t0 + tn], y[:, :]), reads=[y_r], writes=[GV_r])
                else:
                    op("act", lambda a: a.activation(ZS[:, t0:t0 + tn], pt[:, 0:tn], AF.Silu), reads=[pr], writes=[ZS_r])

            proj_tile(gcol, 512, dst)
            for k in range(4):
                blk = ti * 4 + k
                pt, pr = next_ps()
                for kc in range(KC):
                    op("pe", lambda pe: pe.matmul(pt[:, 0:2], xn[:, kc, k * 128:(k + 1) * 128], w1[:, kc, 896:898],
                                                  start=(kc == 0), stop=(kc == KC - 1)),
                       reads=[w1_r, xn_rs[kc]], writes=[pr], inc=(kc == KC - 1))
                sp1, sp1_r = rt("sp1", [128, 1])
                softplus_to(sp1[:, :], sp1_r, pt[:, 0:1], [pr], prm[:, P_DTB:P_DTB + 1], 1.0, [128, 1])
                op("dve", lambda v: v.tensor_scalar(gtm[:, blk:blk + 1], sp1[:, :], nexpA[:, 0:1], None, ALU.mult),
                   reads=[sp1_r, nea_r, gtm_r], writes=[gtm_r])
                eb, eb_r = rt("eb", [128, 1])
                op("act", lambda a: a.activation(eb[:, :], pt[:, 1:2], AF.Exp, scale=-1.0), reads=[pr], writes=[eb_r])
                op("dve", lambda v: v.tensor_scalar(eb[:, :], eb[:, :], 1.0, None, ALU.add), reads=[eb_r], writes=[eb_r])
                op("dve", lambda v: v.reciprocal(btm[:, blk:blk + 1], eb[:, :]), reads=[eb_r, btm_r], writes=[btm_r])

        _sc.__exit__(None, None, None); _sc = nc.named_scope('mix%d' % b); _sc.__enter__()
        fns = []
        if 'attn' in stages:
            fns.append(lambda: attn_batch(b))
        if 'gdn' in stages:
            gstate[b] = dict(wy_done=0, scan_done=0, blk={})
            fns.append(lambda: gdn_wy(b))
            fns.append(lambda: gdn_scan(b))
        if not fns:
            pass
        elif 'attn' not in stages:
            PAR.run([lambda: None] + fns)
        else:
            PAR.run(fns)
        _sc.__exit__(None, None, None)
    store(gcv_o[:, :, :, :], gcv_s[:, :, :, :], gcv_r, "gcv")
    if do_sample:
        with nc.named_scope('sample'):
            sample_part()
    C.finish()
    S0.close()
    return nc

_O_SB_Q, _O_SB_K, _O_SB_V = 0, 1024, 2048
_O_GDN_QKV = 3072
_O_GDN_Z = _O_GDN_QKV + 3072
_O_GDN_A = _O_GDN_Z + 1024
_O_GDN_B = _O_GDN_A + 8
_NC_CACHE = {}


def _fm(a, kc):
    return np.ascontiguousarray(a.T.reshape(kc, 128, a.shape[0]).transpose(1, 0, 2))


def kernel(x_prompt, x_sample, cache_sb_k, cache_sb_v, page_table, state_gdn_conv,
           state_gdn_rec, state_ffn_conv, p_prompt, p_sample, attn_norm, w_in,
           sb_logit_bias, sb_out_norm, gdn_conv_w, gdn_a_log, gdn_dt_bias, gdn_out_norm,
           w_out, ffn_norm, w_ffn_gate, w_ffn_up, ffn_conv_w, w_ffn_down, ple_norm,
           w_ple_gate, w_ple_proj, final_norm):
    f32 = np.float32
    A = lambda v: np.asarray(v)
    x_prompt, x_sample = A(x_prompt).astype(f32), A(x_sample).astype(f32)
    B, T, _ = x_prompt.shape
    NS = x_sample.shape[0]
    NP = B * T
    w_in0 = A(w_in)[0]
    ck, cv = A(cache_sb_k)[0], A(cache_sb_v)[0]
    NPOOL = ck.shape[0]
    sgc, sgr, sfc = A(state_gdn_conv)[0], A(state_gdn_rec)[0], A(state_ffn_conv)[0]
    xall = np.concatenate([x_prompt.reshape(NP, D), x_sample.reshape(NS, D)], 0)
    xT = _fm(xall, KC)
    cst = make_consts()
    ptabT = np.ascontiguousarray(A(page_table).T.astype(np.int32))
    an = A(attn_norm)[0]
    gcw = A(gdn_conv_w)[0]
    in_maps = []
    for c in range(8):
        hc = slice(c * 128, (c + 1) * 128)
        cols = [w_in0[:, _O_SB_Q:][:, hc], w_in0[:, _O_SB_K:][:, hc], w_in0[:, _O_SB_V:][:, hc],
                w_in0[:, _O_GDN_QKV:][:, hc], w_in0[:, _O_GDN_QKV + 1024:][:, hc], w_in0[:, _O_GDN_QKV + 2048:][:, hc],
                w_in0[:, _O_GDN_Z:][:, hc], w_in0[:, _O_GDN_A + c:_O_GDN_A + c + 1], w_in0[:, _O_GDN_B + c:_O_GDN_B + c + 1]]
        w1 = np.ascontiguousarray(np.concatenate(cols, 1)).astype(f32)
        prm = np.zeros((128, NPRM), f32)
        prm[:, P_CB] = A(sb_logit_bias)[0, c]
        prm[:, P_SBN] = A(sb_out_norm)[0]
        prm[:, P_ALOG] = A(gdn_a_log)[0, c]
        prm[:, P_DTB] = A(gdn_dt_bias)[0, c]
        for q in range(3):
            for j in range(4):
                prm[:, P_CW + q * 4 + j] = gcw[j, q * 1024 + c * 128:q * 1024 + (c + 1) * 128]
        prm[:, P_GNC] = A(gdn_out_norm)[0]
        prm[:, P_GNR:P_GNR + 128] = A(gdn_out_norm)[0][None, :]
        prm[:, P_AN:P_AN + KC] = an.reshape(KC, 128).T
        gh = np.stack([sgc[:, :, q * 1024 + c * 128:q * 1024 + (c + 1) * 128] for q in range(3)], 0)
        in_maps.append(dict(
            xT=xT, w1=w1, prm=prm, cst=cst,
            poolk=np.ascontiguousarray(ck[:, :, c, :].reshape(NPOOL, -1)),
            poolv=np.ascontiguousarray(cv[:, :, c, :].reshape(NPOOL, -1)),
            ptab=ptabT, ghist=np.ascontiguousarray(gh.transpose(3, 0, 2, 1)),
            grec=np.ascontiguousarray(sgr[:, c])))
    key1 = ("l1", T, B, NS, NPOOL)
    if key1 not in _NC_CACHE:
        _NC_CACHE[key1] = build_l1(T=T, NB=B, NS=NS, NPOOL=NPOOL, do_sample=True)
    res1 = run_bass_kernel_spmd(_NC_CACHE[key1], in_maps, core_ids=list(range(8))).results
    del in_maps

    sb_k_p = np.zeros((1, B, T, 8, 128), f32)
    sb_v_p = np.zeros((1, B, T, 8, 128), f32)
    sb_k_s = np.zeros((1, NS, 1, 8, 128), f32)
    sb_v_s = np.zeros((1, NS, 1, 8, 128), f32)
    gconv_p = np.zeros((1, B, 3, 3072), f32)
    gconv_s = np.zeros((1, NS, 3, 3072), f32)
    grec_p = np.zeros((1, B, 8, 128, 128), f32)
    grec_s = np.zeros((1, NS, 8, 128, 128), f32)
    mixT = np.zeros((D, NP + NS), f32)
    for c in range(8):
        r = res1[c]
        kT, vT = np.asarray(r["kT_o"]), np.asarray(r["vT_o"])
        sb_k_p[0, :, :, c, :] = kT[:, :NP].T.reshape(B, T, 128)
        sb_v_p[0, :, :, c, :] = vT[:, :NP].T.reshape(B, T, 128)
        sb_k_s[0, :, 0, c, :] = kT[:, NP:].T
        sb_v_s[0, :, 0, c, :] = vT[:, NP:].T
        gcvo = np.asarray(r["gcv_o"])
        gcso = np.asarray(r["gcs_o"])
        for q in range(3):
            gconv_p[0, :, :, q * 1024 + c * 128:q * 1024 + (c + 1) * 128] = gcvo[:, q].transpose(1, 2, 0)
            gconv_s[0, :, :, q * 1024 + c * 128:q * 1024 + (c + 1) * 128] = gcso[:, q].transpose(2, 1, 0)
        grec_p[0, :, c] = np.asarray(r["grp_o"])
        grec_s[0, :, c] = np.asarray(r["grs_o"])
        mixT[c * 128:(c + 1) * 128] = np.asarray(r["msb_o"])
        mixT[1024 + c * 128:1024 + (c + 1) * 128] = np.asarray(r["mgd_o"])

    NM = (B * T) // 8
    per_b = T // NM
    pall = np.concatenate([A(p_prompt)[0].reshape(NP, -1), A(p_sample)[0].reshape(NS, -1)], 0).astype(f32)
    xallT = np.ascontiguousarray(xall.T)
    pallT = np.ascontiguousarray(pall.T)
    nrm3 = np.ascontiguousarray(np.stack([A(ffn_norm)[0], A(ple_norm)[0], A(final_norm)]).reshape(3, KC, 128).transpose(2, 0, 1)).astype(f32)
    fcw = _fm(A(ffn_conv_w)[0].astype(f32), NJ)
    wts = l2_weight_layouts(A(w_out)[0], A(w_ffn_gate)[0], A(w_ffn_up)[0], A(w_ffn_down)[0], A(w_ple_gate)[0], A(w_ple_proj)[0])
    in_maps = []
    for j in range(8):
        b, s = j // per_b, j % per_b
        g0 = b * T + s * NM
        cols = np.concatenate([np.arange(g0 - 2, g0 + NM), [NP + j]])
        valid = np.ones(NM + 3, bool)
        if s == 0:
            valid[0:2] = False
            cols[0:2] = g0

        def take(MT, kc):
            t = MT[:, cols].copy()
            t[:, ~valid] = 0.0
            return np.ascontiguousarray(t.reshape(kc, 128, NM + 3).transpose(1, 0, 2))
        m = dict(xT=take(xallT, KC), mixT=take(mixT, KC), pT=take(pallT, 2), fhist=_fm(sfc[j].astype(f32), NJ), nrm=nrm3, fcw=fcw)
        m.update(wts)
        in_maps.append(m)
    key2 = ("l2", NM)
    if key2 not in _NC_CACHE:
        _NC_CACHE[key2] = build_l2(NM)
    res2 = run_bass_kernel_spmd(_NC_CACHE[key2], in_maps, core_ids=list(range(8))).results
    y_p = np.zeros((B, T, D), f32)
    y_s = np.zeros((NS, 1, D), f32)
    fconv_p = np.zeros((1, B, 2, DFF), f32)
    fconv_s = np.zeros((1, NS, 2, DFF), f32)
    for j in range(8):
        b, s = j // per_b, j % per_b
        r = res2[j]
        yk = np.asarray(r["yT"]).transpose(2, 1, 0).reshape(NM + 1, D)
        y_p[b, s * NM:(s + 1) * NM] = yk[:NM]
        y_s[j, 0] = yk[NM]
        fc = np.asarray(r["fcnew"]).transpose(2, 1, 0).reshape(4, DFF)
        if s == per_b - 1:
            fconv_p[0, b] = fc[0:2]
        fconv_s[0, j, 0] = fc[3]
        fconv_s[0, j, 1] = fc[2]
    return (y_p, y_s, sb_k_p, sb_v_p, gconv_p, grec_p, fconv_p, sb_k_s, sb_v_s, gconv_s, grec_s, fconv_s)
```

```python
from concourse.bass_utils import run_bass_kernel_spmd
import contextlib
import numpy as np
import concourse.bass as bass
import concourse.mybir as mybir

F32 = mybir.dt.float32
BF16 = mybir.dt.bfloat16
I32 = mybir.dt.int32
AF = mybir.ActivationFunctionType
ALU = mybir.AluOpType
AX = mybir.AxisListType


class Res:
    __slots__ = ("name", "w", "r", "dsem", "dval", "excl", "dq")

    def __init__(self, name, inherit=None):
        self.name = name
        self.w = {}
        self.r = dict(inherit) if inherit else {}
        self.dsem = None
        self.dval = 0
        self.excl = False
        self.dq = None


class Ctx:
    def __init__(self, nc):
        self.nc = nc
        self.eng = {}
        for name, h in (("pe", nc.tensor), ("act", nc.scalar), ("dve", nc.vector),
                        ("pool", nc.gpsimd), ("sp", nc.sync)):
            sem = nc.alloc_semaphore("sem_" + name)
            self.eng[name] = dict(h=h, sem=sem, cnt=0, seen={})
        self.freed = {}
        self.out_toks = {}
        self.nres = 0
        self.dsem_pool = {}
        self.par = None

    def res(self, name=None):
        self.nres += 1
        return Res(name or f"r{self.nres}", inherit=self.freed)

    def sb(self, stack, name, shape, dt):
        t = stack.enter_context(self.nc.sbuf_tensor("sb_" + name, list(shape), dt))
        return t, self.res(name)

    def ps(self, stack, name, shape, dt):
        t = stack.enter_context(self.nc.psum_tensor("pp_" + name, list(shape), dt))
        r = self.res(name)
        r.excl = True
        return t, r

    def free(self, *ress):
        for r in ress:
            for d in (r.w, r.r):
                for s, v in d.items():
                    if self.freed.get(s, 0) < v:
                        self.freed[s] = v
            if r.dsem is not None:
                self.dsem_pool.setdefault(r.dq, []).append((r.dsem, r.dval))
                r.dsem = None

    def _wait(self, e, toks):
        E = self.eng[e]
        for sem, val in toks.items():
            if E["seen"].get(sem, 0) < val:
                E["h"].wait_ge(sem, val)
                E["seen"][sem] = val

    @staticmethod
    def _merge(dst, src):
        for s, v in src.items():
            if dst.get(s, 0) < v:
                dst[s] = v

    def _deps(self, e, reads, writes, skip_sem=None):
        toks = {}
        own = self.eng[e]["sem"]
        for r in reads:
            self._merge(toks, r.w)
            if r.excl:
                self._merge(toks, {s_: v_ for s_, v_ in r.r.items() if s_ != own})
        for w in writes:
            self._merge(toks, w.w)
            self._merge(toks, w.r)
        if skip_sem is not None:
            toks.pop(skip_sem, None)
        return toks

    def op(self, e, fn, reads=(), writes=(), inc=True):
        E = self.eng[e]
        if self.par is not None:
            inc = True
        toks = self._deps(e, reads, writes, skip_sem=E["sem"] if e == "pe" else None)
        self._wait(e, toks)
        ins = fn(E["h"])
        if inc:
            E["cnt"] += 1
            ins.then_inc(E["sem"], 1)
            val = E["cnt"]
        else:
            val = E["cnt"] + 1
        for r in reads:
            if r.r.get(E["sem"], 0) < val:
                r.r[E["sem"]] = val
        for w in writes:
            w.w = {E["sem"]: val}
            w.r = {}
        if self.par is not None:
            self.par.switch()
        return ins

    def dma(self, q, out, in_, reads=(), writes=(), part=False, is_out=False, owner=None, **kw):
        E = self.eng[q]
        W = owner if owner is not None else writes[0]
        if W.dsem is None:
            W.dq = "pool" if q == "pool" else "hw"
            if self.dsem_pool.get(W.dq):
                W.dsem, W.dval = self.dsem_pool[W.dq].pop()
            else:
                W.dsem = self.nc.alloc_semaphore("d_" + W.name)
                W.dval = 0
        toks = self._deps(q, reads, writes, skip_sem=W.dsem if part else None)
        self._wait(q, toks)
        ins = E["h"].dma_start(out=out, in_=in_, **kw)
        W.dval += 16
        ins.then_inc(W.dsem, 16)
        for r in reads:
            if r.r.get(W.dsem, 0) < W.dval:
                r.r[W.dsem] = W.dval
        for w in writes:
            if part:
                w.w[W.dsem] = W.dval
            else:
                w.w = {W.dsem: W.dval}
                w.r = {}
        if is_out:
            if self.out_toks.get(W.dsem, 0) < W.dval:
                self.out_toks[W.dsem] = W.dval
        return ins

    def indirect(self, out, in_, in_offset, reads=(), writes=(), part=False):
        E = self.eng["pool"]
        W = writes[0]
        if W.dsem is None:
            W.dq = "pool"
            if self.dsem_pool.get("pool"):
                W.dsem, W.dval = self.dsem_pool["pool"].pop()
            else:
                W.dsem = self.nc.alloc_semaphore("d_" + W.name)
                W.dval = 0
        toks = self._deps("pool", reads, writes, skip_sem=W.dsem if part else None)
        self._wait("pool", toks)
        ins = E["h"].indirect_dma_start(out=out, out_offset=None, in_=in_, in_offset=in_offset)
        W.dval += 16
        ins.then_inc(W.dsem, 16)
        for r in reads:
            if r.r.get(W.dsem, 0) < W.dval:
                r.r[W.dsem] = W.dval
        for w in writes:
            if part:
                w.w[W.dsem] = W.dval
            else:
                w.w = {W.dsem: W.dval}
                w.r = {}
        return ins

    def finish(self):
        toks = dict(self.out_toks)
        for name, E in self.eng.items():
            if name != "sp" and E["cnt"] > 0:
                toks[E["sem"]] = E["cnt"]
        self._wait("sp", toks)


class Par:
    def __init__(self, C):
        import threading
        self.C = C
        self.th = threading
        self.cv = threading.Condition()
        self.turn = 0
        self.alive = []
        self.ids = {}
        self.err = None
        self.weights = {}
        self.credit = {}

    def cur(self):
        return self.ids.get(self.th.get_ident(), None)

    def _next(self, i):
        n = len(self.alive)
        for k in range(1, n + 1):
            j = (i + k) % n
            if self.alive[j]:
                return j
        return -1

    def switch(self):
        i = self.cur()
        if i is None:
            return
        w = self.weights.get(i, 1)
        if w > 1:
            self.credit[i] = self.credit.get(i, 0) + 1
            if self.credit[i] % w != 0:
                return
        with self.cv:
            self.turn = self._next(i)
            self.cv.notify_all()
            while self.turn != i and self.err is None:
                self.cv.wait()
            if self.err is not None and self.turn != i:
                raise RuntimeError("peer failed")

    def idle(self):
        self.switch()

    def run(self, fns):
        self.alive = [True] * len(fns)
        self.turn = 0

        def body(i, fn):
            self.ids[self.th.get_ident()] = i
            try:
                with self.cv:
                    while self.turn != i and self.err is None:
                        self.cv.wait()
                if self.err is None:
                    fn()
            except BaseException as e:
                if self.err is None:
                    self.err = e
            finally:
                with self.cv:
                    self.alive[i] = False
                    self.turn = self._next(i)
                    self.cv.notify_all()
        ts = [self.th.Thread(target=body, args=(i, f)) for i, f in enumerate(fns)]
        self.C.par = self
        for t in ts:
            t.start()
        for t in ts:
            t.join()
        self.C.par = None
        if self.err is not None:
            raise self.err

D = 2048
KC = 16
DFF = 5504
NJ = 43
PLE = 256
EPS = 1e-6


def tiles_of(n, step=512):
    return [(s, min(step, n - s)) for s in range(0, n, step)]


def build_l2(NM=1024):
    NT = NM + 3
    TL = tiles_of(NT)
    nc = bass.Bass("TRN2", target_bir_lowering=False)
    dt_in = lambda name, shape, dt=F32: nc.dram_tensor(name, list(shape), dt, kind="ExternalInput")
    dt_out = lambda name, shape, dt=F32: nc.dram_tensor(name, list(shape), dt, kind="ExternalOutput")
    xT_d = dt_in("xT", [128, KC, NT])
    mixT_d = dt_in("mixT", [128, KC, NT])
    pT_d = dt_in("pT", [128, 2, NT])
    fh_d = dt_in("fhist", [128, NJ, 2])
    w_out_d = dt_in("w_out", [4, 128, KC, 512])
    w_gate_d = dt_in("w_gate", [22, 128, KC, 256])
    w_up_d = dt_in("w_up", [22, 128, KC, 256])
    w_down_d = dt_in("w_down", [KC, 128, NJ, 128])
    w_pg_d = dt_in("w_pg", [4, 128, KC, 512])
    w_pp_d = dt_in("w_pp", [128, 2, D])
    nrm_d = dt_in("nrm", [128, 3, KC])
    fcw_d = dt_in("fcw", [128, NJ, 3])
    yT_d = dt_out("yT", [128, KC, NM + 1])
    fc_d = dt_out("fcnew", [128, NJ, 4])
    hs_d = nc.dram_tensor("h_scratch", [128, KC, NT], F32)
    h2s_d = nc.dram_tensor("h2_scratch", [128, KC, NT], F32)

    C = Ctx(nc)
    hs_res = C.res("hs")
    with contextlib.ExitStack() as S0:
        S0.enter_context(nc.allow_low_precision("bf16 matmul operands, fp32 accumulation"))
        ones_bf, ones_r = C.sb(S0, "ones_bf", [128, 128], BF16)
        C.op("pool", lambda g: g.memset(ones_bf[:, :], 1.0), writes=[ones_r])
        nrm, nrm_r = C.sb(S0, "nrm", [128, 3, KC], F32)
        C.dma("sp", nrm[:, :, :], nrm_d[:, :, :], writes=[nrm_r])
        fcw, fcw_r = C.sb(S0, "fcw", [128, NJ, 3], F32)
        C.dma("sp", fcw[:, :, :], fcw_d[:, :, :], writes=[fcw_r])
        fh, fh_r = C.sb(S0, "fh", [128, NJ, 2], F32)
        C.dma("sp", fh[:, :, :], fh_d[:, :, :], writes=[fh_r])
        fcs, fcs_r = C.sb(S0, "fcs", [128, NJ, 4], F32)
        C.op("pool", lambda p: p.tensor_copy(fcs[:, :, 3], fh[:, :, 1]), reads=[fh_r], writes=[fcs_r])
        rstd, rstd_r = C.sb(S0, "rstd", [128, NT], F32)
        PS = [C.ps(S0, f"ps{i}", [128, 512], F32) for i in range(8)]
        psi = [0]

        def next_ps():
            t = PS[psi[0] % 8]
            psi[0] += 1
            return t

        def rms_stats(src, src_rs, sq_bufs):
            pss = [next_ps() for _ in TL]
            for kc in range(KC):
                sq, sq_r = sq_bufs[kc % len(sq_bufs)]
                C.op("act", lambda a: a.activation(sq[:, :], src[:, kc, :], AF.Square),
                     reads=[src_rs[kc]], writes=[sq_r])
                for ti, (t0, tn) in enumerate(TL):
                    pt, pr = pss[ti]
                    C.op("pe", lambda pe: pe.matmul(pt[:, 0:tn], ones_bf[:, :], sq[:, t0:t0 + tn],
                                                    start=(kc == 0), stop=(kc == KC - 1)),
                         reads=[ones_r, sq_r], writes=[pr], inc=(kc == KC - 1) or True)
            for ti, (t0, tn) in enumerate(TL):
                pt, pr = pss[ti]
                C.op("dve", lambda v: v.tensor_scalar(rstd[:, t0:t0 + tn], pt[:, 0:tn], 1.0 / D, EPS,
                                                      ALU.mult, ALU.add), reads=[pr], writes=[rstd_r])
            C.op("act", lambda a: a.activation(rstd[:, :], rstd[:, :], AF.Sqrt), reads=[rstd_r], writes=[rstd_r])
            C.op("dve", lambda v: v.reciprocal(rstd[:, :], rstd[:, :]), reads=[rstd_r], writes=[rstd_r])

        _sc = nc.named_scope('st1'); _sc.__enter__()
        Sf = contextlib.ExitStack()
        fT, _ = C.sb(Sf, "fT", [128, KC, NT], BF16)
        fT_rs = [C.res(f"fT{k}") for k in range(KC)]
        S1 = contextlib.ExitStack()
        hT, _ = C.sb(S1, "hT", [128, KC, NT], F32)
        hT_rs = [C.res(f"hT{k}") for k in range(KC)]
        for kc in range(KC):
            C.dma("sp", hT[:, kc, :], xT_d[:, kc, :], writes=[hT_rs[kc]])
        Sm = contextlib.ExitStack()
        mixT, _ = C.sb(Sm, "mixT", [128, KC, NT], BF16)
        mix_rs = [C.res(f"mix{k}") for k in range(KC)]
        for kc in range(KC):
            C.dma("pool", mixT[:, kc, :], mixT_d[:, kc, :], writes=[mix_rs[kc]])
        wbufs = [C.sb(Sm, f"wo{i}", [128, KC, 512], BF16) for i in range(2)]
        pass
        for g in range(4):
            wt, wr = wbufs[g % 2]
            C.dma("pool", wt[:, :, :], w_out_d[g, :, :, :], writes=[wr])
            for o in range(4):
                oc = g * 4 + o
                for (t0, tn) in TL:
                    pt, pr = next_ps()
                    for kc in range(KC):
                        C.op("pe", lambda pe: pe.matmul(pt[:, 0:tn], wt[:, kc, o * 128:(o + 1) * 128],
                                                        mixT[:, kc, t0:t0 + tn], start=(kc == 0), stop=(kc == KC - 1)),
                             reads=[wr, mix_rs[kc]], writes=[pr], inc=(kc == KC - 1))
                    C.op("dve", lambda v: v.tensor_tensor(hT[:, oc, t0:t0 + tn], hT[:, oc, t0:t0 + tn], pt[:, 0:tn], ALU.add),
                         reads=[pr, hT_rs[oc]], writes=[hT_rs[oc]])
        C.free(*mix_rs, *[r for _, r in wbufs])
        Sm.close()

        _sc.__exit__(None, None, None); _sc = nc.named_scope('st2'); _sc.__enter__()
        Sq = contextlib.ExitStack()
        sq_bufs = [C.sb(Sq, f"sq{i}", [128, NT], BF16) for i in range(2)]
        rms_stats(hT, hT_rs, sq_bufs)
        for kc in range(KC):
            C.op("dve", lambda v: v.scalar_tensor_tensor(fT[:, kc, :], hT[:, kc, :], nrm[:, 0, kc:kc + 1], rstd[:, :],
                                                         ALU.mult, ALU.mult),
                 reads=[hT_rs[kc], nrm_r, rstd_r], writes=[fT_rs[kc]])
            C.dma("sp", hs_d[:, kc, :], hT[:, kc, :], reads=[hT_rs[kc]], writes=[hs_res], part=True)
        C.free(*hT_rs, *[r for _, r in sq_bufs])
        Sq.close()
        S1.close()

        _sc.__exit__(None, None, None); _sc = nc.named_scope('st3'); _sc.__enter__()
        Sa = contextlib.ExitStack()
        actT, _ = C.sb(Sa, "actT", [128, NJ, NT], BF16)
        act_rs = [C.res(f"act{j}") for j in range(NJ)]
        Sg = contextlib.ExitStack()
        GW = 256
        wg_b = [C.sb(Sg, f"wg{i}", [128, KC, GW], BF16) for i in range(2)]
        wu_b = [C.sb(Sg, f"wu{i}", [128, KC, GW], BF16) for i in range(2)]
        gp_b = [C.sb(Sg, f"gp{i}", [128, NT], F32) for i in range(2)]
        cv_b = [C.sb(Sg, f"cv{i}", [128, NT], F32) for i in range(2)]
        sl_b = [C.sb(Sg, f"sl{i}", [128, NT], F32) for i in range(2)]
        ub_b = [C.sb(Sg, f"ub{i}", [128, NT], BF16) for i in range(2)]
        pass
        pass
        ngrp = (DFF + GW - 1) // GW
        pend = [None]
        stg_b = [C.sb(Sg, f"stg{i}", [128, 4, GW], F32) for i in range(4)]
        wg_rs = [[C.res(f"wg{i}_{k}") for k in range(4)] for i in range(2)]
        wu_rs = [[C.res(f"wu{i}_{k}") for k in range(4)] for i in range(2)]
        stgi = [0]

        def load_group(g):
            for (wb, rs, src) in ((wg_b, wg_rs, w_gate_d), (wu_b, wu_rs, w_up_d)):
                wt, _ = wb[g % 2]
                for k in range(4):
                    stg, stg_r = stg_b[stgi[0] % 4]
                    C.dma("sp", stg[:, :, :], src[g, :, 4 * k:4 * k + 4, :], writes=[stg_r])
                    if stgi[0] % 2 == 0:
                        C.op("act", lambda a: a.copy(wt[:, 4 * k:4 * k + 4, :], stg[:, :, :]), reads=[stg_r], writes=[rs[g % 2][k]])
                    else:
                        C.op("dve", lambda v: v.tensor_copy(wt[:, 4 * k:4 * k + 4, :], stg[:, :, :]), reads=[stg_r], writes=[rs[g % 2][k]])
                    stgi[0] += 1

        load_group(0)
        for g in range(ngrp):
            c0 = g * GW
            cw = min(GW, DFF - c0)
            wg, _ = wg_b[g % 2]
            wu, _ = wu_b[g % 2]
            if g + 1 < ngrp:
                load_group(g + 1)
            for o in range(cw // 128):
                j = (c0 // 128) + o
                gp, gp_r = gp_b[j % 2]
                cv, cv_r = cv_b[j % 2]
                sl, sl_r = sl_b[j % 2]
                pus = []
                for (t0, tn) in TL:
                    pg, pg_r = next_ps()
                    for kc in range(KC):
                        C.op("pe", lambda pe: pe.matmul(pg[:, 0:tn], wg[:, kc, o * 128:(o + 1) * 128],
                                                        fT[:, kc, t0:t0 + tn], start=(kc == 0), stop=(kc == KC - 1)),
                             reads=[wg_rs[g % 2][kc // 4], fT_rs[kc]], writes=[pg_r], inc=(kc == KC - 1))
                    C.op("act", lambda a: a.copy(gp[:, t0:t0 + tn], pg[:, 0:tn]), reads=[pg_r], writes=[gp_r])
                    pu, pu_r = next_ps()
                    for kc in range(KC):
                        C.op("pe", lambda pe: pe.matmul(pu[:, 0:tn], wu[:, kc, o * 128:(o + 1) * 128],
                                                        fT[:, kc, t0:t0 + tn], start=(kc == 0), stop=(kc == KC - 1)),
                             reads=[wu_rs[g % 2][kc // 4], fT_rs[kc]], writes=[pu_r], inc=(kc == KC - 1))
                    ub, ub_r = ub_b[j % 2]
                    C.op("act", lambda a: a.copy(ub[:, t0:t0 + tn], pu[:, 0:tn]), reads=[pu_r], writes=[ub_r])
                    pus.append((ub, ub_r, t0, tn))
                C.op("act", lambda a: a.activation(cv[:, 2:NM + 2], gp[:, 0:NM], AF.Copy, scale=fcw[:, j, 0:1]),
                     reads=[gp_r, fcw_r], writes=[cv_r])
                C.op("dve", lambda p: p.scalar_tensor_tensor(cv[:, 2:NM + 2], gp[:, 1:NM + 1], fcw[:, j, 1:2], cv[:, 2:NM + 2],
                                                              ALU.mult, ALU.add), reads=[gp_r, fcw_r, cv_r], writes=[cv_r])
                C.op("dve", lambda v: v.scalar_tensor_tensor(cv[:, 2:NM + 2], gp[:, 2:NM + 2], fcw[:, j, 2:3], cv[:, 2:NM + 2],
                                                             ALU.mult, ALU.add), reads=[gp_r, fcw_r, cv_r], writes=[cv_r])
                C.op("pool", lambda p: p.memset(cv[:, 0:2], 0.0), writes=[cv_r], reads=[cv_r])
                sc = NM + 2
                C.op("dve", lambda v: v.tensor_scalar(cv[:, sc:sc + 1], fh[:, j, 0:1], fcw[:, j, 0:1], None, ALU.mult),
                     reads=[fh_r, fcw_r, cv_r], writes=[cv_r])
                C.op("dve", lambda v: v.scalar_tensor_tensor(cv[:, sc:sc + 1], fh[:, j, 1:2], fcw[:, j, 1:2], cv[:, sc:sc + 1],
                                                             ALU.mult, ALU.add), reads=[fh_r, fcw_r, cv_r], writes=[cv_r])
                C.op("dve", lambda v: v.scalar_tensor_tensor(cv[:, sc:sc + 1], gp[:, sc:sc + 1], fcw[:, j, 2:3], cv[:, sc:sc + 1],
                                                             ALU.mult, ALU.add), reads=[gp_r, fcw_r, cv_r], writes=[cv_r])
                C.op("pool", lambda p: p.tensor_copy(fcs[:, j, 0:3], gp[:, NM:NM + 3]), reads=[gp_r, fcs_r], writes=[fcs_r])
                def tail(j=j, sl=sl, sl_r=sl_r, cv=cv, cv_r=cv_r, ub=ub, ub_r=ub_r):
                    C.op("act", lambda a: a.activation(sl[:, :], cv[:, :], AF.Silu), reads=[cv_r], writes=[sl_r])
                    C.op("dve", lambda v: v.tensor_tensor(actT[:, j, :], sl[:, :], ub[:, :], ALU.mult),
                         reads=[sl_r, ub_r], writes=[act_rs[j]])
                if pend[0] is not None:
                    pend[0]()
                pend[0] = tail
        pend[0]()
        C.dma("sp", fc_d[:, :, :], fcs[:, :, :], reads=[fcs_r], writes=[C.res("fc_out")], is_out=True)
        C.free(*[r for _, r in wg_b + wu_b + gp_b + cv_b + sl_b + ub_b + stg_b], *[r for l in wg_rs + wu_rs for r in l])
        Sg.close()

        _sc.__exit__(None, None, None); _sc = nc.named_scope('st4'); _sc.__enter__()
        Sd = contextlib.ExitStack()
        wd_b = [C.sb(Sd, f"wd{i}", [128, NJ, 128], BF16) for i in range(2)]
        hc_b = [C.sb(Sd, f"hc{i}", [128, NT], F32) for i in range(2)]
        h2s_res = C.res("h2s")
        pass
        JS = [(0, 11), (11, 11), (22, 11), (33, 10)]
        stg4 = [C.sb(Sd, f"stgd{i}", [128, 11, 128], F32) for i in range(4)]
        wd_rs = [[C.res(f"wd{i}_{k}") for k in range(4)] for i in range(2)]
        s4i = [0]

        def load_wd(oc):
            wd, _ = wd_b[oc % 2]
            for k, (j0, jn) in enumerate(JS):
                stg, stg_r = stg4[s4i[0] % 4]
                C.dma("sp", stg[:, 0:jn, :], w_down_d[oc, :, j0:j0 + jn, :], writes=[stg_r])
                if s4i[0] % 2 == 0:
                    C.op("act", lambda a: a.copy(wd[:, j0:j0 + jn, :], stg[:, 0:jn, :]), reads=[stg_r], writes=[wd_rs[oc % 2][k]])
                else:
                    C.op("dve", lambda v: v.tensor_copy(wd[:, j0:j0 + jn, :], stg[:, 0:jn, :]), reads=[stg_r], writes=[wd_rs[oc % 2][k]])
                s4i[0] += 1

        load_wd(0)
        for oc in range(KC):
            wd, _ = wd_b[oc % 2]
            hc, hc_r = hc_b[oc % 2]
            if oc + 1 < KC:
                load_wd(oc + 1)
            C.dma("sp", hc[:, :], hs_d[:, oc, :], reads=[hs_res], writes=[hc_r])
            for (t0, tn) in TL:
                pt, pr = next_ps()
                for j in range(NJ):
                    C.op("pe", lambda pe: pe.matmul(pt[:, 0:tn], wd[:, j, :], actT[:, j, t0:t0 + tn],
                                                    start=(j == 0), stop=(j == NJ - 1)),
                         reads=[wd_rs[oc % 2][min(j // 11, 3)], act_rs[j]], writes=[pr], inc=(j == NJ - 1))
                C.op("dve", lambda v: v.tensor_tensor(hc[:, t0:t0 + tn], hc[:, t0:t0 + tn], pt[:, 0:tn], ALU.add),
                     reads=[pr, hc_r], writes=[hc_r])
            C.dma("sp", h2s_d[:, oc, :], hc[:, :], reads=[hc_r], writes=[h2s_res], part=True, owner=hc_r)
        C.free(*[r for _, r in wd_b + hc_b + stg4], *[r for l in wd_rs for r in l], *act_rs, *fT_rs)
        Sd.close()
        Sa.close()
        Sf.close()
        Sh = contextlib.ExitStack()
        h2T, _ = C.sb(Sh, "h2T", [128, KC, NT], F32)
        h2_rs = [C.res(f"h2{k}") for k in range(KC)]
        for kc in range(KC):
            C.dma("sp", h2T[:, kc, :], h2s_d[:, kc, :], reads=[h2s_res], writes=[h2_rs[kc]])

        _sc.__exit__(None, None, None); _sc = nc.named_scope('st5'); _sc.__enter__()
        Sp = contextlib.ExitStack()
        gT, _ = C.sb(Sp, "gT", [128, KC, NT], BF16)
        gT_rs = [C.res(f"gT{k}") for k in range(KC)]
        sq_bufs = [C.sb(Sp, f"sqb{i}", [128, NT], BF16) for i in range(2)]
        rms_stats(h2T, h2_rs, sq_bufs)
        for kc in range(KC):
            C.op("dve", lambda v: v.scalar_tensor_tensor(gT[:, kc, :], h2T[:, kc, :], nrm[:, 1, kc:kc + 1], rstd[:, :],
                                                         ALU.mult, ALU.mult),
                 reads=[h2_rs[kc], nrm_r, rstd_r], writes=[gT_rs[kc]])
        pT, pT_r = C.sb(Sp, "pT", [128, 2, NT], BF16)
        C.dma("pool", pT[:, :, :], pT_d[:, :, :], writes=[pT_r])
        wpp, wpp_r = C.sb(Sp, "wpp", [128, 2, D], BF16)
        C.dma("pool", wpp[:, :, :], w_pp_d[:, :, :], writes=[wpp_r])
        wpg_b = [C.sb(Sp, f"wpg{i}", [128, KC, 512], BF16) for i in range(2)]
        sg_b = [C.sb(Sp, f"sg{i}", [128, 512], F32) for i in range(2)]
        pass
        cnt = 0
        for g in range(4):
            wt, wr = wpg_b[g % 2]
            C.dma("pool", wt[:, :, :], w_pg_d[g, :, :, :], writes=[wr])
            for o in range(4):
                oc = g * 4 + o
                for (t0, tn) in TL:
                    pt, pr = next_ps()
                    for kc in range(KC):
                        C.op("pe", lambda pe: pe.matmul(pt[:, 0:tn], wt[:, kc, o * 128:(o + 1) * 128],
                                                        gT[:, kc, t0:t0 + tn], start=(kc == 0), stop=(kc == KC - 1)),
                             reads=[wr, gT_rs[kc]], writes=[pr], inc=(kc == KC - 1))
                    sg, sg_r = sg_b[cnt % 2]
                    cnt += 1
                    C.op("act", lambda a: a.activation(sg[:, 0:tn], pt[:, 0:tn], AF.Sigmoid), reads=[pr], writes=[sg_r])
                    p2, p2r = next_ps()
                    for kc in range(2):
                        C.op("pe", lambda pe: pe.matmul(p2[:, 0:tn], wpp[:, kc, oc * 128:(oc + 1) * 128],
                                                        pT[:, kc, t0:t0 + tn], start=(kc == 0), stop=(kc == 1)),
                             reads=[wpp_r, pT_r], writes=[p2r], inc=(kc == 1))
                    C.op("dve", lambda v: v.tensor_tensor(sg[:, 0:tn], sg[:, 0:tn], p2[:, 0:tn], ALU.mult),
                         reads=[sg_r, p2r], writes=[sg_r])
                    C.op("dve", lambda p: p.tensor_tensor(h2T[:, oc, t0:t0 + tn], h2T[:, oc, t0:t0 + tn], sg[:, 0:tn], ALU.add),
                         reads=[sg_r, h2_rs[oc]], writes=[h2_rs[oc]])
        _sc.__exit__(None, None, None); _sc = nc.named_scope('st6'); _sc.__enter__()
        rms_stats(h2T, h2_rs, sq_bufs)
        yb = [C.sb(Sp, f"yb{i}", [128, NM + 1], F32) for i in range(2)]
        y_res = C.res("y_out")
        for kc in range(KC):
            yt, yr = yb[kc % 2]
            C.op("dve", lambda v: v.scalar_tensor_tensor(yt[:, :], h2T[:, kc, 2:NT], nrm[:, 2, kc:kc + 1], rstd[:, 2:NT],
                                                         ALU.mult, ALU.mult),
                 reads=[h2_rs[kc], nrm_r, rstd_r], writes=[yr])
            C.dma("sp", yT_d[:, kc, :], yt[:, :], reads=[yr], writes=[y_res], part=True, is_out=True, owner=yr)
        _sc.__exit__(None, None, None)
        C.finish()
        Sp.close()
        Sh.close()
    return nc


def l2_weight_layouts(w_out, w_gate, w_up, w_down, w_pg, w_pp):
    f32 = np.float32

    def grp(w, gw):
        n = w.shape[1]
        ng = (n + gw - 1) // gw
        wp = np.zeros((w.shape[0], ng * gw), f32)
        wp[:, :n] = w
        return np.ascontiguousarray(wp.reshape(KC, 128, ng, gw).transpose(2, 1, 0, 3))
    return dict(w_out=grp(w_out, 512), w_gate=grp(w_gate, 256), w_up=grp(w_up, 256),
                w_down=np.ascontiguousarray(w_down.astype(f32).reshape(NJ, 128, KC, 128).transpose(2, 1, 0, 3)),
                w_pg=grp(w_pg, 512),
                w_pp=np.ascontiguousarray(w_pp.astype(f32).reshape(2, 128, D).transpose(1, 0, 2)))


HD = 128
SB_SCALE = HD ** -0.5
NW = 898
C_ID, C_TRIL, C_MST, C_UBD, C_BONES, C_MPOS, C_MNEG, C_SEL0, C_SEL1, C_SU = range(10)
NCST = 10 * 128
P_CB, P_SBN, P_ALOG, P_DTB, P_CW, P_GNC, P_GNR, P_AN = 0, 1, 2, 3, 4, 16, 17, 17 + 128
NPRM = P_AN + KC


def make_consts():
    c = np.zeros((128, 10, 128), np.float32)
    i = np.arange(128)
    same = (i[:, None] // 64) == (i[None, :] // 64)
    c[:, C_ID] = np.eye(128)
    c[:, C_TRIL] = (i[:, None] >= i[None, :])
    c[:, C_MST] = (i[None, :] > i[:, None])
    c[:, C_UBD] = (i[:, None] <= i[None, :]) & same
    c[:, C_BONES] = same
    c[:, C_MPOS] = np.where((i[:, None] > i[None, :]) & same, 0.0, 30000.0)
    c[:, C_MNEG] = np.where((i[None, :] >= i[:, None]) & same, 0.0, -30000.0)
    c[:, C_SEL0] = (i[:, None] < 64) * np.ones((1, 128))
    c[:, C_SEL1] = (i[:, None] >= 64) * np.ones((1, 128))
    c[:, C_SU] = (i[:, None] > i[None, :])
    return c.reshape(128, NCST)


def build_l1(T=4096, NB=2, NS=8, NPOOL=1280, do_sample=True, stages=('attn', 'gdn'), par=True):
    NTOK = NB * T + NS
    NBLK = T // 128
    NTILE = T // 512
    nc = bass.Bass("TRN2", target_bir_lowering=False)
    din = lambda name, shape, dt=F32: nc.dram_tensor(name, list(shape), dt, kind="ExternalInput")
    dout = lambda name, shape, dt=F32: nc.dram_tensor(name, list(shape), dt, kind="ExternalOutput")
    xT_d = din("xT", [128, KC, NTOK])
    w1_d = din("w1", [D, NW])
    prm_d = din("prm", [128, NPRM])
    cst_d = din("cst", [128, NCST])
    poolk_d = din("poolk", [NPOOL, 128 * HD])
    poolv_d = din("poolv", [NPOOL, 128 * HD])
    ptab_d = din("ptab", [128, NS], I32)
    ghist_d = din("ghist", [128, 3, 3, NS])
    grec_d = din("grec", [NS, 128, 128])
    kT_o = dout("kT_o", [128, NTOK])
    vT_o = dout("vT_o", [128, NTOK])
    msb_o = dout("msb_o", [128, NTOK])
    mgd_o = dout("mgd_o", [128, NTOK])
    gcv_o = dout("gcv_o", [128, 3, NB, 3])
    gcs_o = dout("gcs_o", [128, 3, 3, NS])
    grp_o = dout("grp_o", [NB, 128, 128])
    grs_o = dout("grs_o", [NS, 128, 128])

    C = Ctx(nc)
    S0 = contextlib.ExitStack()
    S0.enter_context(nc.allow_low_precision("bf16 matmul operands, fp32 accumulation"))
    sbt = lambda name, shape, dt=F32: C.sb(S0, name, shape, dt)
    op = C.op

    cst, cst_r = sbt("cst", [128, NCST])
    C.dma("sp", cst[:, :], cst_d[:, :], writes=[cst_r])
    prm, prm_r = sbt("prm", [128, NPRM])
    C.dma("sp", prm[:, :], prm_d[:, :], writes=[prm_r])
    cb = lambda k: cst[:, k * 128:(k + 1) * 128]
    ident_f = cb(C_ID)
    w1, w1_r = sbt("w1", [128, KC, NW], BF16)
    C.dma("pool", w1[:, :, :], w1_d.ap().rearrange("(kc p) n -> p kc n", p=128), writes=[w1_r])
    ident_b, idb_r = sbt("ident_b", [128, 128], BF16)
    op("dve", lambda v: v.tensor_copy(ident_b[:, :], ident_f), reads=[cst_r], writes=[idb_r])
    tril_b, trb_r = sbt("tril_b", [128, 128], BF16)
    op("dve", lambda v: v.tensor_copy(tril_b[:, :], cb(C_TRIL)), reads=[cst_r], writes=[trb_r])
    ones_b, onb_r = sbt("ones_b", [128, 128], BF16)
    op("pool", lambda g: g.memset(ones_b[:, :], 1.0), writes=[onb_r])
    ones_f, onf_r = sbt("ones_f", [128, 128])
    op("pool", lambda g: g.memset(ones_f[:, :], 1.0), writes=[onf_r])
    zeros_b, zb_r = sbt("zeros_b", [128, 512], BF16)
    op("pool", lambda g: g.memset(zeros_b[:, :], 0.0), writes=[zb_r])
    nexpA, nea_r = sbt("nexpA", [128, 1])
    op("act", lambda a: a.activation(nexpA[:, :], prm[:, P_ALOG:P_ALOG + 1], AF.Exp), reads=[prm_r], writes=[nea_r])
    op("dve", lambda v: v.tensor_scalar(nexpA[:, :], nexpA[:, :], -1.0, None, ALU.mult), reads=[nea_r], writes=[nea_r])

    PS = [C.ps(S0, f"ps{i}", [128, 512], F32) for i in range(6)]
    PSB = [C.ps(S0, f"psb{i}", [128, 1024], BF16) for i in range(1)]
    PSG = [C.ps(S0, f"psg{i}", [128, 512], F32) for i in range(1)] + PS[1:3]
    psbi = [0]
    psi = [0]

    def next_psb():
        t = PSB[0]
        psbi[0] += 1
        return t

    PAR = Par(C)
    import os as _os
    PAR.weights = {0: int(_os.environ.get('ATTW', '1')), 1: int(_os.environ.get('WYW', '2')), 2: int(_os.environ.get('SCW', '1'))}
    ps_cnt = {}

    def next_ps():
        who = PAR.cur() if C.par is not None else None
        if who == 0:
            lst = PS[3:5]
        elif who == 1:
            lst = PS[1:3]
        elif who == 2:
            lst = [PSG[0], PS[5]]
        else:
            lst = PS[1:6]
        k = ps_cnt.get(who, 0)
        ps_cnt[who] = k + 1
        return lst[k % len(lst)]

    def next_acc():
        return PS[0]

    rot = {}

    def rt(name, shape, dt=F32, n=2):
        if name not in rot:
            rot[name] = [[sbt(f"{name}_{i}", shape, dt) for i in range(n)], 0]
        lst = rot[name]
        t = lst[0][lst[1] % n]
        lst[1] += 1
        return t

    def rsqrt_to(out_ap, out_r, src_ap, src_rs, scale, n):
        tmp, tmp_r = rt(f"rsq{n}", [128, n], F32, 1)
        w = out_ap.shape[-1]
        op("dve", lambda v: v.tensor_scalar(tmp[:, 0:w], src_ap, scale, EPS, ALU.mult, ALU.add), reads=src_rs, writes=[tmp_r])
        op("act", lambda a: a.activation(tmp[:, 0:w], tmp[:, 0:w], AF.Ln), reads=[tmp_r], writes=[tmp_r])
        op("act", lambda a: a.activation(out_ap, tmp[:, 0:w], AF.Exp, scale=-0.5), reads=[tmp_r], writes=[out_r])

    Sm = S0
    BIGW = max(8 * T, 2 * 128 * HD)
    BIG, _ = C.sb(Sm, "BIG", [128, BIGW], BF16)
    bigi = [0]

    def big(name):
        i = bigi[0]
        bigi[0] += 1
        return BIG[:, i * T:(i + 1) * T], C.res(name)
    QT, QT_r = big("QT")
    KT, KT_r = big("KT")
    VTMf, VTM_r = big("VTM")
    VTM = VTMf.rearrange("p (b d) -> p b d", d=128)
    GQ, GQ_r = big("GQ")
    GK, GK_r = big("GK")
    GV, GV_r = big("GV")
    ZS, ZS_r = big("ZS")
    gtm, gtm_r = C.sb(Sm, "gtm", [128, NBLK], F32)
    btm, btm_r = C.sb(Sm, "btm", [128, NBLK], F32)
    xt, _ = C.sb(Sm, "xt", [128, KC, 256], F32)
    xt_rs = [C.res(f"xt{k}") for k in range(KC)]
    xn, _ = C.sb(Sm, "xn", [128, KC, 512], BF16)
    xn_rs = [C.res(f"xn{k}") for k in range(KC)]
    cin = [C.sb(Sm, f"cin{i}", [128, 3 + 512], F32) for i in range(3)]
    gcv_s, gcv_r = sbt("gcv_s", [128, 3, NB, 3])

    def proj_tile(col0, tn, dst):
        for h0 in range(0, tn, 256):
            hn = min(256, tn - h0)
            for kc in range(KC):
                C.dma("sp", xt[:, kc, 0:hn], xT_d[:, kc, col0 + h0:col0 + h0 + hn], writes=[xt_rs[kc]])
            pss, pss_r = next_ps()
            for kc in range(KC):
                sq, sq_r = rt("sq", [128, 512], BF16)
                op("act", lambda a: a.activation(sq[:, 0:hn], xt[:, kc, 0:hn], AF.Square), reads=[xt_rs[kc]], writes=[sq_r])
                op("pe", lambda pe: pe.matmul(pss[:, 0:hn], ones_b[:, :], sq[:, 0:hn], start=(kc == 0), stop=(kc == KC - 1)),
                   reads=[onb_r, sq_r], writes=[pss_r], inc=True)
            rstd, rstd_r = rt("rstd", [128, 512], F32, 1)
            rsqrt_to(rstd[:, 0:hn], rstd_r, pss[:, 0:hn], [pss_r], 1.0 / D, 512)
            for kc in range(KC):
                op("dve", lambda v: v.scalar_tensor_tensor(xn[:, kc, h0:h0 + hn], xt[:, kc, 0:hn], prm[:, P_AN + kc:P_AN + kc + 1],
                                                           rstd[:, 0:hn], ALU.mult, ALU.mult),
                   reads=[xt_rs[kc], prm_r, rstd_r], writes=[xn_rs[kc]])
        for cc in range(7):
            pt, pr = next_ps()
            for kc in range(KC):
                op("pe", lambda pe: pe.matmul(pt[:, 0:tn], w1[:, kc, cc * 128:(cc + 1) * 128], xn[:, kc, 0:tn],
                                              start=(kc == 0), stop=(kc == KC - 1)),
                   reads=[w1_r, xn_rs[kc]], writes=[pr], inc=(kc == KC - 1))
            dst(cc, pt, pr)

    def l2n_to(out_ap, out_r, y, y_r, tn, scale):
        sq, sq_r = rt("sq", [128, 512], BF16)
        op("act", lambda a: a.activation(sq[:, 0:tn], y[:, 0:tn], AF.Square), reads=[y_r], writes=[sq_r])
        pt, pr = next_ps()
        op("pe", lambda pe: pe.matmul(pt[:, 0:tn], ones_b[:, :], sq[:, 0:tn], start=True, stop=True),
           reads=[onb_r, sq_r], writes=[pr])
        rs, rs_r = rt("rstd", [128, 512], F32, 1)
        rsqrt_to(rs[:, 0:tn], rs_r, pt[:, 0:tn], [pr], 1.0, 512)
        op("dve", lambda v: v.scalar_tensor_tensor(out_ap, y[:, 0:tn], scale, rs[:, 0:tn], ALU.mult, ALU.mult),
           reads=[y_r, rs_r], writes=[out_r])

    def softplus_to(out_ap, out_r, src_ap, src_rs, bias_ap, scale, shape):
        tmp, tmp_r = rt("spl%d_%d" % tuple(shape), shape)
        kw = dict(scale=scale)
        if bias_ap is not None:
            kw["bias"] = bias_ap
        op("act", lambda a: a.activation(tmp[:, :], src_ap, AF.Exp, **kw), reads=src_rs + [prm_r], writes=[tmp_r])
        op("act", lambda a: a.activation(out_ap, tmp[:, :], AF.Ln, bias=1.0), reads=[tmp_r], writes=[out_r])

    def headnorm_out(po, po_r, tn, out_d, col0, wcol):
        osb, osb_r = rt("osb", [128, 512], F32, 1)
        op("act", lambda a: a.copy(osb[:, 0:tn], po[:, 0:tn]), reads=[po_r], writes=[osb_r])
        sq, sq_r = rt("sq", [128, 512], BF16)
        op("act", lambda a: a.activation(sq[:, 0:tn], osb[:, 0:tn], AF.Square), reads=[osb_r], writes=[sq_r])
        pt, pr = next_ps()
        op("pe", lambda pe: pe.matmul(pt[:, 0:tn], ones_b[:, :], sq[:, 0:tn], start=True, stop=True),
           reads=[onb_r, sq_r], writes=[pr])
        rs, rs_r = rt("rstd", [128, 512], F32, 1)
        rsqrt_to(rs[:, 0:tn], rs_r, pt[:, 0:tn], [pr], 1.0 / HD, 512)
        op("dve", lambda v: v.scalar_tensor_tensor(osb[:, 0:tn], osb[:, 0:tn], prm[:, wcol:wcol + 1], rs[:, 0:tn],
                                                   ALU.mult, ALU.mult), reads=[osb_r, prm_r, rs_r], writes=[osb_r])
        return osb, osb_r

    out_res = {k: C.res("o_" + k) for k in ("kT", "vT", "msb", "mgd", "gcv", "gcs", "grp", "grs")}

    def store(out_ap, src_ap, src_r, key):
        C.dma("sp", out_ap, src_ap, reads=[src_r], writes=[out_res[key]], part=True, is_out=True, owner=src_r)


    def zt(name, dt):
        t, r = sbt(name, [128, 128], dt)
        op("pool", lambda g: g.memset(t[:, :], 0.0), writes=[r])
        return t, r
    WT0 = [zt("wTm0_%d" % i, BF16) for i in range(2)]
    WT1 = [zt("wTm1_%d" % i, BF16) for i in range(2)]
    QM0 = [zt("qm0_%d" % i, BF16) for i in range(2)]
    QM1 = [zt("qm1_%d" % i, BF16) for i in range(2)]
    KD0 = [zt("kd0_%d" % i, BF16) for i in range(2)]
    KD1 = [zt("kd1_%d" % i, BF16) for i in range(2)]
    gnr = prm[:, P_GNR:P_GNR + 128]

    def mmf(lhsT, rhs, reads, n=128):
        pt, pr = next_ps()
        op("pe", lambda pe: pe.matmul(pt[:, 0:n], lhsT, rhs, start=True, stop=True), reads=reads, writes=[pr])
        return pt, pr

    gstate = {}

    def gdn_wy(b):
        H = gstate[b]
        for nb in range(NBLK):
            while nb - H['scan_done'] > 1:
                PAR.idle()
            cs = slice(nb * 128, (nb + 1) * 128)
            pp_ = nb % 2
            (wTm0, wTm0_r), (wTm1, wTm1_r) = WT0[pp_], WT1[pp_]
            (qm0, qm0_r), (qm1, qm1_r) = QM0[pp_], QM1[pp_]
            (kd0, kd0_r), (kd1, kd1_r) = KD0[pp_], KD1[pp_]
            gcol = gtm[:, nb:nb + 1]
            bcol = btm[:, nb:nb + 1]
            p1, p1_r = next_ps()
            for i, blkc in enumerate((C_UBD, C_BONES, C_SEL0, C_SEL1)):
                op("pe", lambda pe: pe.matmul(p1[:, i:i + 1], cb(blkc), gcol, start=True, stop=True),
                   reads=[cst_r, gtm_r], writes=[p1_r])
            sc, sc_r = rt("sc", [128, 8])
            op("dve", lambda v: v.tensor_copy(sc[:, 0:4], p1[:, 0:4]), reads=[p1_r], writes=[sc_r])
            op("act", lambda a: a.activation(sc[:, 4:5], sc[:, 0:1], AF.Exp), reads=[sc_r], writes=[sc_r])
            op("dve", lambda v: v.tensor_tensor(sc[:, 5:6], sc[:, 1:2], sc[:, 0:1], ALU.subtract), reads=[sc_r], writes=[sc_r])
            op("act", lambda a: a.activation(sc[:, 5:6], sc[:, 5:6], AF.Exp), reads=[sc_r], writes=[sc_r])
            op("dve", lambda v: v.tensor_tensor(sc[:, 6:7], sc[:, 4:5], bcol, ALU.mult), reads=[sc_r, btm_r], writes=[sc_r])
            op("act", lambda a: a.activation(sc[:, 2:4], sc[:, 2:4], AF.Exp), reads=[sc_r], writes=[sc_r])
            Dg, Dg_r = rt("Dg", [128, 128])
            op("dve", lambda v: v.tensor_scalar(Dg[:, :], ident_f, sc[:, 0:1], None, ALU.mult), reads=[cst_r, sc_r], writes=[Dg_r])
            pG, pG_r = mmf(ones_f[:, :], Dg[:, :], [onf_r, Dg_r])
            E1, E1_r = rt("E1", [128, 128])
            op("dve", lambda v: v.scalar_tensor_tensor(E1[:, :], pG[:, 0:128], sc[:, 0:1], cb(C_MPOS), ALU.subtract, ALU.add),
               reads=[pG_r, sc_r, cst_r], writes=[E1_r])
            op("act", lambda a: a.activation(E1[:, :], E1[:, :], AF.Exp, scale=-1.0), reads=[E1_r], writes=[E1_r])
            E2, E2_r = rt("E2", [128, 128])
            op("dve", lambda v: v.scalar_tensor_tensor(E2[:, :], pG[:, 0:128], sc[:, 0:1], cb(C_MNEG), ALU.subtract, ALU.add),
               reads=[pG_r, sc_r, cst_r], writes=[E2_r])
            op("act", lambda a: a.activation(E2[:, :], E2[:, :], AF.Exp), reads=[E2_r], writes=[E2_r])
            pK, pK_r = next_ps()
            op("pe", lambda pe: pe.matmul(pK[:, 0:128], GK[:, cs], GK[:, cs], start=True, stop=True), reads=[GK_r], writes=[pK_r])
            op("pe", lambda pe: pe.matmul(pK[:, 128:256], GK[:, cs], GQ[:, cs], start=True, stop=True), reads=[GK_r, GQ_r], writes=[pK_r])
            Nl, Nl_r = rt("Nl", [128, 128], F32, n=3)
            op("dve", lambda v: v.scalar_tensor_tensor(Nl[:, :], pK[:, 0:128], bcol, E1[:, :], ALU.mult, ALU.mult),
               reads=[pK_r, btm_r, E1_r], writes=[Nl_r])
            QKd, QKd_r = rt("QKd", [128, 128], BF16)
            op("dve", lambda v: v.tensor_tensor(QKd[:, :], pK[:, 128:256], E2[:, :], ALU.mult), reads=[pK_r, E2_r], writes=[QKd_r])
            pM, pM_r = mmf(Nl[:, :], ident_f, [Nl_r, cst_r])
            Ml, Ml_r = rt("Ml", [128, 128], F32, n=3)
            op("act", lambda a: a.copy(Ml[:, :], pM[:, 0:128]), reads=[pM_r], writes=[Ml_r])
            W, W_r = rt("W", [128, 128], F32, n=3)
            op("dve", lambda v: v.tensor_tensor(W[:, :], ident_f, pM[:, 0:128], ALU.subtract), reads=[pM_r, cst_r], writes=[W_r])
            for k in range(1, 6):
                pN, pN_r = mmf(Ml[:, :], Nl[:, :], [Ml_r, Nl_r])
                if k < 5:
                    pM2, pM2_r = mmf(Nl[:, :], Ml[:, :], [Ml_r, Nl_r])
                Nl2, Nl2_r = rt("Nl", [128, 128], F32, n=3)
                op("act", lambda a: a.copy(Nl2[:, :], pN[:, 0:128]), reads=[pN_r], writes=[Nl2_r])
                if k < 5:
                    Ml2, Ml2_r = rt("Ml", [128, 128], F32, n=3)
                    op("dve", lambda v: v.tensor_copy(Ml2[:, :], pM2[:, 0:128]), reads=[pM2_r], writes=[Ml2_r])
                pW, pW_r = mmf(Nl2[:, :], W[:, :], [Nl2_r, W_r])
                W2, W2_r = rt("W", [128, 128], F32, n=3)
                op("dve", lambda v: v.tensor_tensor(W2[:, :], W[:, :], pW[:, 0:128], ALU.add), reads=[W_r, pW_r], writes=[W2_r])
                Nl, Nl_r, W, W_r = Nl2, Nl2_r, W2, W2_r
                if k < 5:
                    Ml, Ml_r = Ml2, Ml2_r
            tpk, tpk_r = next_psb()
            op("pe", lambda pe: pe.transpose(tpk[:, 0:128], GK[:, cs], ident_b[:, :]), reads=[GK_r, idb_r], writes=[tpk_r])
            kbg, kbg_r = rt("kbg", [128, 128])
            op("dve", lambda v: v.tensor_scalar(kbg[:, :], tpk[:, 0:128], sc[:, 6:7], None, ALU.mult), reads=[tpk_r, sc_r], writes=[kbg_r])
            op("act", lambda a: a.activation(kd0[0:64, :], tpk[0:64, 0:128], AF.Copy, scale=sc[0:64, 5:6]), reads=[tpk_r, sc_r, kd0_r], writes=[kd0_r])
            op("act", lambda a: a.activation(kd1[64:128, :], tpk[64:128, 0:128], AF.Copy, scale=sc[64:128, 5:6]), reads=[tpk_r, sc_r, kd1_r], writes=[kd1_r])
            tpv, tpv_r = next_psb()
            op("pe", lambda pe: pe.transpose(tpv[:, 0:128], GV[:, cs], ident_b[:, :]), reads=[GV_r, idb_r], writes=[tpv_r])
            vb_, vb_r = rt("vbt", [128, 128])
            op("dve", lambda v: v.tensor_scalar(vb_[:, :], tpv[:, 0:128], bcol, None, ALU.mult), reads=[tpv_r, btm_r], writes=[vb_r])
            pu, pu_r = mmf(W[:, :], vb_[:, :], [W_r, vb_r])
            u, u_r = rt("u", [128, 128])
            op("act", lambda a: a.copy(u[:, :], pu[:, 0:128]), reads=[pu_r], writes=[u_r])
            pw, pw_r = mmf(kbg[:, :], W[:, :], [kbg_r, W_r])
            op("act", lambda a: a.mul(wTm0[:, 0:64], pw[:, 0:64], -1.0), reads=[pw_r, wTm0_r], writes=[wTm0_r])
            op("dve", lambda v: v.tensor_scalar(wTm1[:, 64:128], pw[:, 64:128], -1.0, None, ALU.mult), reads=[pw_r, wTm1_r], writes=[wTm1_r])
            op("pool", lambda g: g.tensor_copy(qm0[:, 0:64], GQ[:, nb * 128:nb * 128 + 64]), reads=[GQ_r, qm0_r], writes=[qm0_r])
            op("pool", lambda g: g.tensor_copy(qm1[:, 64:128], GQ[:, nb * 128 + 64:nb * 128 + 128]), reads=[GQ_r, qm1_r], writes=[qm1_r])
            H['blk'][nb] = dict(u=(u, u_r), QKd=(QKd, QKd_r), sc=(sc, sc_r), par=nb % 2)
            H['wy_done'] = nb + 1

    def gdn_scan(b):
        H = gstate[b]
        S, S_r = rt("S", [128, 128], F32, n=3)
        Sb, Sb_r = rt("Sb", [128, 128], BF16, n=3)
        op("pool", lambda g: g.memset(S[:, :], 0.0), reads=[S_r], writes=[S_r])
        op("pool", lambda g: g.memset(Sb[:, :], 0.0), reads=[Sb_r], writes=[Sb_r])
        mg, mg_r = None, None
        for nb in range(NBLK):
            while H['wy_done'] <= nb:
                PAR.idle()
            B_ = H['blk'].pop(nb)
            (u, u_r), (QKd, QKd_r), (sc, sc_r) = B_['u'], B_['QKd'], B_['sc']
            pp_ = B_['par']
            (wTm0, wTm0_r), (wTm1, wTm1_r) = WT0[pp_], WT1[pp_]
            (qm0, qm0_r), (qm1, qm1_r) = QM0[pp_], QM1[pp_]
            (kd0, kd0_r), (kd1, kd1_r) = KD0[pp_], KD1[pp_]
            cs = slice(nb * 128, (nb + 1) * 128)
            pa, pa_r = mmf(wTm0[:, :], Sb[:, :], [wTm0_r, Sb_r])
            vn, vn_r = rt("vn", [128, 128])
            op("dve", lambda v: v.tensor_tensor(vn[:, :], u[:, :], pa[:, 0:128], ALU.add), reads=[u_r, pa_r], writes=[vn_r])
            vnb, vnb_r = rt("vnb", [128, 128], BF16)
            op("act", lambda a: a.copy(vnb[:, :], vn[:, :]), reads=[vn_r], writes=[vnb_r])
            pq, pq_r = mmf(qm0[:, :], Sb[:, :], [qm0_r, Sb_r])
            o1, o1_r = rt("o1", [128, 128])
            op("act", lambda a: a.activation(o1[:, :], pq[:, 0:128], AF.Copy, scale=sc[:, 4:5]), reads=[pq_r, sc_r], writes=[o1_r])
            pS1, pS1_r = mmf(kd0[:, :], vnb[:, :], [kd0_r, vnb_r])
            S1, S1_r = rt("S", [128, 128], F32, n=3)
            op("dve", lambda v: v.scalar_tensor_tensor(S1[:, :], S[:, :], sc[:, 2:3], pS1[:, 0:128], ALU.mult, ALU.add),
               reads=[S_r, sc_r, pS1_r], writes=[S1_r])
            S1b, S1b_r = rt("Sb", [128, 128], BF16, n=3)
            op("act", lambda a: a.copy(S1b[:, :], S1[:, :]), reads=[S1_r], writes=[S1b_r])
            pb, pb_r = mmf(wTm1[:, :], S1b[:, :], [wTm1_r, S1b_r])
            vn2, vn2_r = rt("vn", [128, 128])
            op("dve", lambda v: v.tensor_tensor(vn2[:, :], vn[:, :], pb[:, 0:128], ALU.add), reads=[vn_r, pb_r], writes=[vn2_r])
            vn2b, vn2b_r = rt("vnb", [128, 128], BF16)
            op("act", lambda a: a.copy(vn2b[:, :], vn2[:, :]), reads=[vn2_r], writes=[vn2b_r])
            pq2, pq2_r = mmf(qm1[:, :], S1b[:, :], [qm1_r, S1b_r])
            op("act", lambda a: a.activation(o1[64:128, :], pq2[64:128, 0:128], AF.Copy, scale=sc[64:128, 4:5]), reads=[pq2_r, sc_r, o1_r], writes=[o1_r])
            pS2, pS2_r = mmf(kd1[:, :], vn2b[:, :], [kd1_r, vn2b_r])
            S2, S2_r = rt("S", [128, 128], F32, n=3)
            op("dve", lambda v: v.scalar_tensor_tensor(S2[:, :], S1[:, :], sc[:, 3:4], pS2[:, 0:128], ALU.mult, ALU.add),
               reads=[S1_r, sc_r, pS2_r], writes=[S2_r])
            S2b, S2b_r = rt("Sb", [128, 128], BF16, n=3)
            op("act", lambda a: a.copy(S2b[:, :], S2[:, :]), reads=[S2_r], writes=[S2b_r])
            pqk, pqk_r = mmf(QKd[:, :], vn2b[:, :], [QKd_r, vn2b_r])
            op("dve", lambda v: v.tensor_tensor(o1[:, :], o1[:, :], pqk[:, 0:128], ALU.add), reads=[o1_r, pqk_r], writes=[o1_r])
            S, S_r, Sb, Sb_r = S2, S2_r, S2b, S2b_r
            jk, jk_r = rt("jk", [128, 128])
            ssc, ssc_r = rt("ssc", [128, 1])
            op("act", lambda a: a.activation(jk[:, :], o1[:, :], AF.Square, accum_out=ssc[:, :]), reads=[o1_r], writes=[jk_r, ssc_r])
            rs1, rs1_r = rt("rs1", [128, 1])
            rsqrt_to(rs1[:, :], rs1_r, ssc[:, :], [ssc_r], 1.0 / HD, 1)
            on, on_r = rt("on", [128, 128], F32)
            op("dve", lambda v: v.scalar_tensor_tensor(on[:, :], o1[:, :], rs1[:, 0:1], gnr, ALU.mult, ALU.mult),
               reads=[o1_r, rs1_r, prm_r], writes=[on_r])
            tpo, tpo_r = mmf(on[:, :], ident_f, [on_r, cst_r])
            k4 = nb % 4
            if k4 == 0:
                mg, mg_r = rt("mg", [128, 512], F32, 1)
            op("dve", lambda v: v.tensor_tensor(mg[:, k4 * 128:(k4 + 1) * 128], tpo[:, 0:128], ZS[:, cs], ALU.mult),
               reads=[tpo_r, ZS_r, mg_r], writes=[mg_r])
            if k4 == 3:
                store(mgd_o[:, b * T + (nb - 3) * 128:b * T + (nb + 1) * 128], mg[:, :], mg_r, "mgd")
            H['scan_done'] = nb + 1
        store(grp_o[b, :, :], S[:, :], S_r, "grp")


    def sample_part():
        col0 = NB * T
        C.free(QT_r, KT_r, VTM_r, GQ_r, GK_r, GV_r, ZS_r)
        Kg = BIG[:, 0:128 * HD]
        Vg = BIG[:, 128 * HD:2 * 128 * HD]
        Kg_r, Vg_r = C.res("Kg"), C.res("Vg")
        idx, idx_r = sbt("idx", [128, NS], I32)
        C.dma("sp", idx[:, :], ptab_d[:, :], writes=[idx_r])
        gh, gh_r = sbt("gh", [128, 3, 3, NS])
        C.dma("sp", gh[:, :, :, :], ghist_d[:, :, :, :], writes=[gh_r])
        Sall, Sall_r = sbt("Sall", [128, NS, 128])
        for s_ in range(NS):
            C.dma("sp", Sall[:, s_, :], grec_d[s_, :, :], writes=[Sall_r], part=True)
        sm = {}
        t8 = lambda name: sbt(name, [128, NS])

        def dst(cc, pt, pr):
            if cc == 0:
                t_, r_ = t8("s_q")
                op("act", lambda a: a.copy(t_[:, :], pt[:, 0:NS]), reads=[pr], writes=[r_])
                sm["q"] = (t_, r_)
            elif cc in (1, 2):
                t_, r_ = t8("s_kv%d" % cc)
                op("act", lambda a: a.copy(t_[:, :], pt[:, 0:NS]), reads=[pr], writes=[r_])
                store((kT_o if cc == 1 else vT_o)[:, col0:col0 + NS], t_[:, :], r_, "kT" if cc == 1 else "vT")
            elif cc in (3, 4, 5):
                q = cc - 3
                cn, cn_r = t8("s_cn%d" % q)
                op("act", lambda a: a.copy(cn[:, :], pt[:, 0:NS]), reads=[pr], writes=[cn_r])
                store(gcs_o[:, q, 2, :], cn[:, :], cn_r, "gcs")
                C.dma("sp", gcs_o[:, q, 0:2, :], gh[:, q, 1:3, :], reads=[gh_r], writes=[out_res["gcs"]], part=True, is_out=True, owner=C.res("ghst%d" % q))
                y, y_r = t8("s_y%d" % q)
                cw = lambda j: prm[:, P_CW + q * 4 + j:P_CW + q * 4 + j + 1]
                op("dve", lambda v: v.tensor_scalar(y[:, :], gh[:, q, 0, :], cw(0), None, ALU.mult), reads=[gh_r, prm_r], writes=[y_r])
                for j in (1, 2):
                    op("dve", lambda v: v.scalar_tensor_tensor(y[:, :], gh[:, q, j, :], cw(j), y[:, :], ALU.mult, ALU.add),
                       reads=[gh_r, prm_r, y_r], writes=[y_r])
                op("dve", lambda v: v.scalar_tensor_tensor(y[:, :], cn[:, :], cw(3), y[:, :], ALU.mult, ALU.add),
                   reads=[cn_r, prm_r, y_r], writes=[y_r])
                op("act", lambda a: a.activation(y[:, :], y[:, :], AF.Silu), reads=[y_r], writes=[y_r])
                if q == 2:
                    sm["gv"] = (y, y_r)
                else:
                    o_, r_ = t8("s_g%d" % q)
                    l2n_to(o_[:, :], r_, y, y_r, NS, HD ** -0.5 if q == 0 else 1.0)
                    sm["gq" if q == 0 else "gk"] = (o_, r_)
            else:
                t_, r_ = t8("s_zs")
                op("act", lambda a: a.activation(t_[:, :], pt[:, 0:NS], AF.Silu), reads=[pr], writes=[r_])
                sm["zs"] = (t_, r_)

        proj_tile(col0, NS, dst)
        pA, pA_r = next_ps()
        for kc in range(KC):
            op("pe", lambda pe: pe.matmul(pA[0:1, 0:NS], w1[:, kc, 896:897], xn[:, kc, 0:NS], start=(kc == 0), stop=(kc == KC - 1)),
               reads=[w1_r, xn_rs[kc]], writes=[pA_r], inc=(kc == KC - 1))
        row, row_r = sbt("s_row", [1, 2, NS])
        tmp1, tmp1_r = sbt("s_tmp1", [1, NS])
        op("act", lambda a: a.activation(tmp1[:, :], pA[0:1, 0:NS], AF.Exp, bias=prm[0:1, P_DTB:P_DTB + 1]), reads=[pA_r, prm_r], writes=[tmp1_r])
        op("act", lambda a: a.activation(tmp1[:, :], tmp1[:, :], AF.Ln, bias=1.0), reads=[tmp1_r], writes=[tmp1_r])
        op("dve", lambda v: v.tensor_scalar(row[:, 0, :], tmp1[:, :], nexpA[0:1, 0:1], None, ALU.mult), reads=[tmp1_r, nea_r], writes=[row_r])
        pB, pB_r = next_ps()
        for kc in range(KC):
            op("pe", lambda pe: pe.matmul(pB[0:1, 0:NS], w1[:, kc, 897:898], xn[:, kc, 0:NS], start=(kc == 0), stop=(kc == KC - 1)),
               reads=[w1_r, xn_rs[kc]], writes=[pB_r], inc=(kc == KC - 1))
        op("act", lambda a: a.activation(tmp1[:, :], pB[0:1, 0:NS], AF.Exp, scale=-1.0), reads=[pB_r, tmp1_r], writes=[tmp1_r])
        op("dve", lambda v: v.tensor_scalar(tmp1[:, :], tmp1[:, :], 1.0, None, ALU.add), reads=[tmp1_r], writes=[tmp1_r])
        op("dve", lambda v: v.reciprocal(row[:, 1, :], tmp1[:, :]), reads=[tmp1_r, row_r], writes=[row_r])
        pbc, pbc_r = mmf(ones_f[0:1, :], row[0:1, :, :].rearrange("p a n -> p (a n)"), [onf_r, row_r], n=2 * NS)
        bc, bc_r = sbt("s_bc", [128, 3, NS])
        op("dve", lambda v: v.tensor_copy(bc[:, 0:2, :].rearrange("p a n -> p (a n)"), pbc[:, 0:2 * NS]), reads=[pbc_r], writes=[bc_r])
        op("act", lambda a: a.activation(bc[:, 2, :], bc[:, 0, :], AF.Exp), reads=[bc_r], writes=[bc_r])
        egb, beb = bc[:, 2, :], bc[:, 1, :]

        pos, pos_r = next_acc()
        q_, q_r = sm["q"]
        for s_ in range(NS):
            C.indirect(Kg, poolk_d[:, :], bass.IndirectOffsetOnAxis(ap=idx[:, s_:s_ + 1], axis=0), reads=[idx_r], writes=[Kg_r])
            C.indirect(Vg, poolv_d[:, :], bass.IndirectOffsetOnAxis(ap=idx[:, s_:s_ + 1], axis=0), reads=[idx_r], writes=[Vg_r])
            Qrep, Qrep_r = rt("Qrep", [128, 128])
            op("dve", lambda v: v.tensor_scalar(Qrep[:, :], ones_f[:, :], q_[:, s_:s_ + 1], None, ALU.mult), reads=[onf_r, q_r], writes=[Qrep_r])
            pqb, pqb_r = mmf(Qrep[:, :], ident_f, [Qrep_r, cst_r])
            qbc, qbc_r = rt("qbc", [128, 128], BF16)
            op("act", lambda a: a.copy(qbc[:, :], pqb[:, 0:128]), reads=[pqb_r], writes=[qbc_r])
            z, z_r = rt("zs_", [128, 128])
            for r0 in range(0, 128, 8):
                prod, prod_r = rt("prod", [128, 8, 128], F32, 1)
                op("dve", lambda v: v.tensor_tensor(prod[:, :, :], Kg[:, r0 * 128:(r0 + 8) * 128].rearrange("p (r d) -> p r d", d=128),
                                                    qbc[:, :].unsqueeze(1).to_broadcast([128, 8, 128]), ALU.mult),
                   reads=[Kg_r, qbc_r], writes=[prod_r])
                op("dve", lambda v: v.reduce_sum(z[:, r0:r0 + 8], prod[:, :, :], AX.X), reads=[prod_r, z_r], writes=[z_r])
            pzT, pzT_r = mmf(z[:, :], ident_f, [z_r, cst_r])
            e, e_r = rt("se", [128, 128])
            op("act", lambda a: a.activation(e[:, :], pzT[:, 0:128], AF.Exp, bias=prm[:, P_CB:P_CB + 1], scale=SB_SCALE),
               reads=[pzT_r, prm_r], writes=[e_r])
            L, L_r = rt("sL", [128, 128])
            op("act", lambda a: a.activation(L[:, :], e[:, :], AF.Ln, bias=1.0), reads=[e_r], writes=[L_r])
            pS, pS_r = next_ps()
            op("pe", lambda pe: pe.matmul(pS[:, 0:128], cb(C_TRIL), L[:, :], start=True, stop=False), reads=[cst_r, L_r], writes=[pS_r], inc=False)
            pT, pT_r = mmf(L[:, :], ones_f[:, :], [L_r, onf_r])
            Tbt, Tbt_r = rt("Tbt", [128, 128])
            op("act", lambda a: a.copy(Tbt[:, :], pT[:, 0:128]), reads=[pT_r], writes=[Tbt_r])
            op("pe", lambda pe: pe.matmul(pS[:, 0:128], Tbt[:, :], cb(C_SU), start=False, stop=True), reads=[Tbt_r, cst_r], writes=[pS_r])
            w_, w_r = rt("sw", [128, 128])
            op("act", lambda a: a.activation(w_[:, :], pS[:, 0:128], AF.Exp, scale=-1.0), reads=[pS_r], writes=[w_r])
            op("dve", lambda v: v.tensor_tensor(w_[:, :], w_[:, :], e[:, :], ALU.mult), reads=[w_r, e_r], writes=[w_r])
            paT, paT_r = mmf(w_[:, :], ident_f, [w_r, cst_r])
            aT, aT_r = rt("aT", [128, 128], BF16)
            op("act", lambda a: a.copy(aT[:, :], paT[:, 0:128]), reads=[paT_r], writes=[aT_r])
            for r in range(128):
                op("pe", lambda pe: pe.matmul(pos[:, s_:s_ + 1], Vg[:, r * 128:(r + 1) * 128], aT[:, r:r + 1], start=(r == 0), stop=(r == 127)),
                   reads=[Vg_r, aT_r], writes=[pos_r], inc=(r == 127))
        osb, osb_r = headnorm_out(pos, pos_r, NS, msb_o, col0, P_SBN)
        store(msb_o[:, col0:col0 + NS], osb[:, 0:NS], osb_r, "msb")

        gq_, gq_r = sm["gq"]
        gk_, gk_r = sm["gk"]
        gv_, gv_r = sm["gv"]
        zs_, zs_r = sm["zs"]
        pks, pks_r = next_ps()
        for s_ in range(NS):
            op("pe", lambda pe: pe.matmul(pks[:, s_:s_ + 1], Sall[:, s_, :], gk_[:, s_:s_ + 1], start=True, stop=True), reads=[Sall_r, gk_r], writes=[pks_r])
            op("pe", lambda pe: pe.matmul(pks[:, NS + s_:NS + s_ + 1], Sall[:, s_, :], gq_[:, s_:s_ + 1], start=True, stop=True), reads=[Sall_r, gq_r], writes=[pks_r])
        vn, vn_r = t8("s_vn")
        op("dve", lambda v: v.tensor_tensor(vn[:, :], pks[:, 0:NS], egb, ALU.mult), reads=[pks_r, bc_r], writes=[vn_r])
        op("dve", lambda v: v.tensor_tensor(vn[:, :], gv_[:, :], vn[:, :], ALU.subtract), reads=[gv_r, vn_r], writes=[vn_r])
        op("dve", lambda v: v.tensor_tensor(vn[:, :], vn[:, :], beb, ALU.mult), reads=[vn_r, bc_r], writes=[vn_r])
        so, so_r = t8("s_o")
        op("dve", lambda v: v.tensor_tensor(so[:, :], pks[:, NS:2 * NS], egb, ALU.mult), reads=[pks_r, bc_r], writes=[so_r])
        pr_, pr_r = t8("s_pr")
        op("dve", lambda v: v.tensor_tensor(pr_[:, :], gq_[:, :], gk_[:, :], ALU.mult), reads=[gq_r, gk_r], writes=[pr_r])
        pqk, pqk_r = mmf(ones_f[:, :], pr_[:, :], [onf_r, pr_r], n=NS)
        op("dve", lambda v: v.tensor_tensor(pr_[:, :], pqk[:, 0:NS], vn[:, :], ALU.mult), reads=[pqk_r, vn_r, pr_r], writes=[pr_r])
        op("dve", lambda v: v.tensor_tensor(so[:, :], so[:, :], pr_[:, :], ALU.add), reads=[so_r, pr_r], writes=[so_r])
        og, og_r = headnorm_out(so, so_r, NS, mgd_o, col0, P_GNC)
        op("dve", lambda v: v.tensor_tensor(og[:, 0:NS], og[:, 0:NS], zs_[:, :], ALU.mult), reads=[og_r, zs_r], writes=[og_r])
        store(mgd_o[:, col0:col0 + NS], og[:, 0:NS], og_r, "mgd")
        pkr, pkr_r = next_ps()
        op("pe", lambda pe: pe.matmul(pkr[0:NS, 0:128], gk_[:, :], ident_f, start=True, stop=True), reads=[gk_r, cst_r], writes=[pkr_r])
        op("pe", lambda pe: pe.matmul(pkr[0:NS, 128:256], vn[:, :], ident_f, start=True, stop=True), reads=[vn_r, cst_r], writes=[pkr_r])
        kvr, kvr_r = sbt("s_kvr", [NS, 256])
        op("act", lambda a: a.copy(kvr[:, :], pkr[0:NS, 0:256]), reads=[pkr_r], writes=[kvr_r])
        for s_ in range(NS):
            vm, vm_r = rt("s_vm", [NS, 128])
            op("dve", lambda v: v.tensor_scalar(vm[:, :], kvr[:, 128:256], cst[0:NS, C_ID * 128 + s_:C_ID * 128 + s_ + 1], None, ALU.mult),
               reads=[kvr_r, cst_r], writes=[vm_r])
            pO, pO_r = mmf(kvr[:, 0:128], vm[:, :], [kvr_r, vm_r])
            Sn, Sn_r = rt("s_Sn", [128, 128])
            op("dve", lambda v: v.scalar_tensor_tensor(Sn[:, :], Sall[:, s_, :], bc[:, 2, s_:s_ + 1], pO[:, 0:128], ALU.mult, ALU.add),
               reads=[Sall_r, bc_r, pO_r], writes=[Sn_r])
            store(grs_o[s_, :, :], Sn[:, :], Sn_r, "grs")

    def attn_batch(b):
        for G in (range(NTILE) if 'attn' in stages else []):
            racc, racc_r = rt("racc", [128, 512], F32, 1)
            op("pool", lambda g: g.memset(racc[:, :], 0.0), reads=[racc_r], writes=[racc_r])
            po, po_r = next_acc()
            op("pe", lambda pe: pe.matmul(po[:, 0:512], zeros_b[:, 0:128], zeros_b[:, 0:512], start=True, stop=False),
               reads=[zb_r], writes=[po_r], inc=False)
            for jb in range(4 * G + 3, -1, -1):
                c0 = max(0, jb - 4 * G) * 128
                wd = 512 - c0
                pz, pz_r = next_ps()
                op("pe", lambda pe: pe.matmul(pz[:, c0:512], KT[:, jb * 128:(jb + 1) * 128], QT[:, G * 512 + c0:(G + 1) * 512],
                                              start=True, stop=True), reads=[KT_r, QT_r], writes=[pz_r])
                e, e_r = rt("e", [128, 512])
                op("act", lambda a: a.activation(e[:, c0:512], pz[:, c0:512], AF.Exp, bias=prm[:, P_CB:P_CB + 1], scale=SB_SCALE),
                   reads=[pz_r, prm_r], writes=[e_r])
                if jb >= 4 * G:
                    op("dve", lambda g: g.tensor_tensor(e[:, c0:c0 + 128], e[:, c0:c0 + 128], cb(C_MST), ALU.mult),
                       reads=[e_r, cst_r], writes=[e_r])
                L, L_r = rt("L", [128, 512], BF16)
                op("act", lambda a: a.activation(L[:, c0:512], e[:, c0:512], AF.Ln, bias=1.0), reads=[e_r], writes=[L_r])
                pS, pS_r = next_ps()
                op("pe", lambda pe: pe.matmul(pS[:, c0:512], tril_b[:, :], L[:, c0:512], start=True, stop=True),
                   reads=[trb_r, L_r], writes=[pS_r])
                if jb > 0:
                    pR, pR_r = next_ps()
                    op("pe", lambda pe: pe.matmul(pR[:, c0:512], ones_b[:, :], L[:, c0:512], start=True, stop=True),
                       reads=[onb_r, L_r], writes=[pR_r])
                t, t_r = rt("t", [128, 512], F32, 1)
                op("dve", lambda v: v.tensor_tensor(t[:, c0:512], pS[:, c0:512], racc[:, c0:512], ALU.add),
                   reads=[pS_r, racc_r], writes=[t_r])
                op("act", lambda a: a.activation(t[:, c0:512], t[:, c0:512], AF.Exp, scale=-1.0), reads=[t_r], writes=[t_r])
                av, av_r = rt("a", [128, 512], BF16)
                op("dve", lambda g: g.tensor_tensor(av[:, c0:512], e[:, c0:512], t[:, c0:512], ALU.mult),
                   reads=[e_r, t_r], writes=[av_r])
                if jb > 0:
                    op("dve", lambda v: v.tensor_tensor(racc[:, c0:512], racc[:, c0:512], pR[:, c0:512], ALU.add),
                       reads=[pR_r, racc_r], writes=[racc_r])
                op("pe", lambda pe: pe.matmul(po[:, c0:512], VTM[:, jb, :], av[:, c0:512], start=False, stop=(jb == 0)),
                   reads=[VTM_r, av_r], writes=[po_r], inc=True)
            osb, osb_r = headnorm_out(po, po_r, 512, msb_o, 0, P_SBN)
            store(msb_o[:, b * T + G * 512:b * T + (G + 1) * 512], osb[:, :], osb_r, "msb")


    for b in range(NB):
        _sc = nc.named_scope('proj%d' % b); _sc.__enter__()
        for ti in range(NTILE):
            t0 = ti * 512
            gcol = b * T + t0

            def dst(cc, pt, pr, t0=t0, gcol=gcol, ti=ti):
                tn = 512
                if cc == 0:
                    op("act", lambda a: a.copy(QT[:, t0:t0 + tn], pt[:, 0:tn]), reads=[pr], writes=[QT_r])
                elif cc in (1, 2):
                    st, st_r = rt("kvst", [128, 512], F32, 1)
                    op("act", lambda a: a.copy(st[:, :], pt[:, 0:tn]), reads=[pr], writes=[st_r])
                    store((kT_o if cc == 1 else vT_o)[:, gcol:gcol + tn], st[:, :], st_r, "kT" if cc == 1 else "vT")
                    if cc == 1:
                        op("dve", lambda v: v.tensor_copy(KT[:, t0:t0 + tn], pt[:, 0:tn]), reads=[pr], writes=[KT_r])
                    else:
                        vb, vb_r = rt("vbf", [128, 512], BF16, 1)
                        op("dve", lambda v: v.tensor_copy(vb[:, :], pt[:, 0:tn]), reads=[pr], writes=[vb_r])
                        for k in range(4):
                            tp, tp_r = next_psb()
                            op("pe", lambda pe: pe.transpose(tp[:, 0:128], vb[:, k * 128:(k + 1) * 128], ident_b[:, :]),
                               reads=[vb_r, idb_r], writes=[tp_r])
                            op("act", lambda a: a.copy(VTM[:, ti * 4 + k, :], tp[:, 0:128]), reads=[tp_r], writes=[VTM_r])
                elif cc in (3, 4, 5):
                    ci, ci_r = cin[cc - 3]
                    q = cc - 3
                    if ti == 0:
                        op("pool", lambda g: g.memset(ci[:, 0:3], 0.0), reads=[ci_r], writes=[ci_r])
                    else:
                        op("pool", lambda g: g.tensor_copy(ci[:, 0:3], ci[:, 512:515]), reads=[ci_r], writes=[ci_r])
                    op("act", lambda a: a.copy(ci[:, 3:515], pt[:, 0:tn]), reads=[pr], writes=[ci_r])
                    if ti == NTILE - 1:
                        op("pool", lambda g: g.tensor_copy(gcv_s[:, q, b, :], ci[:, 512:515]), reads=[ci_r], writes=[gcv_r])
                    y, y_r = rt("cvy", [128, 512], F32, 1)
                    cw = lambda j: prm[:, P_CW + q * 4 + j:P_CW + q * 4 + j + 1]
                    op("dve", lambda v: v.tensor_scalar(y[:, :], ci[:, 0:512], cw(0), None, ALU.mult), reads=[ci_r, prm_r], writes=[y_r])
                    for j in (1, 2, 3):
                        op("dve", lambda v: v.scalar_tensor_tensor(y[:, :], ci[:, j:j + 512], cw(j), y[:, :], ALU.mult, ALU.add),
                           reads=[ci_r, prm_r, y_r], writes=[y_r])
                    op("act", lambda a: a.activation(y[:, :], y[:, :], AF.Silu), reads=[y_r], writes=[y_r])
                    if q == 0:
                        l2n_to(GQ[:, t0:t0 + tn], GQ_r, y, y_r, tn, HD ** -0.5)
                    elif q == 1:
                        l2n_to(GK[:, t0:t0 + tn], GK_r, y, y_r, tn, 1.0)
                    else:
                        op("dve", lambda v: v.tensor_copy(GV[:, t0:t0 + tn], y[:, :]), reads=[y_r], writes=[GV_r])
                else:
                    op("act", lambda a: a.activation(ZS[:, t0:t0 + tn], pt[:, 0:tn], AF.Silu), reads=[pr], writes=[ZS_r])

            proj_tile(gcol, 512, dst)
            for k in range(4):
                blk = ti * 4 + k
                pt, pr = next_ps()
                for kc in range(KC):
                    op("pe", lambda pe: pe.matmul(pt[:, 0:2], xn[:, kc, k * 128:(k + 1) * 128], w1[:, kc, 896:898],
                                                  start=(kc == 0), stop=(kc == KC - 1)),
                       reads=[w1_r, xn_rs[kc]], writes=[pr], inc=(kc == KC - 1))
                sp1, sp1_r = rt("sp1", [128, 1])
                softplus_to(sp1[:, :], sp1_r, pt[:, 0:1], [pr], prm[:, P_DTB:P_DTB + 1], 1.0, [128, 1])
                op("dve", lambda v: v.tensor_scalar(gtm[:, blk:blk + 1], sp1[:, :], nexpA[:, 0:1], None, ALU.mult),
                   reads=[sp1_r, nea_r, gtm_r], writes=[gtm_r])
                eb, eb_r = rt("eb", [128, 1])
                op("act", lambda a: a.activation(eb[:, :], pt[:, 1:2], AF.Exp, scale=-1.0), reads=[pr], writes=[eb_r])
                op("dve", lambda v: v.tensor_scalar(eb[:, :], eb[:, :], 1.0, None, ALU.add), reads=[eb_r], writes=[eb_r])
                op("dve", lambda v: v.reciprocal(btm[:, blk:blk + 1], eb[:, :]), reads=[eb_r, btm_r], writes=[btm_r])

        _sc.__exit__(None, None, None); _sc = nc.named_scope('mix%d' % b); _sc.__enter__()
        fns = []
        if 'attn' in stages:
            fns.append(lambda: attn_batch(b))
        if 'gdn' in stages:
            gstate[b] = dict(wy_done=0, scan_done=0, blk={})
            fns.append(lambda: gdn_wy(b))
            fns.append(lambda: gdn_scan(b))
        if not fns:
            pass
        elif 'attn' not in stages:
            PAR.run([lambda: None] + fns)
        else:
            PAR.run(fns)
        _sc.__exit__(None, None, None)
    store(gcv_o[:, :, :, :], gcv_s[:, :, :, :], gcv_r, "gcv")
    if do_sample:
        with nc.named_scope('sample'):
            sample_part()
    C.finish()
    S0.close()
    return nc

_O_SB_Q, _O_SB_K, _O_SB_V = 0, 1024, 2048
_O_GDN_QKV = 3072
_O_GDN_Z = _O_GDN_QKV + 3072
_O_GDN_A = _O_GDN_Z + 1024
_O_GDN_B = _O_GDN_A + 8
_NC_CACHE = {}


def _fm(a, kc):
    return np.ascontiguousarray(a.T.reshape(kc, 128, a.shape[0]).transpose(1, 0, 2))


def kernel(x_prompt, x_sample, cache_sb_k, cache_sb_v, page_table, state_gdn_conv,
           state_gdn_rec, state_ffn_conv, p_prompt, p_sample, attn_norm, w_in,
           sb_logit_bias, sb_out_norm, gdn_conv_w, gdn_a_log, gdn_dt_bias, gdn_out_norm,
           w_out, ffn_norm, w_ffn_gate, w_ffn_up, ffn_conv_w, w_ffn_down, ple_norm,
           w_ple_gate, w_ple_proj, final_norm):
    f32 = np.float32
    A = lambda v: np.asarray(v)
    x_prompt, x_sample = A(x_prompt).astype(f32), A(x_sample).astype(f32)
    B, T, _ = x_prompt.shape
    NS = x_sample.shape[0]
    NP = B * T
    w_in0 = A(w_in)[0]
    ck, cv = A(cache_sb_k)[0], A(cache_sb_v)[0]
    NPOOL = ck.shape[0]
    sgc, sgr, sfc = A(state_gdn_conv)[0], A(state_gdn_rec)[0], A(state_ffn_conv)[0]
    xall = np.concatenate([x_prompt.reshape(NP, D), x_sample.reshape(NS, D)], 0)
    xT = _fm(xall, KC)
    cst = make_consts()
    ptabT = np.ascontiguousarray(A(page_table).T.astype(np.int32))
    an = A(attn_norm)[0]
    gcw = A(gdn_conv_w)[0]
    in_maps = []
    for c in range(8):
        hc = slice(c * 128, (c + 1) * 128)
        cols = [w_in0[:, _O_SB_Q:][:, hc], w_in0[:, _O_SB_K:][:, hc], w_in0[:, _O_SB_V:][:, hc],
                w_in0[:, _O_GDN_QKV:][:, hc], w_in0[:, _O_GDN_QKV + 1024:][:, hc], w_in0[:, _O_GDN_QKV + 2048:][:, hc],
                w_in0[:, _O_GDN_Z:][:, hc], w_in0[:, _O_GDN_A + c:_O_GDN_A + c + 1], w_in0[:, _O_GDN_B + c:_O_GDN_B + c + 1]]
        w1 = np.ascontiguousarray(np.concatenate(cols, 1)).astype(f32)
        prm = np.zeros((128, NPRM), f32)
        prm[:, P_CB] = A(sb_logit_bias)[0, c]
        prm[:, P_SBN] = A(sb_out_norm)[0]
        prm[:, P_ALOG] = A(gdn_a_log)[0, c]
        prm[:, P_DTB] = A(gdn_dt_bias)[0, c]
        for q in range(3):
            for j in range(4):
                prm[:, P_CW + q * 4 + j] = gcw[j, q * 1024 + c * 128:q * 1024 + (c + 1) * 128]
        prm[:, P_GNC] = A(gdn_out_norm)[0]
        prm[:, P_GNR:P_GNR + 128] = A(gdn_out_norm)[0][None, :]
        prm[:, P_AN:P_AN + KC] = an.reshape(KC, 128).T
        gh = np.stack([sgc[:, :, q * 1024 + c * 128:q * 1024 + (c + 1) * 128] for q in range(3)], 0)
        in_maps.append(dict(
            xT=xT, w1=w1, prm=prm, cst=cst,
            poolk=np.ascontiguousarray(ck[:, :, c, :].reshape(NPOOL, -1)),
            poolv=np.ascontiguousarray(cv[:, :, c, :].reshape(NPOOL, -1)),
            ptab=ptabT, ghist=np.ascontiguousarray(gh.transpose(3, 0, 2, 1)),
            grec=np.ascontiguousarray(sgr[:, c])))
    key1 = ("l1", T, B, NS, NPOOL)
    if key1 not in _NC_CACHE:
        _NC_CACHE[key1] = build_l1(T=T, NB=B, NS=NS, NPOOL=NPOOL, do_sample=True)
    res1 = run_bass_kernel_spmd(_NC_CACHE[key1], in_maps, core_ids=list(range(8))).results
    del in_maps

    sb_k_p = np.zeros((1, B, T, 8, 128), f32)
    sb_v_p = np.zeros((1, B, T, 8, 128), f32)
    sb_k_s = np.zeros((1, NS, 1, 8, 128), f32)
    sb_v_s = np.zeros((1, NS, 1, 8, 128), f32)
    gconv_p = np.zeros((1, B, 3, 3072), f32)
    gconv_s = np.zeros((1, NS, 3, 3072), f32)
    grec_p = np.zeros((1, B, 8, 128, 128), f32)
    grec_s = np.zeros((1, NS, 8, 128, 128), f32)
    mixT = np.zeros((D, NP + NS), f32)
    for c in range(8):
        r = res1[c]
        kT, vT = np.asarray(r["kT_o"]), np.asarray(r["vT_o"])
        sb_k_p[0, :, :, c, :] = kT[:, :NP].T.reshape(B, T, 128)
        sb_v_p[0, :, :, c, :] = vT[:, :NP].T.reshape(B, T, 128)
        sb_k_s[0, :, 0, c, :] = kT[:, NP:].T
        sb_v_s[0, :, 0, c, :] = vT[:, NP:].T
        gcvo = np.asarray(r["gcv_o"])
        gcso = np.asarray(r["gcs_o"])
        for q in range(3):
            gconv_p[0, :, :, q * 1024 + c * 128:q * 1024 + (c + 1) * 128] = gcvo[:, q].transpose(1, 2, 0)
            gconv_s[0, :, :, q * 1024 + c * 128:q * 1024 + (c + 1) * 128] = gcso[:, q].transpose(2, 1, 0)
        grec_p[0, :, c] = np.asarray(r["grp_o"])
        grec_s[0, :, c] = np.asarray(r["grs_o"])
        mixT[c * 128:(c + 1) * 128] = np.asarray(r["msb_o"])
        mixT[1024 + c * 128:1024 + (c + 1) * 128] = np.asarray(r["mgd_o"])

    NM = (B * T) // 8
    per_b = T // NM
    pall = np.concatenate([A(p_prompt)[0].reshape(NP, -1), A(p_sample)[0].reshape(NS, -1)], 0).astype(f32)
    xallT = np.ascontiguousarray(xall.T)
    pallT = np.ascontiguousarray(pall.T)
    nrm3 = np.ascontiguousarray(np.stack([A(ffn_norm)[0], A(ple_norm)[0], A(final_norm)]).reshape(3, KC, 128).transpose(2, 0, 1)).astype(f32)
    fcw = _fm(A(ffn_conv_w)[0].astype(f32), NJ)
    wts = l2_weight_layouts(A(w_out)[0], A(w_ffn_gate)[0], A(w_ffn_up)[0], A(w_ffn_down)[0], A(w_ple_gate)[0], A(w_ple_proj)[0])
    in_maps = []
    for j in range(8):
        b, s = j // per_b, j % per_b
        g0 = b * T + s * NM
        cols = np.concatenate([np.arange(g0 - 2, g0 + NM), [NP + j]])
        valid = np.ones(NM + 3, bool)
        if s == 0:
            valid[0:2] = False
            cols[0:2] = g0

        def take(MT, kc):
            t = MT[:, cols].copy()
            t[:, ~valid] = 0.0
            return np.ascontiguousarray(t.reshape(kc, 128, NM + 3).transpose(1, 0, 2))
        m = dict(xT=take(xallT, KC), mixT=take(mixT, KC), pT=take(pallT, 2), fhist=_fm(sfc[j].astype(f32), NJ), nrm=nrm3, fcw=fcw)
        m.update(wts)
        in_maps.append(m)
    key2 = ("l2", NM)
    if key2 not in _NC_CACHE:
        _NC_CACHE[key2] = build_l2(NM)
    res2 = run_bass_kernel_spmd(_NC_CACHE[key2], in_maps, core_ids=list(range(8))).results
    y_p = np.zeros((B, T, D), f32)
    y_s = np.zeros((NS, 1, D), f32)
    fconv_p = np.zeros((1, B, 2, DFF), f32)
    fconv_s = np.zeros((1, NS, 2, DFF), f32)
    for j in range(8):
        b, s = j // per_b, j % per_b
        r = res2[j]
        yk = np.asarray(r["yT"]).transpose(2, 1, 0).reshape(NM + 1, D)
        y_p[b, s * NM:(s + 1) * NM] = yk[:NM]
        y_s[j, 0] = yk[NM]
        fc = np.asarray(r["fcnew"]).transpose(2, 1, 0).reshape(4, DFF)
        if s == per_b - 1:
            fconv_p[0, b] = fc[0:2]
        fconv_s[0, j, 0] = fc[3]
        fconv_s[0, j, 1] = fc[2]
    return (y_p, y_s, sb_k_p, sb_v_p, gconv_p, grec_p, fconv_p, sb_k_s, sb_v_s, gconv_s, grec_s, fconv_s)
```

```python
from concourse.bass_utils import run_bass_kernel_spmd
import contextlib
import numpy as np
import concourse.bass as bass
import concourse.mybir as mybir

F32 = mybir.dt.float32
BF16 = mybir.dt.bfloat16
I32 = mybir.dt.int32
AF = mybir.ActivationFunctionType
ALU = mybir.AluOpType
AX = mybir.AxisListType


class Res:
    __slots__ = ("name", "w", "r", "dsem", "dval", "excl", "dq")

    def __init__(self, name, inherit=None):
        self.name = name
        self.w = {}
        self.r = dict(inherit) if inherit else {}
        self.dsem = None
        self.dval = 0
        self.excl = False
        self.dq = None


class Ctx:
    def __init__(self, nc):
        self.nc = nc
        self.eng = {}
        for name, h in (("pe", nc.tensor), ("act", nc.scalar), ("dve", nc.vector),
                        ("pool", nc.gpsimd), ("sp", nc.sync)):
            sem = nc.alloc_semaphore("sem_" + name)
            self.eng[name] = dict(h=h, sem=sem, cnt=0, seen={})
        self.freed = {}
        self.out_toks = {}
        self.nres = 0
        self.dsem_pool = {}
        self.par = None

    def res(self, name=None):
        self.nres += 1
        return Res(name or f"r{self.nres}", inherit=self.freed)

    def sb(self, stack, name, shape, dt):
        t = stack.enter_context(self.nc.sbuf_tensor("sb_" + name, list(shape), dt))
        return t, self.res(name)

    def ps(self, stack, name, shape, dt):
        t = stack.enter_context(self.nc.psum_tensor("pp_" + name, list(shape), dt))
        r = self.res(name)
        r.excl = True
        return t, r

    def free(self, *ress):
        for r in ress:
            for d in (r.w, r.r):
                for s, v in d.items():
                    if self.freed.get(s, 0) < v:
                        self.freed[s] = v
            if r.dsem is not None:
                self.dsem_pool.setdefault(r.dq, []).append((r.dsem, r.dval))
                r.dsem = None

    def _wait(self, e, toks):
        E = self.eng[e]
        for sem, val in toks.items():
            if E["seen"].get(sem, 0) < val:
                E["h"].wait_ge(sem, val)
                E["seen"][sem] = val

    @staticmethod
    def _merge(dst, src):
        for s, v in src.items():
            if dst.get(s, 0) < v:
                dst[s] = v

    def _deps(self, e, reads, writes, skip_sem=None):
        toks = {}
        own = self.eng[e]["sem"]
        for r in reads:
            self._merge(toks, r.w)
            if r.excl:
                self._merge(toks, {s_: v_ for s_, v_ in r.r.items() if s_ != own})
        for w in writes:
            self._merge(toks, w.w)
            self._merge(toks, w.r)
        if skip_sem is not None:
            toks.pop(skip_sem, None)
        return toks

    def op(self, e, fn, reads=(), writes=(), inc=True):
        E = self.eng[e]
        if self.par is not None:
            inc = True
        toks = self._deps(e, reads, writes, skip_sem=E["sem"] if e == "pe" else None)
        self._wait(e, toks)
        ins = fn(E["h"])
        if inc:
            E["cnt"] += 1
            ins.then_inc(E["sem"], 1)
            val = E["cnt"]
        else:
            val = E["cnt"] + 1
        for r in reads:
            if r.r.get(E["sem"], 0) < val:
                r.r[E["sem"]] = val
        for w in writes:
            w.w = {E["sem"]: val}
            w.r = {}
        if self.par is not None:
            self.par.switch()
        return ins

    def dma(self, q, out, in_, reads=(), writes=(), part=False, is_out=False, owner=None, **kw):
        E = self.eng[q]
        W = owner if owner is not None else writes[0]
        if W.dsem is None:
            W.dq = "pool" if q == "pool" else "hw"
            if self.dsem_pool.get(W.dq):
                W.dsem, W.dval = self.dsem_pool[W.dq].pop()
            else:
                W.dsem = self.nc.alloc_semaphore("d_" + W.name)
                W.dval = 0
        toks = self._deps(q, reads, writes, skip_sem=W.dsem if part else None)
        self._wait(q, toks)
        ins = E["h"].dma_start(out=out, in_=in_, **kw)
        W.dval += 16
        ins.then_inc(W.dsem, 16)
        for r in reads:
            if r.r.get(W.dsem, 0) < W.dval:
                r.r[W.dsem] = W.dval
        for w in writes:
            if part:
                w.w[W.dsem] = W.dval
            else:
                w.w = {W.dsem: W.dval}
                w.r = {}
        if is_out:
            if self.out_toks.get(W.dsem, 0) < W.dval:
                self.out_toks[W.dsem] = W.dval
        return ins

    def indirect(self, out, in_, in_offset, reads=(), writes=(), part=False):
        E = self.eng["pool"]
        W = writes[0]
        if W.dsem is None:
            W.dq = "pool"
            if self.dsem_pool.get("pool"):
                W.dsem, W.dval = self.dsem_pool["pool"].pop()
            else:
                W.dsem = self.nc.alloc_semaphore("d_" + W.name)
                W.dval = 0
        toks = self._deps("pool", reads, writes, skip_sem=W.dsem if part else None)
        self._wait("pool", toks)
        ins = E["h"].indirect_dma_start(out=out, out_offset=None, in_=in_, in_offset=in_offset)
        W.dval += 16
        ins.then_inc(W.dsem, 16)
        for r in reads:
            if r.r.get(W.dsem, 0) < W.dval:
                r.r[W.dsem] = W.dval
        for w in writes:
            if part:
                w.w[W.dsem] = W.dval
            else:
                w.w = {W.dsem: W.dval}
                w.r = {}
        return ins

    def finish(self):
        toks = dict(self.out_toks)
        for name, E in self.eng.items():
            if name != "sp" and E["cnt"] > 0:
                toks[E["sem"]] = E["cnt"]
        self._wait("sp", toks)


class Par:
    def __init__(self, C):
        import threading
        self.C = C
        self.th = threading
        self.cv = threading.Condition()
        self.turn = 0
        self.alive = []
        self.ids = {}
        self.err = None
        self.weights = {}
        self.credit = {}

    def cur(self):
        return self.ids.get(self.th.get_ident(), None)

    def _next(self, i):
        n = len(self.alive)
        for k in range(1, n + 1):
            j = (i + k) % n
            if self.alive[j]:
                return j
        return -1

    def switch(self):
        i = self.cur()
        if i is None:
            return
        w = self.weights.get(i, 1)
        if w > 1:
            self.credit[i] = self.credit.get(i, 0) + 1
            if self.credit[i] % w != 0:
                return
        with self.cv:
            self.turn = self._next(i)
            self.cv.notify_all()
            while self.turn != i and self.err is None:
                self.cv.wait()
            if self.err is not None and self.turn != i:
                raise RuntimeError("peer failed")

    def idle(self):
        self.switch()

    def run(self, fns):
        self.alive = [True] * len(fns)
        self.turn = 0

        def body(i, fn):
            self.ids[self.th.get_ident()] = i
            try:
                with self.cv:
                    while self.turn != i and self.err is None:
                        self.cv.wait()
                if self.err is None:
                    fn()
            except BaseException as e:
                if self.err is None:
                    self.err = e
            finally:
                with self.cv:
                    self.alive[i] = False
                    self.turn = self._next(i)
                    self.cv.notify_all()
        ts = [self.th.Thread(target=body, args=(i, f)) for i, f in enumerate(fns)]
        self.C.par = self
        for t in ts:
            t.start()
        for t in ts:
            t.join()
        self.C.par = None
        if self.err is not None:
            raise self.err

D = 2048
KC = 16
DFF = 5504
NJ = 43
PLE = 256
EPS = 1e-6


def tiles_of(n, step=512):
    return [(s, min(step, n - s)) for s in range(0, n, step)]


def build_l2(NM=1024):
    NT = NM + 3
    TL = tiles_of(NT)
    nc = bass.Bass("TRN2", target_bir_lowering=False)
    dt_in = lambda name, shape, dt=F32: nc.dram_tensor(name, list(shape), dt, kind="ExternalInput")
    dt_out = lambda name, shape, dt=F32: nc.dram_tensor(name, list(shape), dt, kind="ExternalOutput")
    xT_d = dt_in("xT", [128, KC, NT])
    mixT_d = dt_in("mixT", [128, KC, NT])
    pT_d = dt_in("pT", [128, 2, NT])
    fh_d = dt_in("fhist", [128, NJ, 2])
    w_out_d = dt_in("w_out", [4, 128, KC, 512])
    w_gate_d = dt_in("w_gate", [22, 128, KC, 256])
    w_up_d = dt_in("w_up", [22, 128, KC, 256])
    w_down_d = dt_in("w_down", [KC, 128, NJ, 128])
    w_pg_d = dt_in("w_pg", [4, 128, KC, 512])
    w_pp_d = dt_in("w_pp", [128, 2, D])
    nrm_d = dt_in("nrm", [128, 3, KC])
    fcw_d = dt_in("fcw", [128, NJ, 3])
    yT_d = dt_out("yT", [128, KC, NM + 1])
    fc_d = dt_out("fcnew", [128, NJ, 4])
    hs_d = nc.dram_tensor("h_scratch", [128, KC, NT], F32)
    h2s_d = nc.dram_tensor("h2_scratch", [128, KC, NT], F32)

    C = Ctx(nc)
    hs_res = C.res("hs")
    with contextlib.ExitStack() as S0:
        S0.enter_context(nc.allow_low_precision("bf16 matmul operands, fp32 accumulation"))
        ones_bf, ones_r = C.sb(S0, "ones_bf", [128, 128], BF16)
        C.op("pool", lambda g: g.memset(ones_bf[:, :], 1.0), writes=[ones_r])
        nrm, nrm_r = C.sb(S0, "nrm", [128, 3, KC], F32)
        C.dma("sp", nrm[:, :, :], nrm_d[:, :, :], writes=[nrm_r])
        fcw, fcw_r = C.sb(S0, "fcw", [128, NJ, 3], F32)
        C.dma("sp", fcw[:, :, :], fcw_d[:, :, :], writes=[fcw_r])
        fh, fh_r = C.sb(S0, "fh", [128, NJ, 2], F32)
        C.dma("sp", fh[:, :, :], fh_d[:, :, :], writes=[fh_r])
        fcs, fcs_r = C.sb(S0, "fcs", [128, NJ, 4], F32)
        C.op("pool", lambda p: p.tensor_copy(fcs[:, :, 3], fh[:, :, 1]), reads=[fh_r], writes=[fcs_r])
        rstd, rstd_r = C.sb(S0, "rstd", [128, NT], F32)
        PS = [C.ps(S0, f"ps{i}", [128, 512], F32) for i in range(8)]
        psi = [0]

        def next_ps():
            t = PS[psi[0] % 8]
            psi[0] += 1
            return t

        def rms_stats(src, src_rs, sq_bufs):
            pss = [next_ps() for _ in TL]
            for kc in range(KC):
                sq, sq_r = sq_bufs[kc % len(sq_bufs)]
                C.op("act", lambda a: a.activation(sq[:, :], src[:, kc, :], AF.Square),
                     reads=[src_rs[kc]], writes=[sq_r])
                for ti, (t0, tn) in enumerate(TL):
                    pt, pr = pss[ti]
                    C.op("pe", lambda pe: pe.matmul(pt[:, 0:tn], ones_bf[:, :], sq[:, t0:t0 + tn],
                                                    start=(kc == 0), stop=(kc == KC - 1)),
                         reads=[ones_r, sq_r], writes=[pr], inc=(kc == KC - 1) or True)
            for ti, (t0, tn) in enumerate(TL):
                pt, pr = pss[ti]
                C.op("dve", lambda v: v.tensor_scalar(rstd[:, t0:t0 + tn], pt[:, 0:tn], 1.0 / D, EPS,
                                                      ALU.mult, ALU.add), reads=[pr], writes=[rstd_r])
            C.op("act", lambda a: a.activation(rstd[:, :], rstd[:, :], AF.Sqrt), reads=[rstd_r], writes=[rstd_r])
            C.op("dve", lambda v: v.reciprocal(rstd[:, :], rstd[:, :]), reads=[rstd_r], writes=[rstd_r])

        _sc = nc.named_scope('st1'); _sc.__enter__()
        Sf = contextlib.ExitStack()
        fT, _ = C.sb(Sf, "fT", [128, KC, NT], BF16)
        fT_rs = [C.res(f"fT{k}") for k in range(KC)]
        S1 = contextlib.ExitStack()
        hT, _ = C.sb(S1, "hT", [128, KC, NT], F32)
        hT_rs = [C.res(f"hT{k}") for k in range(KC)]
        for kc in range(KC):
            C.dma("sp", hT[:, kc, :], xT_d[:, kc, :], writes=[hT_rs[kc]])
        Sm = contextlib.ExitStack()
        mixT, _ = C.sb(Sm, "mixT", [128, KC, NT], BF16)
        mix_rs = [C.res(f"mix{k}") for k in range(KC)]
        for kc in range(KC):
            C.dma("pool", mixT[:, kc, :], mixT_d[:, kc, :], writes=[mix_rs[kc]])
        wbufs = [C.sb(Sm, f"wo{i}", [128, KC, 512], BF16) for i in range(2)]
        pass
        for g in range(4):
            wt, wr = wbufs[g % 2]
            C.dma("pool", wt[:, :, :], w_out_d[g, :, :, :], writes=[wr])
            for o in range(4):
                oc = g * 4 + o
                for (t0, tn) in TL:
                    pt, pr = next_ps()
                    for kc in range(KC):
                        C.op("pe", lambda pe: pe.matmul(pt[:, 0:tn], wt[:, kc, o * 128:(o + 1) * 128],
                                                        mixT[:, kc, t0:t0 + tn], start=(kc == 0), stop=(kc == KC - 1)),
                             reads=[wr, mix_rs[kc]], writes=[pr], inc=(kc == KC - 1))
                    C.op("dve", lambda v: v.tensor_tensor(hT[:, oc, t0:t0 + tn], hT[:, oc, t0:t0 + tn], pt[:, 0:tn], ALU.add),
                         reads=[pr, hT_rs[oc]], writes=[hT_rs[oc]])
        C.free(*mix_rs, *[r for _, r in wbufs])
        Sm.close()

        _sc.__exit__(None, None, None); _sc = nc.named_scope('st2'); _sc.__enter__()
        Sq = contextlib.ExitStack()
        sq_bufs = [C.sb(Sq, f"sq{i}", [128, NT], BF16) for i in range(2)]
        rms_stats(hT, hT_rs, sq_bufs)
        for kc in range(KC):
            C.op("dve", lambda v: v.scalar_tensor_tensor(fT[:, kc, :], hT[:, kc, :], nrm[:, 0, kc:kc + 1], rstd[:, :],
                                                         ALU.mult, ALU.mult),
                 reads=[hT_rs[kc], nrm_r, rstd_r], writes=[fT_rs[kc]])
            C.dma("sp", hs_d[:, kc, :], hT[:, kc, :], reads=[hT_rs[kc]], writes=[hs_res], part=True)
        C.free(*hT_rs, *[r for _, r in sq_bufs])
        Sq.close()
        S1.close()

        _sc.__exit__(None, None, None); _sc = nc.named_scope('st3'); _sc.__enter__()
        Sa = contextlib.ExitStack()
        actT, _ = C.sb(Sa, "actT", [128, NJ, NT], BF16)
        act_rs = [C.res(f"act{j}") for j in range(NJ)]
        Sg = contextlib.ExitStack()
        GW = 256
        wg_b = [C.sb(Sg, f"wg{i}", [128, KC, GW], BF16) for i in range(2)]
        wu_b = [C.sb(Sg, f"wu{i}", [128, KC, GW], BF16) for i in range(2)]
        gp_b = [C.sb(Sg, f"gp{i}", [128, NT], F32) for i in range(2)]
        cv_b = [C.sb(Sg, f"cv{i}", [128, NT], F32) for i in range(2)]
        sl_b = [C.sb(Sg, f"sl{i}", [128, NT], F32) for i in range(2)]
        ub_b = [C.sb(Sg, f"ub{i}", [128, NT], BF16) for i in range(2)]
        pass
        pass
        ngrp = (DFF + GW - 1) // GW
        pend = [None]
        stg_b = [C.sb(Sg, f"stg{i}", [128, 4, GW], F32) for i in range(4)]
        wg_rs = [[C.res(f"wg{i}_{k}") for k in range(4)] for i in range(2)]
        wu_rs = [[C.res(f"wu{i}_{k}") for k in range(4)] for i in range(2)]
        stgi = [0]

        def load_group(g):
            for (wb, rs, src) in ((wg_b, wg_rs, w_gate_d), (wu_b, wu_rs, w_up_d)):
                wt, _ = wb[g % 2]
                for k in range(4):
                    stg, stg_r = stg_b[stgi[0] % 4]
                    C.dma("sp", stg[:, :, :], src[g, :, 4 * k:4 * k + 4, :], writes=[stg_r])
                    if stgi[0] % 2 == 0:
                        C.op("act", lambda a: a.copy(wt[:, 4 * k:4 * k + 4, :], stg[:, :, :]), reads=[stg_r], writes=[rs[g % 2][k]])
                    else:
                        C.op("dve", lambda v: v.tensor_copy(wt[:, 4 * k:4 * k + 4, :], stg[:, :, :]), reads=[stg_r], writes=[rs[g % 2][k]])
                    stgi[0] += 1

        load_group(0)
        for g in range(ngrp):
            c0 = g * GW
            cw = min(GW, DFF - c0)
            wg, _ = wg_b[g % 2]
            wu, _ = wu_b[g % 2]
            if g + 1 < ngrp:
                load_group(g + 1)
            for o in range(cw // 128):
                j = (c0 // 128) + o
                gp, gp_r = gp_b[j % 2]
                cv, cv_r = cv_b[j % 2]
                sl, sl_r = sl_b[j % 2]
                pus = []
                for (t0, tn) in TL:
                    pg, pg_r = next_ps()
                    for kc in range(KC):
                        C.op("pe", lambda pe: pe.matmul(pg[:, 0:tn], wg[:, kc, o * 128:(o + 1) * 128],
                                                        fT[:, kc, t0:t0 + tn], start=(kc == 0), stop=(kc == KC - 1)),
                             reads=[wg_rs[g % 2][kc // 4], fT_rs[kc]], writes=[pg_r], inc=(kc == KC - 1))
                    C.op("act", lambda a: a.copy(gp[:, t0:t0 + tn], pg[:, 0:tn]), reads=[pg_r], writes=[gp_r])
                    pu, pu_r = next_ps()
                    for kc in range(KC):
                        C.op("pe", lambda pe: pe.matmul(pu[:, 0:tn], wu[:, kc, o * 128:(o + 1) * 128],
                                                        fT[:, kc, t0:t0 + tn], start=(kc == 0), stop=(kc == KC - 1)),
                             reads=[wu_rs[g % 2][kc // 4], fT_rs[kc]], writes=[pu_r], inc=(kc == KC - 1))
                    ub, ub_r = ub_b[j % 2]
                    C.op("act", lambda a: a.copy(ub[:, t0:t0 + tn], pu[:, 0:tn]), reads=[pu_r], writes=[ub_r])
                    pus.append((ub, ub_r, t0, tn))
                C.op("act", lambda a: a.activation(cv[:, 2:NM + 2], gp[:, 0:NM], AF.Copy, scale=fcw[:, j, 0:1]),
                     reads=[gp_r, fcw_r], writes=[cv_r])
                C.op("dve", lambda p: p.scalar_tensor_tensor(cv[:, 2:NM + 2], gp[:, 1:NM + 1], fcw[:, j, 1:2], cv[:, 2:NM + 2],
                                                              ALU.mult, ALU.add), reads=[gp_r, fcw_r, cv_r], writes=[cv_r])
                C.op("dve", lambda v: v.scalar_tensor_tensor(cv[:, 2:NM + 2], gp[:, 2:NM + 2], fcw[:, j, 2:3], cv[:, 2:NM + 2],
                                                             ALU.mult, ALU.add), reads=[gp_r, fcw_r, cv_r], writes=[cv_r])
                C.op("pool", lambda p: p.memset(cv[:, 0:2], 0.0), writes=[cv_r], reads=[cv_r])
                sc = NM + 2
                C.op("dve", lambda v: v.tensor_scalar(cv[:, sc:sc + 1], fh[:, j, 0:1], fcw[:, j, 0:1], None, ALU.mult),
                     reads=[fh_r, fcw_r, cv_r], writes=[cv_r])
                C.op("dve", lambda v: v.scalar_tensor_tensor(cv[:, sc:sc + 1], fh[:, j, 1:2], fcw[:, j, 1:2], cv[:, sc:sc + 1],
                                                             ALU.mult, ALU.add), reads=[fh_r, fcw_r, cv_r], writes=[cv_r])
                C.op("dve", lambda v: v.scalar_tensor_tensor(cv[:, sc:sc + 1], gp[:, sc:sc + 1], fcw[:, j, 2:3], cv[:, sc:sc + 1],
                                                             ALU.mult, ALU.add), reads=[gp_r, fcw_r, cv_r], writes=[cv_r])
                C.op("pool", lambda p: p.tensor_copy(fcs[:, j, 0:3], gp[:, NM:NM + 3]), reads=[gp_r, fcs_r], writes=[fcs_r])
                def tail(j=j, sl=sl, sl_r=sl_r, cv=cv, cv_r=cv_r, ub=ub, ub_r=ub_r):
                    C.op("act", lambda a: a.activation(sl[:, :], cv[:, :], AF.Silu), reads=[cv_r], writes=[sl_r])
                    C.op("dve", lambda v: v.tensor_tensor(actT[:, j, :], sl[:, :], ub[:, :], ALU.mult),
                         reads=[sl_r, ub_r], writes=[act_rs[j]])
                if pend[0] is not None:
                    pend[0]()
                pend[0] = tail
        pend[0]()
        C.dma("sp", fc_d[:, :, :], fcs[:, :, :], reads=[fcs_r], writes=[C.res("fc_out")], is_out=True)
        C.free(*[r for _, r in wg_b + wu_b + gp_b + cv_b + sl_b + ub_b + stg_b], *[r for l in wg_rs + wu_rs for r in l])
        Sg.close()

        _sc.__exit__(None, None, None); _sc = nc.named_scope('st4'); _sc.__enter__()
        Sd = contextlib.ExitStack()
        wd_b = [C.sb(Sd, f"wd{i}", [128, NJ, 128], BF16) for i in range(2)]
        hc_b = [C.sb(Sd, f"hc{i}", [128, NT], F32) for i in range(2)]
        h2s_res = C.res("h2s")
        pass
        JS = [(0, 11), (11, 11), (22, 11), (33, 10)]
        stg4 = [C.sb(Sd, f"stgd{i}", [128, 11, 128], F32) for i in range(4)]
        wd_rs = [[C.res(f"wd{i}_{k}") for k in range(4)] for i in range(2)]
        s4i = [0]

        def load_wd(oc):
            wd, _ = wd_b[oc % 2]
            for k, (j0, jn) in enumerate(JS):
                stg, stg_r = stg4[s4i[0] % 4]
                C.dma("sp", stg[:, 0:jn, :], w_down_d[oc, :, j0:j0 + jn, :], writes=[stg_r])
                if s4i[0] % 2 == 0:
                    C.op("act", lambda a: a.copy(wd[:, j0:j0 + jn, :], stg[:, 0:jn, :]), reads=[stg_r], writes=[wd_rs[oc % 2][k]])
                else:
                    C.op("dve", lambda v: v.tensor_copy(wd[:, j0:j0 + jn, :], stg[:, 0:jn, :]), reads=[stg_r], writes=[wd_rs[oc % 2][k]])
                s4i[0] += 1

        load_wd(0)
        for oc in range(KC):
            wd, _ = wd_b[oc % 2]
            hc, hc_r = hc_b[oc % 2]
            if oc + 1 < KC:
                load_wd(oc + 1)
            C.dma("sp", hc[:, :], hs_d[:, oc, :], reads=[hs_res], writes=[hc_r])
            for (t0, tn) in TL:
                pt, pr = next_ps()
                for j in range(NJ):
                    C.op("pe", lambda pe: pe.matmul(pt[:, 0:tn], wd[:, j, :], actT[:, j, t0:t0 + tn],
                                                    start=(j == 0), stop=(j == NJ - 1)),
                         reads=[wd_rs[oc % 2][min(j // 11, 3)], act_rs[j]], writes=[pr], inc=(j == NJ - 1))
                C.op("dve", lambda v: v.tensor_tensor(hc[:, t0:t0 + tn], hc[:, t0:t0 + tn], pt[:, 0:tn], ALU.add),
                     reads=[pr, hc_r], writes=[hc_r])
            C.dma("sp", h2s_d[:, oc, :], hc[:, :], reads=[hc_r], writes=[h2s_res], part=True, owner=hc_r)
        C.free(*[r for _, r in wd_b + hc_b + stg4], *[r for l in wd_rs for r in l], *act_rs, *fT_rs)
        Sd.close()
        Sa.close()
        Sf.close()
        Sh = contextlib.ExitStack()
        h2T, _ = C.sb(Sh, "h2T", [128, KC, NT], F32)
        h2_rs = [C.res(f"h2{k}") for k in range(KC)]
        for kc in range(KC):
            C.dma("sp", h2T[:, kc, :], h2s_d[:, kc, :], reads=[h2s_res], writes=[h2_rs[kc]])

        _sc.__exit__(None, None, None); _sc = nc.named_scope('st5'); _sc.__enter__()
        Sp = contextlib.ExitStack()
        gT, _ = C.sb(Sp, "gT", [128, KC, NT], BF16)
        gT_rs = [C.res(f"gT{k}") for k in range(KC)]
        sq_bufs = [C.sb(Sp, f"sqb{i}", [128, NT], BF16) for i in range(2)]
        rms_stats(h2T, h2_rs, sq_bufs)
        for kc in range(KC):
            C.op("dve", lambda v: v.scalar_tensor_tensor(gT[:, kc, :], h2T[:, kc, :], nrm[:, 1, kc:kc + 1], rstd[:, :],
                                                         ALU.mult, ALU.mult),
                 reads=[h2_rs[kc], nrm_r, rstd_r], writes=[gT_rs[kc]])
        pT, pT_r = C.sb(Sp, "pT", [128, 2, NT], BF16)
        C.dma("pool", pT[:, :, :], pT_d[:, :, :], writes=[pT_r])
        wpp, wpp_r = C.sb(Sp, "wpp", [128, 2, D], BF16)
        C.dma("pool", wpp[:, :, :], w_pp_d[:, :, :], writes=[wpp_r])
        wpg_b = [C.sb(Sp, f"wpg{i}", [128, KC, 512], BF16) for i in range(2)]
        sg_b = [C.sb(Sp, f"sg{i}", [128, 512], F32) for i in range(2)]
        pass
        cnt = 0
        for g in range(4):
            wt, wr = wpg_b[g % 2]
            C.dma("pool", wt[:, :, :], w_pg_d[g, :, :, :], writes=[wr])
            for o in range(4):
                oc = g * 4 + o
                for (t0, tn) in TL:
                    pt, pr = next_ps()
                    for kc in range(KC):
                        C.op("pe", lambda pe: pe.matmul(pt[:, 0:tn], wt[:, kc, o * 128:(o + 1) * 128],
                                                        gT[:, kc, t0:t0 + tn], start=(kc == 0), stop=(kc == KC - 1)),
                             reads=[wr, gT_rs[kc]], writes=[pr], inc=(kc == KC - 1))
                    sg, sg_r = sg_b[cnt % 2]
                    cnt += 1
                    C.op("act", lambda a: a.activation(sg[:, 0:tn], pt[:, 0:tn], AF.Sigmoid), reads=[pr], writes=[sg_r])
                    p2, p2r = next_ps()
                    for kc in range(2):
                        C.op("pe", lambda pe: pe.matmul(p2[:, 0:tn], wpp[:, kc, oc * 128:(oc + 1) * 128],
                                                        pT[:, kc, t0:t0 + tn], start=(kc == 0), stop=(kc == 1)),
                             reads=[wpp_r, pT_r], writes=[p2r], inc=(kc == 1))
                    C.op("dve", lambda v: v.tensor_tensor(sg[:, 0:tn], sg[:, 0:tn], p2[:, 0:tn], ALU.mult),
                         reads=[sg_r, p2r], writes=[sg_r])
                    C.op("dve", lambda p: p.tensor_tensor(h2T[:, oc, t0:t0 + tn], h2T[:, oc, t0:t0 + tn], sg[:, 0:tn], ALU.add),
                         reads=[sg_r, h2_rs[oc]], writes=[h2_rs[oc]])
        _sc.__exit__(None, None, None); _sc = nc.named_scope('st6'); _sc.__enter__()
        rms_stats(h2T, h2_rs, sq_bufs)
        yb = [C.sb(Sp, f"yb{i}", [128, NM + 1], F32) for i in range(2)]
        y_res = C.res("y_out")
        for kc in range(KC):
            yt, yr = yb[kc % 2]
            C.op("dve", lambda v: v.scalar_tensor_tensor(yt[:, :], h2T[:, kc, 2:NT], nrm[:, 2, kc:kc + 1], rstd[:, 2:NT],
                                                         ALU.mult, ALU.mult),
                 reads=[h2_rs[kc], nrm_r, rstd_r], writes=[yr])
            C.dma("sp", yT_d[:, kc, :], yt[:, :], reads=[yr], writes=[y_res], part=True, is_out=True, owner=yr)
        _sc.__exit__(None, None, None)
        C.finish()
        Sp.close()
        Sh.close()
    return nc


def l2_weight_layouts(w_out, w_gate, w_up, w_down, w_pg, w_pp):
    f32 = np.float32

    def grp(w, gw):
        n = w.shape[1]
        ng = (n + gw - 1) // gw
        wp = np.zeros((w.shape[0], ng * gw), f32)
        wp[:, :n] = w
        return np.ascontiguousarray(wp.reshape(KC, 128, ng, gw).transpose(2, 1, 0, 3))
    return dict(w_out=grp(w_out, 512), w_gate=grp(w_gate, 256), w_up=grp(w_up, 256),
                w_down=np.ascontiguousarray(w_down.astype(f32).reshape(NJ, 128, KC, 128).transpose(2, 1, 0, 3)),
                w_pg=grp(w_pg, 512),
                w_pp=np.ascontiguousarray(w_pp.astype(f32).reshape(2, 128, D).transpose(1, 0, 2)))


HD = 128
SB_SCALE = HD ** -0.5
NW = 898
C_ID, C_TRIL, C_MST, C_UBD, C_BONES, C_MPOS, C_MNEG, C_SEL0, C_SEL1, C_SU = range(10)
NCST = 10 * 128
P_CB, P_SBN, P_ALOG, P_DTB, P_CW, P_GNC, P_GNR, P_AN = 0, 1, 2, 3, 4, 16, 17, 17 + 128
NPRM = P_AN + KC


def make_consts():
    c = np.zeros((128, 10, 128), np.float32)
    i = np.arange(128)
    same = (i[:, None] // 64) == (i[None, :] // 64)
    c[:, C_ID] = np.eye(128)
    c[:, C_TRIL] = (i[:, None] >= i[None, :])
    c[:, C_MST] = (i[None, :] > i[:, None])
    c[:, C_UBD] = (i[:, None] <= i[None, :]) & same
    c[:, C_BONES] = same
    c[:, C_MPOS] = np.where((i[:, None] > i[None, :]) & same, 0.0, 30000.0)
    c[:, C_MNEG] = np.where((i[None, :] >= i[:, None]) & same, 0.0, -30000.0)
    c[:, C_SEL0] = (i[:, None] < 64) * np.ones((1, 128))
    c[:, C_SEL1] = (i[:, None] >= 64) * np.ones((1, 128))
    c[:, C_SU] = (i[:, None] > i[None, :])
    return c.reshape(128, NCST)


def build_l1(T=4096, NB=2, NS=8, NPOOL=1280, do_sample=True, stages=('attn', 'gdn'), par=True):
    NTOK = NB * T + NS
    NBLK = T // 128
    NTILE = T // 512
    nc = bass.Bass("TRN2", target_bir_lowering=False)
    din = lambda name, shape, dt=F32: nc.dram_tensor(name, list(shape), dt, kind="ExternalInput")
    dout = lambda name, shape, dt=F32: nc.dram_tensor(name, list(shape), dt, kind="ExternalOutput")
    xT_d = din("xT", [128, KC, NTOK])
    w1_d = din("w1", [D, NW])
    prm_d = din("prm", [128, NPRM])
    cst_d = din("cst", [128, NCST])
    poolk_d = din("poolk", [NPOOL, 128 * HD])
    poolv_d = din("poolv", [NPOOL, 128 * HD])
    ptab_d = din("ptab", [128, NS], I32)
    ghist_d = din("ghist", [128, 3, 3, NS])
    grec_d = din("grec", [NS, 128, 128])
    kT_o = dout("kT_o", [128, NTOK])
    vT_o = dout("vT_o", [128, NTOK])
    msb_o = dout("msb_o", [128, NTOK])
    mgd_o = dout("mgd_o", [128, NTOK])
    gcv_o = dout("gcv_o", [128, 3, NB, 3])
    gcs_o = dout("gcs_o", [128, 3, 3, NS])
    grp_o = dout("grp_o", [NB, 128, 128])
    grs_o = dout("grs_o", [NS, 128, 128])

    C = Ctx(nc)
    S0 = contextlib.ExitStack()
    S0.enter_context(nc.allow_low_precision("bf16 matmul operands, fp32 accumulation"))
    sbt = lambda name, shape, dt=F32: C.sb(S0, name, shape, dt)
    op = C.op

    cst, cst_r = sbt("cst", [128, NCST])
    C.dma("sp", cst[:, :], cst_d[:, :], writes=[cst_r])
    prm, prm_r = sbt("prm", [128, NPRM])
    C.dma("sp", prm[:, :], prm_d[:, :], writes=[prm_r])
    cb = lambda k: cst[:, k * 128:(k + 1) * 128]
    ident_f = cb(C_ID)
    w1, w1_r = sbt("w1", [128, KC, NW], BF16)
    C.dma("pool", w1[:, :, :], w1_d.ap().rearrange("(kc p) n -> p kc n", p=128), writes=[w1_r])
    ident_b, idb_r = sbt("ident_b", [128, 128], BF16)
    op("dve", lambda v: v.tensor_copy(ident_b[:, :], ident_f), reads=[cst_r], writes=[idb_r])
    tril_b, trb_r = sbt("tril_b", [128, 128], BF16)
    op("dve", lambda v: v.tensor_copy(tril_b[:, :], cb(C_TRIL)), reads=[cst_r], writes=[trb_r])
    ones_b, onb_r = sbt("ones_b", [128, 128], BF16)
    op("pool", lambda g: g.memset(ones_b[:, :], 1.0), writes=[onb_r])
    ones_f, onf_r = sbt("ones_f", [128, 128])
    op("pool", lambda g: g.memset(ones_f[:, :], 1.0), writes=[onf_r])
    zeros_b, zb_r = sbt("zeros_b", [128, 512], BF16)
    op("pool", lambda g: g.memset(zeros_b[:, :], 0.0), writes=[zb_r])
    nexpA, nea_r = sbt("nexpA", [128, 1])
    op("act", lambda a: a.activation(nexpA[:, :], prm[:, P_ALOG:P_ALOG + 1], AF.Exp), reads=[prm_r], writes=[nea_r])
    op("dve", lambda v: v.tensor_scalar(nexpA[:, :], nexpA[:, :], -1.0, None, ALU.mult), reads=[nea_r], writes=[nea_r])

    PS = [C.ps(S0, f"ps{i}", [128, 512], F32) for i in range(6)]
    PSB = [C.ps(S0, f"psb{i}", [128, 1024], BF16) for i in range(1)]
    PSG = [C.ps(S0, f"psg{i}", [128, 512], F32) for i in range(1)] + PS[1:3]
    psbi = [0]
    psi = [0]

    def next_psb():
        t = PSB[0]
        psbi[0] += 1
        return t

    PAR = Par(C)
    import os as _os
    PAR.weights = {0: int(_os.environ.get('ATTW', '1')), 1: int(_os.environ.get('WYW', '2')), 2: int(_os.environ.get('SCW', '1'))}
    ps_cnt = {}

    def next_ps():
        who = PAR.cur() if C.par is not None else None
        if who == 0:
            lst = PS[3:5]
        elif who == 1:
            lst = PS[1:3]
        elif who == 2:
            lst = [PSG[0], PS[5]]
        else:
            lst = PS[1:6]
        k = ps_cnt.get(who, 0)
        ps_cnt[who] = k + 1
        return lst[k % len(lst)]

    def next_acc():
        return PS[0]

    rot = {}

    def rt(name, shape, dt=F32, n=2):
        if name not in rot:
            rot[name] = [[sbt(f"{name}_{i}", shape, dt) for i in range(n)], 0]
        lst = rot[name]
        t = lst[0][lst[1] % n]
        lst[1] += 1
        return t

    def rsqrt_to(out_ap, out_r, src_ap, src_rs, scale, n):
        tmp, tmp_r = rt(f"rsq{n}", [128, n], F32, 1)
        w = out_ap.shape[-1]
        op("dve", lambda v: v.tensor_scalar(tmp[:, 0:w], src_ap, scale, EPS, ALU.mult, ALU.add), reads=src_rs, writes=[tmp_r])
        op("act", lambda a: a.activation(tmp[:, 0:w], tmp[:, 0:w], AF.Ln), reads=[tmp_r], writes=[tmp_r])
        op("act", lambda a: a.activation(out_ap, tmp[:, 0:w], AF.Exp, scale=-0.5), reads=[tmp_r], writes=[out_r])

    Sm = S0
    BIGW = max(8 * T, 2 * 128 * HD)
    BIG, _ = C.sb(Sm, "BIG", [128, BIGW], BF16)
    bigi = [0]

    def big(name):
        i = bigi[0]
        bigi[0] += 1
        return BIG[:, i * T:(i + 1) * T], C.res(name)
    QT, QT_r = big("QT")
    KT, KT_r = big("KT")
    VTMf, VTM_r = big("VTM")
    VTM = VTMf.rearrange("p (b d) -> p b d", d=128)
    GQ, GQ_r = big("GQ")
    GK, GK_r = big("GK")
    GV, GV_r = big("GV")
    ZS, ZS_r = big("ZS")
    gtm, gtm_r = C.sb(Sm, "gtm", [128, NBLK], F32)
    btm, btm_r = C.sb(Sm, "btm", [128, NBLK], F32)
    xt, _ = C.sb(Sm, "xt", [128, KC, 256], F32)
    xt_rs = [C.res(f"xt{k}") for k in range(KC)]
    xn, _ = C.sb(Sm, "xn", [128, KC, 512], BF16)
    xn_rs = [C.res(f"xn{k}") for k in range(KC)]
    cin = [C.sb(Sm, f"cin{i}", [128, 3 + 512], F32) for i in range(3)]
    gcv_s, gcv_r = sbt("gcv_s", [128, 3, NB, 3])

    def proj_tile(col0, tn, dst):
        for h0 in range(0, tn, 256):
            hn = min(256, tn - h0)
            for kc in range(KC):
                C.dma("sp", xt[:, kc, 0:hn], xT_d[:, kc, col0 + h0:col0 + h0 + hn], writes=[xt_rs[kc]])
            pss, pss_r = next_ps()
            for kc in range(KC):
                sq, sq_r = rt("sq", [128, 512], BF16)
                op("act", lambda a: a.activation(sq[:, 0:hn], xt[:, kc, 0:hn], AF.Square), reads=[xt_rs[kc]], writes=[sq_r])
                op("pe", lambda pe: pe.matmul(pss[:, 0:hn], ones_b[:, :], sq[:, 0:hn], start=(kc == 0), stop=(kc == KC - 1)),
                   reads=[onb_r, sq_r], writes=[pss_r], inc=True)
            rstd, rstd_r = rt("rstd", [128, 512], F32, 1)
            rsqrt_to(rstd[:, 0:hn], rstd_r, pss[:, 0:hn], [pss_r], 1.0 / D, 512)
            for kc in range(KC):
                op("dve", lambda v: v.scalar_tensor_tensor(xn[:, kc, h0:h0 + hn], xt[:, kc, 0:hn], prm[:, P_AN + kc:P_AN + kc + 1],
                                                           rstd[:, 0:hn], ALU.mult, ALU.mult),
                   reads=[xt_rs[kc], prm_r, rstd_r], writes=[xn_rs[kc]])
        for cc in range(7):
            pt, pr = next_ps()
            for kc in range(KC):
                op("pe", lambda pe: pe.matmul(pt[:, 0:tn], w1[:, kc, cc * 128:(cc + 1) * 128], xn[:, kc, 0:tn],
                                              start=(kc == 0), stop=(kc == KC - 1)),
                   reads=[w1_r, xn_rs[kc]], writes=[pr], inc=(kc == KC - 1))
            dst(cc, pt, pr)

    def l2n_to(out_ap, out_r, y, y_r, tn, scale):
        sq, sq_r = rt("sq", [128, 512], BF16)
        op("act", lambda a: a.activation(sq[:, 0:tn], y[:, 0:tn], AF.Square), reads=[y_r], writes=[sq_r])
        pt, pr = next_ps()
        op("pe", lambda pe: pe.matmul(pt[:, 0:tn], ones_b[:, :], sq[:, 0:tn], start=True, stop=True),
           reads=[onb_r, sq_r], writes=[pr])
        rs, rs_r = rt("rstd", [128, 512], F32, 1)
        rsqrt_to(rs[:, 0:tn], rs_r, pt[:, 0:tn], [pr], 1.0, 512)
        op("dve", lambda v: v.scalar_tensor_tensor(out_ap, y[:, 0:tn], scale, rs[:, 0:tn], ALU.mult, ALU.mult),
           reads=[y_r, rs_r], writes=[out_r])

    def softplus_to(out_ap, out_r, src_ap, src_rs, bias_ap, scale, shape):
        tmp, tmp_r = rt("spl%d_%d" % tuple(shape), shape)
        kw = dict(scale=scale)
        if bias_ap is not None:
            kw["bias"] = bias_ap
        op("act", lambda a: a.activation(tmp[:, :], src_ap, AF.Exp, **kw), reads=src_rs + [prm_r], writes=[tmp_r])
        op("act", lambda a: a.activation(out_ap, tmp[:, :], AF.Ln, bias=1.0), reads=[tmp_r], writes=[out_r])

    def headnorm_out(po, po_r, tn, out_d, col0, wcol):
        osb, osb_r = rt("osb", [128, 512], F32, 1)
        op("act", lambda a: a.copy(osb[:, 0:tn], po[:, 0:tn]), reads=[po_r], writes=[osb_r])
        sq, sq_r = rt("sq", [128, 512], BF16)
        op("act", lambda a: a.activation(sq[:, 0:tn], osb[:, 0:tn], AF.Square), reads=[osb_r], writes=[sq_r])
        pt, pr = next_ps()
        op("pe", lambda pe: pe.matmul(pt[:, 0:tn], ones_b[:, :], sq[:, 0:tn], start=True, stop=True),
           reads=[onb_r, sq_r], writes=[pr])
        rs, rs_r = rt("rstd", [128, 512], F32, 1)
        rsqrt_to(rs[:, 0:tn], rs_r, pt[:, 0:tn], [pr], 1.0 / HD, 512)
        op("dve", lambda v: v.scalar_tensor_tensor(osb[:, 0:tn], osb[:, 0:tn], prm[:, wcol:wcol + 1], rs[:, 0:tn],
                                                   ALU.mult, ALU.mult), reads=[osb_r, prm_r, rs_r], writes=[osb_r])
        return osb, osb_r

    out_res = {k: C.res("o_" + k) for k in ("kT", "vT", "msb", "mgd", "gcv", "gcs", "grp", "grs")}

    def store(out_ap, src_ap, src_r, key):
        C.dma("sp", out_ap, src_ap, reads=[src_r], writes=[out_res[key]], part=True, is_out=True, owner=src_r)


    def zt(name, dt):
        t, r = sbt(name, [128, 128], dt)
        op("pool", lambda g: g.memset(t[:, :], 0.0), writes=[r])
        return t, r
    WT0 = [zt("wTm0_%d" % i, BF16) for i in range(2)]
    WT1 = [zt("wTm1_%d" % i, BF16) for i in range(2)]
    QM0 = [zt("qm0_%d" % i, BF16) for i in range(2)]
    QM1 = [zt("qm1_%d" % i, BF16) for i in range(2)]
    KD0 = [zt("kd0_%d" % i, BF16) for i in range(2)]
    KD1 = [zt("kd1_%d" % i, BF16) for i in range(2)]
    gnr = prm[:, P_GNR:P_GNR + 128]

    def mmf(lhsT, rhs, reads, n=128):
        pt, pr = next_ps()
        op("pe", lambda pe: pe.matmul(pt[:, 0:n], lhsT, rhs, start=True, stop=True), reads=reads, writes=[pr])
        return pt, pr

    gstate = {}

    def gdn_wy(b):
        H = gstate[b]
        for nb in range(NBLK):
            while nb - H['scan_done'] > 1:
                PAR.idle()
            cs = slice(nb * 128, (nb + 1) * 128)
            pp_ = nb % 2
            (wTm0, wTm0_r), (wTm1, wTm1_r) = WT0[pp_], WT1[pp_]
            (qm0, qm0_r), (qm1, qm1_r) = QM0[pp_], QM1[pp_]
            (kd0, kd0_r), (kd1, kd1_r) = KD0[pp_], KD1[pp_]
            gcol = gtm[:, nb:nb + 1]
            bcol = btm[:, nb:nb + 1]
            p1, p1_r = next_ps()
            for i, blkc in enumerate((C_UBD, C_BONES, C_SEL0, C_SEL1)):
                op("pe", lambda pe: pe.matmul(p1[:, i:i + 1], cb(blkc), gcol, start=True, stop=True),
                   reads=[cst_r, gtm_r], writes=[p1_r])
            sc, sc_r = rt("sc", [128, 8])
            op("dve", lambda v: v.tensor_copy(sc[:, 0:4], p1[:, 0:4]), reads=[p1_r], writes=[sc_r])
            op("act", lambda a: a.activation(sc[:, 4:5], sc[:, 0:1], AF.Exp), reads=[sc_r], writes=[sc_r])
            op("dve", lambda v: v.tensor_tensor(sc[:, 5:6], sc[:, 1:2], sc[:, 0:1], ALU.subtract), reads=[sc_r], writes=[sc_r])
            op("act", lambda a: a.activation(sc[:, 5:6], sc[:, 5:6], AF.Exp), reads=[sc_r], writes=[sc_r])
            op("dve", lambda v: v.tensor_tensor(sc[:, 6:7], sc[:, 4:5], bcol, ALU.mult), reads=[sc_r, btm_r], writes=[sc_r])
            op("act", lambda a: a.activation(sc[:, 2:4], sc[:, 2:4], AF.Exp), reads=[sc_r], writes=[sc_r])
            Dg, Dg_r = rt("Dg", [128, 128])
            op("dve", lambda v: v.tensor_scalar(Dg[:, :], ident_f, sc[:, 0:1], None, ALU.mult), reads=[cst_r, sc_r], writes=[Dg_r])
            pG, pG_r = mmf(ones_f[:, :], Dg[:, :], [onf_r, Dg_r])
            E1, E1_r = rt("E1", [128, 128])
            op("dve", lambda v: v.scalar_tensor_tensor(E1[:, :], pG[:, 0:128], sc[:, 0:1], cb(C_MPOS), ALU.subtract, ALU.add),
               reads=[pG_r, sc_r, cst_r], writes=[E1_r])
            op("act", lambda a: a.activation(E1[:, :], E1[:, :], AF.Exp, scale=-1.0), reads=[E1_r], writes=[E1_r])
            E2, E2_r = rt("E2", [128, 128])
            op("dve", lambda v: v.scalar_tensor_tensor(E2[:, :], pG[:, 0:128], sc[:, 0:1], cb(C_MNEG), ALU.subtract, ALU.add),
               reads=[pG_r, sc_r, cst_r], writes=[E2_r])
            op("act", lambda a: a.activation(E2[:, :], E2[:, :], AF.Exp), reads=[E2_r], writes=[E2_r])
            pK, pK_r = next_ps()
            op("pe", lambda pe: pe.matmul(pK[:, 0:128], GK[:, cs], GK[:, cs], start=True, stop=True), reads=[GK_r], writes=[pK_r])
            op("pe", lambda pe: pe.matmul(pK[:, 128:256], GK[:, cs], GQ[:, cs], start=True, stop=True), reads=[GK_r, GQ_r], writes=[pK_r])
            Nl, Nl_r = rt("Nl", [128, 128], F32, n=3)
            op("dve", lambda v: v.scalar_tensor_tensor(Nl[:, :], pK[:, 0:128], bcol, E1[:, :], ALU.mult, ALU.mult),
               reads=[pK_r, btm_r, E1_r], writes=[Nl_r])
            QKd, QKd_r = rt("QKd", [128, 128], BF16)
            op("dve", lambda v: v.tensor_tensor(QKd[:, :], pK[:, 128:256], E2[:, :], ALU.mult), reads=[pK_r, E2_r], writes=[QKd_r])
            pM, pM_r = mmf(Nl[:, :], ident_f, [Nl_r, cst_r])
            Ml, Ml_r = rt("Ml", [128, 128], F32, n=3)
            op("act", lambda a: a.copy(Ml[:, :], pM[:, 0:128]), reads=[pM_r], writes=[Ml_r])
            W, W_r = rt("W", [128, 128], F32, n=3)
            op("dve", lambda v: v.tensor_tensor(W[:, :], ident_f, pM[:, 0:128], ALU.subtract), reads=[pM_r, cst_r], writes=[W_r])
            for k in range(1, 6):
                pN, pN_r = mmf(Ml[:, :], Nl[:, :], [Ml_r, Nl_r])
                if k < 5:
                    pM2, pM2_r = mmf(Nl[:, :], Ml[:, :], [Ml_r, Nl_r])
                Nl2, Nl2_r = rt("Nl", [128, 128], F32, n=3)
                op("act", lambda a: a.copy(Nl2[:, :], pN[:, 0:128]), reads=[pN_r], writes=[Nl2_r])
                if k < 5:
                    Ml2, Ml2_r = rt("Ml", [128, 128], F32, n=3)
                    op("dve", lambda v: v.tensor_copy(Ml2[:, :], pM2[:, 0:128]), reads=[pM2_r], writes=[Ml2_r])
                pW, pW_r = mmf(Nl2[:, :], W[:, :], [Nl2_r, W_r])
                W2, W2_r = rt("W", [128, 128], F32, n=3)
                op("dve", lambda v: v.tensor_tensor(W2[:, :], W[:, :], pW[:, 0:128], ALU.add), reads=[W_r, pW_r], writes=[W2_r])
                Nl, Nl_r, W, W_r = Nl2, Nl2_r, W2, W2_r
                if k < 5:
                    Ml, Ml_r = Ml2, Ml2_r
            tpk, tpk_r = next_psb()
            op("pe", lambda pe: pe.transpose(tpk[:, 0:128], GK[:, cs], ident_b[:, :]), reads=[GK_r, idb_r], writes=[tpk_r])
            kbg, kbg_r = rt("kbg", [128, 128])
            op("dve", lambda v: v.tensor_scalar(kbg[:, :], tpk[:, 0:128], sc[:, 6:7], None, ALU.mult), reads=[tpk_r, sc_r], writes=[kbg_r])
            op("act", lambda a: a.activation(kd0[0:64, :], tpk[0:64, 0:128], AF.Copy, scale=sc[0:64, 5:6]), reads=[tpk_r, sc_r, kd0_r], writes=[kd0_r])
            op("act", lambda a: a.activation(kd1[64:128, :], tpk[64:128, 0:128], AF.Copy, scale=sc[64:128, 5:6]), reads=[tpk_r, sc_r, kd1_r], writes=[kd1_r])
            tpv, tpv_r = next_psb()
            op("pe", lambda pe: pe.transpose(tpv[:, 0:128], GV[:, cs], ident_b[:, :]), reads=[GV_r, idb_r], writes=[tpv_r])
            vb_, vb_r = rt("vbt", [128, 128])
            op("dve", lambda v: v.tensor_scalar(vb_[:, :], tpv[:, 0:128], bcol, None, ALU.mult), reads=[tpv_r, btm_r], writes=[vb_r])
            pu, pu_r = mmf(W[:, :], vb_[:, :], [W_r, vb_r])
            u, u_r = rt("u", [128, 128])
            op("act", lambda a: a.copy(u[:, :], pu[:, 0:128]), reads=[pu_r], writes=[u_r])
            pw, pw_r = mmf(kbg[:, :], W[:, :], [kbg_r, W_r])
            op("act", lambda a: a.mul(wTm0[:, 0:64], pw[:, 0:64], -1.0), reads=[pw_r, wTm0_r], writes=[wTm0_r])
            op("dve", lambda v: v.tensor_scalar(wTm1[:, 64:128], pw[:, 64:128], -1.0, None, ALU.mult), reads=[pw_r, wTm1_r], writes=[wTm1_r])
            op("pool", lambda g: g.tensor_copy(qm0[:, 0:64], GQ[:, nb * 128:nb * 128 + 64]), reads=[GQ_r, qm0_r], writes=[qm0_r])
            op("pool", lambda g: g.tensor_copy(qm1[:, 64:128], GQ[:, nb * 128 + 64:nb * 128 + 128]), reads=[GQ_r, qm1_r], writes=[qm1_r])
            H['blk'][nb] = dict(u=(u, u_r), QKd=(QKd, QKd_r), sc=(sc, sc_r), par=nb % 2)
            H['wy_done'] = nb + 1

    def gdn_scan(b):
        H = gstate[b]
        S, S_r = rt("S", [128, 128], F32, n=3)
        Sb, Sb_r = rt("Sb", [128, 128], BF16, n=3)
        op("pool", lambda g: g.memset(S[:, :], 0.0), reads=[S_r], writes=[S_r])
        op("pool", lambda g: g.memset(Sb[:, :], 0.0), reads=[Sb_r], writes=[Sb_r])
        mg, mg_r = None, None
        for nb in range(NBLK):
            while H['wy_done'] <= nb:
                PAR.idle()
            B_ = H['blk'].pop(nb)
            (u, u_r), (QKd, QKd_r), (sc, sc_r) = B_['u'], B_['QKd'], B_['sc']
            pp_ = B_['par']
            (wTm0, wTm0_r), (wTm1, wTm1_r) = WT0[pp_], WT1[pp_]
            (qm0, qm0_r), (qm1, qm1_r) = QM0[pp_], QM1[pp_]
            (kd0, kd0_r), (kd1, kd1_r) = KD0[pp_], KD1[pp_]
            cs = slice(nb * 128, (nb + 1) * 128)
            pa, pa_r = mmf(wTm0[:, :], Sb[:, :], [wTm0_r, Sb_r])
            vn, vn_r = rt("vn", [128, 128])
            op("dve", lambda v: v.tensor_tensor(vn[:, :], u[:, :], pa[:, 0:128], ALU.add), reads=[u_r, pa_r], writes=[vn_r])
            vnb, vnb_r = rt("vnb", [128, 128], BF16)
            op("act", lambda a: a.copy(vnb[:, :], vn[:, :]), reads=[vn_r], writes=[vnb_r])
            pq, pq_r = mmf(qm0[:, :], Sb[:, :], [qm0_r, Sb_r])
            o1, o1_r = rt("o1", [128, 128])
            op("act", lambda a: a.activation(o1[:, :], pq[:, 0:128], AF.Copy, scale=sc[:, 4:5]), reads=[pq_r, sc_r], writes=[o1_r])
            pS1, pS1_r = mmf(kd0[:, :], vnb[:, :], [kd0_r, vnb_r])
            S1, S1_r = rt("S", [128, 128], F32, n=3)
            op("dve", lambda v: v.scalar_tensor_tensor(S1[:, :], S[:, :], sc[:, 2:3], pS1[:, 0:128], ALU.mult, ALU.add),
               reads=[S_r, sc_r, pS1_r], writes=[S1_r])
            S1b, S1b_r = rt("Sb", [128, 128], BF16, n=3)
            op("act", lambda a: a.copy(S1b[:, :], S1[:, :]), reads=[S1_r], writes=[S1b_r])
            pb, pb_r = mmf(wTm1[:, :], S1b[:, :], [wTm1_r, S1b_r])
            vn2, vn2_r = rt("vn", [128, 128])
            op("dve", lambda v: v.tensor_tensor(vn2[:, :], vn[:, :], pb[:, 0:128], ALU.add), reads=[vn_r, pb_r], writes=[vn2_r])
            vn2b, vn2b_r = rt("vnb", [128, 128], BF16)
            op("act", lambda a: a.copy(vn2b[:, :], vn2[:, :]), reads=[vn2_r], writes=[vn2b_r])
            pq2, pq2_r = mmf(qm1[:, :], S1b[:, :], [qm1_r, S1b_r])
            op("act", lambda a: a.activation(o1[64:128, :], pq2[64:128, 0:128], AF.Copy, scale=sc[64:128, 4:5]), reads=[pq2_r, sc_r, o1_r], writes=[o1_r])
            pS2, pS2_r = mmf(kd1[:, :], vn2b[:, :], [kd1_r, vn2b_r])
            S2, S2_r = rt("S", [128, 128], F32, n=3)
            op("dve", lambda v: v.scalar_tensor_tensor(S2[:, :], S1[:, :], sc[:, 3:4], pS2[:, 0:128], ALU.mult, ALU.add),
               reads=[S1_r, sc_r, pS2_r], writes=[S2_r])
            S2b, S2b_r = rt("Sb", [128, 128], BF16, n=3)
            op("act", lambda a: a.copy(S2b[:, :], S2[:, :]), reads=[S2_r], writes=[S2b_r])
            pqk, pqk_r = mmf(QKd[:, :], vn2b[:, :], [QKd_r, vn2b_r])
            op("dve", lambda v: v.tensor_tensor(o1[:, :], o1[:, :], pqk[:, 0:128], ALU.add), reads=[o1_r, pqk_r], writes=[o1_r])
            S, S_r, Sb, Sb_r = S2, S2_r, S2b, S2b_r
            jk, jk_r = rt("jk", [128, 128])
            ssc, ssc_r = rt("ssc", [128, 1])
            op("act", lambda a: a.activation(jk[:, :], o1[:, :], AF.Square, accum_out=ssc[:, :]), reads=[o1_r], writes=[jk_r, ssc_r])
            rs1, rs1_r = rt("rs1", [128, 1])
            rsqrt_to(rs1[:, :], rs1_r, ssc[:, :], [ssc_r], 1.0 / HD, 1)
            on, on_r = rt("on", [128, 128], F32)
            op("dve", lambda v: v.scalar_tensor_tensor(on[:, :], o1[:, :], rs1[:, 0:1], gnr, ALU.mult, ALU.mult),
               reads=[o1_r, rs1_r, prm_r], writes=[on_r])
            tpo, tpo_r = mmf(on[:, :], ident_f, [on_r, cst_r])
            k4 = nb % 4
            if k4 == 0:
                mg, mg_r = rt("mg", [128, 512], F32, 1)
            op("dve", lambda v: v.tensor_tensor(mg[:, k4 * 128:(k4 + 1) * 128], tpo[:, 0:128], ZS[:, cs], ALU.mult),
               reads=[tpo_r, ZS_r, mg_r], writes=[mg_r])
            if k4 == 3:
                store(mgd_o[:, b * T + (nb - 3) * 128:b * T + (nb + 1) * 128], mg[:, :], mg_r, "mgd")
            H['scan_done'] = nb + 1
        store(grp_o[b, :, :], S[:, :], S_r, "grp")


    def sample_part():
        col0 = NB * T
        C.free(QT_r, KT_r, VTM_r, GQ_r, GK_r, GV_r, ZS_r)
        Kg = BIG[:, 0:128 * HD]
        Vg = BIG[:, 128 * HD:2 * 128 * HD]
        Kg_r, Vg_r = C.res("Kg"), C.res("Vg")
        idx, idx_r = sbt("idx", [128, NS], I32)
        C.dma("sp", idx[:, :], ptab_d[:, :], writes=[idx_r])
        gh, gh_r = sbt("gh", [128, 3, 3, NS])
        C.dma("sp", gh[:, :, :, :], ghist_d[:, :, :, :], writes=[gh_r])
        Sall, Sall_r = sbt("Sall", [128, NS, 128])
        for s_ in range(NS):
            C.dma("sp", Sall[:, s_, :], grec_d[s_, :, :], writes=[Sall_r], part=True)
        sm = {}
        t8 = lambda name: sbt(name, [128, NS])

        def dst(cc, pt, pr):
            if cc == 0:
                t_, r_ = t8("s_q")
                op("act", lambda a: a.copy(t_[:, :], pt[:, 0:NS]), reads=[pr], writes=[r_])
                sm["q"] = (t_, r_)
            elif cc in (1, 2):
                t_, r_ = t8("s_kv%d" % cc)
                op("act", lambda a: a.copy(t_[:, :], pt[:, 0:NS]), reads=[pr], writes=[r_])
                store((kT_o if cc == 1 else vT_o)[:, col0:col0 + NS], t_[:, :], r_, "kT" if cc == 1 else "vT")
            elif cc in (3, 4, 5):
                q = cc - 3
                cn, cn_r = t8("s_cn%d" % q)
                op("act", lambda a: a.copy(cn[:, :], pt[:, 0:NS]), reads=[pr], writes=[cn_r])
                store(gcs_o[:, q, 2, :], cn[:, :], cn_r, "gcs")
                C.dma("sp", gcs_o[:, q, 0:2, :], gh[:, q, 1:3, :], reads=[gh_r], writes=[out_res["gcs"]], part=True, is_out=True, owner=C.res("ghst%d" % q))
                y, y_r = t8("s_y%d" % q)
                cw = lambda j: prm[:, P_CW + q * 4 + j:P_CW + q * 4 + j + 1]
                op("dve", lambda v: v.tensor_scalar(y[:, :], gh[:, q, 0, :], cw(0), None, ALU.mult), reads=[gh_r, prm_r], writes=[y_r])
                for j in (1, 2):
                    op("dve", lambda v: v.scalar_tensor_tensor(y[:, :], gh[:, q, j, :], cw(j), y[:, :], ALU.mult, ALU.add),
                       reads=[gh_r, prm_r, y_r], writes=[y_r])
                op("dve", lambda v: v.scalar_tensor_tensor(y[:, :], cn[:, :], cw(3), y[:, :], ALU.mult, ALU.add),
                   reads=[cn_r, prm_r, y_r], writes=[y_r])
                op("act", lambda a: a.activation(y[:, :], y[:, :], AF.Silu), reads=[y_r], writes=[y_r])
                if q == 2:
                    sm["gv"] = (y, y_r)
                else:
                    o_, r_ = t8("s_g%d" % q)
                    l2n_to(o_[:, :], r_, y, y_r, NS, HD ** -0.5 if q == 0 else 1.0)
                    sm["gq" if q == 0 else "gk"] = (o_, r_)
            else:
                t_, r_ = t8("s_zs")
                op("act", lambda a: a.activation(t_[:, :], pt[:, 0:NS], AF.Silu), reads=[pr], writes=[r_])
                sm["zs"] = (t_, r_)

        proj_tile(col0, NS, dst)
        pA, pA_r = next_ps()
        for kc in range(KC):
            op("pe", lambda pe: pe.matmul(pA[0:1, 0:NS], w1[:, kc, 896:897], xn[:, kc, 0:NS], start=(kc == 0), stop=(kc == KC - 1)),
               reads=[w1_r, xn_rs[kc]], writes=[pA_r], inc=(kc == KC - 1))
        row, row_r = sbt("s_row", [1, 2, NS])
        tmp1, tmp1_r = sbt("s_tmp1", [1, NS])
        op("act", lambda a: a.activation(tmp1[:, :], pA[0:1, 0:NS], AF.Exp, bias=prm[0:1, P_DTB:P_DTB + 1]), reads=[pA_r, prm_r], writes=[tmp1_r])
        op("act", lambda a: a.activation(tmp1[:, :], tmp1[:, :], AF.Ln, bias=1.0), reads=[tmp1_r], writes=[tmp1_r])
        op("dve", lambda v: v.tensor_scalar(row[:, 0, :], tmp1[:, :], nexpA[0:1, 0:1], None, ALU.mult), reads=[tmp1_r, nea_r], writes=[row_r])
        pB, pB_r = next_ps()
        for kc in range(KC):
            op("pe", lambda pe: pe.matmul(pB[0:1, 0:NS], w1[:, kc, 897:898], xn[:, kc, 0:NS], start=(kc == 0), stop=(kc == KC - 1)),
               reads=[w1_r, xn_rs[kc]], writes=[pB_r], inc=(kc == KC - 1))
        op("act", lambda a: a.activation(tmp1[:, :], pB[0:1, 0:NS], AF.Exp, scale=-1.0), reads=[pB_r, tmp1_r], writes=[tmp1_r])
        op("dve", lambda v: v.tensor_scalar(tmp1[:, :], tmp1[:, :], 1.0, None, ALU.add), reads=[tmp1_r], writes=[tmp1_r])
        op("dve", lambda v: v.reciprocal(row[:, 1, :], tmp1[:, :]), reads=[tmp1_r, row_r], writes=[row_r])
        pbc, pbc_r = mmf(ones_f[0:1, :], row[0:1, :, :].rearrange("p a n -> p (a n)"), [onf_r, row_r], n=2 * NS)
        bc, bc_r = sbt("s_bc", [128, 3, NS])
        op("dve", lambda v: v.tensor_copy(bc[:, 0:2, :].rearrange("p a n -> p (a n)"), pbc[:, 0:2 * NS]), reads=[pbc_r], writes=[bc_r])
        op("act", lambda a: a.activation(bc[:, 2, :], bc[:, 0, :], AF.Exp), reads=[bc_r], writes=[bc_r])
        egb, beb = bc[:, 2, :], bc[:, 1, :]

        pos, pos_r = next_acc()
        q_, q_r = sm["q"]
        for s_ in range(NS):
            C.indirect(Kg, poolk_d[:, :], bass.IndirectOffsetOnAxis(ap=idx[:, s_:s_ + 1], axis=0), reads=[idx_r], writes=[Kg_r])
            C.indirect(Vg, poolv_d[:, :], bass.IndirectOffsetOnAxis(ap=idx[:, s_:s_ + 1], axis=0), reads=[idx_r], writes=[Vg_r])
            Qrep, Qrep_r = rt("Qrep", [128, 128])
            op("dve", lambda v: v.tensor_scalar(Qrep[:, :], ones_f[:, :], q_[:, s_:s_ + 1], None, ALU.mult), reads=[onf_r, q_r], writes=[Qrep_r])
            pqb, pqb_r = mmf(Qrep[:, :], ident_f, [Qrep_r, cst_r])
            qbc, qbc_r = rt("qbc", [128, 128], BF16)
            op("act", lambda a: a.copy(qbc[:, :], pqb[:, 0:128]), reads=[pqb_r], writes=[qbc_r])
            z, z_r = rt("zs_", [128, 128])
            for r0 in range(0, 128, 8):
                prod, prod_r = rt("prod", [128, 8, 128], F32, 1)
                op("dve", lambda v: v.tensor_tensor(prod[:, :, :], Kg[:, r0 * 128:(r0 + 8) * 128].rearrange("p (r d) -> p r d", d=128),
                                                    qbc[:, :].unsqueeze(1).to_broadcast([128, 8, 128]), ALU.mult),
                   reads=[Kg_r, qbc_r], writes=[prod_r])
                op("dve", lambda v: v.reduce_sum(z[:, r0:r0 + 8], prod[:, :, :], AX.X), reads=[prod_r, z_r], writes=[z_r])
            pzT, pzT_r = mmf(z[:, :], ident_f, [z_r, cst_r])
            e, e_r = rt("se", [128, 128])
            op("act", lambda a: a.activation(e[:, :], pzT[:, 0:128], AF.Exp, bias=prm[:, P_CB:P_CB + 1], scale=SB_SCALE),
               reads=[pzT_r, prm_r], writes=[e_r])
            L, L_r = rt("sL", [128, 128])
            op("act", lambda a: a.activation(L[:, :], e[:, :], AF.Ln, bias=1.0), reads=[e_r], writes=[L_r])
            pS, pS_r = next_ps()
            op("pe", lambda pe: pe.matmul(pS[:, 0:128], cb(C_TRIL), L[:, :], start=True, stop=False), reads=[cst_r, L_r], writes=[pS_r], inc=False)
            pT, pT_r = mmf(L[:, :], ones_f[:, :], [L_r, onf_r])
            Tbt, Tbt_r = rt("Tbt", [128, 128])
            op("act", lambda a: a.copy(Tbt[:, :], pT[:, 0:128]), reads=[pT_r], writes=[Tbt_r])
            op("pe", lambda pe: pe.matmul(pS[:, 0:128], Tbt[:, :], cb(C_SU), start=False, stop=True), reads=[Tbt_r, cst_r], writes=[pS_r])
            w_, w_r = rt("sw", [128, 128])
            op("act", lambda a: a.activation(w_[:, :], pS[:, 0:128], AF.Exp, scale=-1.0), reads=[pS_r], writes=[w_r])
            op("dve", lambda v: v.tensor_tensor(w_[:, :], w_[:, :], e[:, :], ALU.mult), reads=[w_r, e_r], writes=[w_r])
            paT, paT_r = mmf(w_[:, :], ident_f, [w_r, cst_r])
            aT, aT_r = rt("aT", [128, 128], BF16)
            op("act", lambda a: a.copy(aT[:, :], paT[:, 0:128]), reads=[paT_r], writes=[aT_r])
            for r in range(128):
                op("pe", lambda pe: pe.matmul(pos[:, s_:s_ + 1], Vg[:, r * 128:(r + 1) * 128], aT[:, r:r + 1], start=(r == 0), stop=(r == 127)),
                   reads=[Vg_r, aT_r], writes=[pos_r], inc=(r == 127))
        osb, osb_r = headnorm_out(pos, pos_r, NS, msb_o, col0, P_SBN)
        store(msb_o[:, col0:col0 + NS], osb[:, 0:NS], osb_r, "msb")

        gq_, gq_r = sm["gq"]
        gk_, gk_r = sm["gk"]
        gv_, gv_r = sm["gv"]
        zs_, zs_r = sm["zs"]
        pks, pks_r = next_ps()
        for s_ in range(NS):
            op("pe", lambda pe: pe.matmul(pks[:, s_:s_ + 1], Sall[:, s_, :], gk_[:, s_:s_ + 1], start=True, stop=True), reads=[Sall_r, gk_r], writes=[pks_r])
            op("pe", lambda pe: pe.matmul(pks[:, NS + s_:NS + s_ + 1], Sall[:, s_, :], gq_[:, s_:s_ + 1], start=True, stop=True), reads=[Sall_r, gq_r], writes=[pks_r])
        vn, vn_r = t8("s_vn")
        op("dve", lambda v: v.tensor_tensor(vn[:, :], pks[:, 0:NS], egb, ALU.mult), reads=[pks_r, bc_r], writes=[vn_r])
        op("dve", lambda v: v.tensor_tensor(vn[:, :], gv_[:, :], vn[:, :], ALU.subtract), reads=[gv_r, vn_r], writes=[vn_r])
        op("dve", lambda v: v.tensor_tensor(vn[:, :], vn[:, :], beb, ALU.mult), reads=[vn_r, bc_r], writes=[vn_r])
        so, so_r = t8("s_o")
        op("dve", lambda v: v.tensor_tensor(so[:, :], pks[:, NS:2 * NS], egb, ALU.mult), reads=[pks_r, bc_r], writes=[so_r])
        pr_, pr_r = t8("s_pr")
        op("dve", lambda v: v.tensor_tensor(pr_[:, :], gq_[:, :], gk_[:, :], ALU.mult), reads=[gq_r, gk_r], writes=[pr_r])
        pqk, pqk_r = mmf(ones_f[:, :], pr_[:, :], [onf_r, pr_r], n=NS)
        op("dve", lambda v: v.tensor_tensor(pr_[:, :], pqk[:, 0:NS], vn[:, :], ALU.mult), reads=[pqk_r, vn_r, pr_r], writes=[pr_r])
        op("dve", lambda v: v.tensor_tensor(so[:, :], so[:, :], pr_[:, :], ALU.add), reads=[so_r, pr_r], writes=[so_r])
        og, og_r = headnorm_out(so, so_r, NS, mgd_o, col0, P_GNC)
        op("dve", lambda v: v.tensor_tensor(og[:, 0:NS], og[:, 0:NS], zs_[:, :], ALU.mult), reads=[og_r, zs_r], writes=[og_r])
        store(mgd_o[:, col0:col0 + NS], og[:, 0:NS], og_r, "mgd")
        pkr, pkr_r = next_ps()
        op("pe", lambda pe: pe.matmul(pkr[0:NS, 0:128], gk_[:, :], ident_f, start=True, stop=True), reads=[gk_r, cst_r], writes=[pkr_r])
        op("pe", lambda pe: pe.matmul(pkr[0:NS, 128:256], vn[:, :], ident_f, start=True, stop=True), reads=[vn_r, cst_r], writes=[pkr_r])
        kvr, kvr_r = sbt("s_kvr", [NS, 256])
        op("act", lambda a: a.copy(kvr[:, :], pkr[0:NS, 0:256]), reads=[pkr_r], writes=[kvr_r])
        for s_ in range(NS):
            vm, vm_r = rt("s_vm", [NS, 128])
            op("dve", lambda v: v.tensor_scalar(vm[:, :], kvr[:, 128:256], cst[0:NS, C_ID * 128 + s_:C_ID * 128 + s_ + 1], None, ALU.mult),
               reads=[kvr_r, cst_r], writes=[vm_r])
            pO, pO_r = mmf(kvr[:, 0:128], vm[:, :], [kvr_r, vm_r])
            Sn, Sn_r = rt("s_Sn", [128, 128])
            op("dve", lambda v: v.scalar_tensor_tensor(Sn[:, :], Sall[:, s_, :], bc[:, 2, s_:s_ + 1], pO[:, 0:128], ALU.mult, ALU.add),
               reads=[Sall_r, bc_r, pO_r], writes=[Sn_r])
            store(grs_o[s_, :, :], Sn[:, :], Sn_r, "grs")

    def attn_batch(b):
        for G in (range(NTILE) if 'attn' in stages else []):
            racc, racc_r = rt("racc", [128, 512], F32, 1)
            op("pool", lambda g: g.memset(racc[:, :], 0.0), reads=[racc_r], writes=[racc_r])
            po, po_r = next_acc()
            op("pe", lambda pe: pe.matmul(po[:, 0:512], zeros_b[:, 0:128], zeros_b[:, 0:512], start=True, stop=False),
               reads=[zb_r], writes=[po_r], inc=False)
            for jb in range(4 * G + 3, -1, -1):
                c0 = max(0, jb - 4 * G) * 128
                wd = 512 - c0
                pz, pz_r = next_ps()
                op("pe", lambda pe: pe.matmul(pz[:, c0:512], KT[:, jb * 128:(jb + 1) * 128], QT[:, G * 512 + c0:(G + 1) * 512],
                                              start=True, stop=True), reads=[KT_r, QT_r], writes=[pz_r])
                e, e_r = rt("e", [128, 512])
                op("act", lambda a: a.activation(e[:, c0:512], pz[:, c0:512], AF.Exp, bias=prm[:, P_CB:P_CB + 1], scale=SB_SCALE),
                   reads=[pz_r, prm_r], writes=[e_r])
                if jb >= 4 * G:
                    op("dve", lambda g: g.tensor_tensor(e[:, c0:c0 + 128], e[:, c0:c0 + 128], cb(C_MST), ALU.mult),
                       reads=[e_r, cst_r], writes=[e_r])
                L, L_r = rt("L", [128, 512], BF16)
                op("act", lambda a: a.activation(L[:, c0:512], e[:, c0:512], AF.Ln, bias=1.0), reads=[e_r], writes=[L_r])
                pS, pS_r = next_ps()
                op("pe", lambda pe: pe.matmul(pS[:, c0:512], tril_b[:, :], L[:, c0:512], start=True, stop=True),
                   reads=[trb_r, L_r], writes=[pS_r])
                if jb > 0:
                    pR, pR_r = next_ps()
                    op("pe", lambda pe: pe.matmul(pR[:, c0:512], ones_b[:, :], L[:, c0:512], start=True, stop=True),
                       reads=[onb_r, L_r], writes=[pR_r])
                t, t_r = rt("t", [128, 512], F32, 1)
                op("dve", lambda v: v.tensor_tensor(t[:, c0:512], pS[:, c0:512], racc[:, c0:512], ALU.add),
                   reads=[pS_r, racc_r], writes=[t_r])
                op("act", lambda a: a.activation(t[:, c0:512], t[:, c0:512], AF.Exp, scale=-1.0), reads=[t_r], writes=[t_r])
                av, av_r = rt("a", [128, 512], BF16)
                op("dve", lambda g: g.tensor_tensor(av[:, c0:512], e[:, c0:512], t[:, c0:512], ALU.mult),
                   reads=[e_r, t_r], writes=[av_r])
                if jb > 0:
                    op("dve", lambda v: v.tensor_tensor(racc[:, c0:512], racc[:, c0:512], pR[:, c0:512], ALU.add),
                       reads=[pR_r, racc_r], writes=[racc_r])
                op("pe", lambda pe: pe.matmul(po[:, c0:512], VTM[:, jb, :], av[:, c0:512], start=False, stop=(jb == 0)),
                   reads=[VTM_r, av_r], writes=[po_r], inc=True)
            osb, osb_r = headnorm_out(po, po_r, 512, msb_o, 0, P_SBN)
            store(msb_o[:, b * T + G * 512:b * T + (G + 1) * 512], osb[:, :], osb_r, "msb")


    for b in range(NB):
        _sc = nc.named_scope('proj%d' % b); _sc.__enter__()
        for ti in range(NTILE):
            t0 = ti * 512
            gcol = b * T + t0

            def dst(cc, pt, pr, t0=t0, gcol=gcol, ti=ti):
                tn = 512
                if cc == 0:
                    op("act", lambda a: a.copy(QT[:, t0:t0 + tn], pt[:, 0:tn]), reads=[pr], writes=[QT_r])
                elif cc in (1, 2):
                    st, st_r = rt("kvst", [128, 512], F32, 1)
                    op("act", lambda a: a.copy(st[:, :], pt[:, 0:tn]), reads=[pr], writes=[st_r])
                    store((kT_o if cc == 1 else vT_o)[:, gcol:gcol + tn], st[:, :], st_r, "kT" if cc == 1 else "vT")
                    if cc == 1:
                        op("dve", lambda v: v.tensor_copy(KT[:, t0:t0 + tn], pt[:, 0:tn]), reads=[pr], writes=[KT_r])
                    else:
                        vb, vb_r = rt("vbf", [128, 512], BF16, 1)
                        op("dve", lambda v: v.tensor_copy(vb[:, :], pt[:, 0:tn]), reads=[pr], writes=[vb_r])
                        for k in range(4):
                            tp, tp_r = next_psb()
                            op("pe", lambda pe: pe.transpose(tp[:, 0:128], vb[:, k * 128:(k + 1) * 128], ident_b[:, :]),
                               reads=[vb_r, idb_r], writes=[tp_r])
                            op("act", lambda a: a.copy(VTM[:, ti * 4 + k, :], tp[:, 0:128]), reads=[tp_r], writes=[VTM_r])
                elif cc in (3, 4, 5):
                    ci, ci_r = cin[cc - 3]
                    q = cc - 3
                    if ti == 0:
                        op("pool", lambda g: g.memset(ci[:, 0:3], 0.0), reads=[ci_r], writes=[ci_r])
                    else:
                        op("pool", lambda g: g.tensor_copy(ci[:, 0:3], ci[:, 512:515]), reads=[ci_r], writes=[ci_r])
                    op("act", lambda a: a.copy(ci[:, 3:515], pt[:, 0:tn]), reads=[pr], writes=[ci_r])
                    if ti == NTILE - 1:
                        op("pool", lambda g: g.tensor_copy(gcv_s[:, q, b, :], ci[:, 512:515]), reads=[ci_r], writes=[gcv_r])
                    y, y_r = rt("cvy", [128, 512], F32, 1)
                    cw = lambda j: prm[:, P_CW + q * 4 + j:P_CW + q * 4 + j + 1]
                    op("dve", lambda v: v.tensor_scalar(y[:, :], ci[:, 0:512], cw(0), None, ALU.mult), reads=[ci_r, prm_r], writes=[y_r])
                    for j in (1, 2, 3):
                        op("dve", lambda v: v.scalar_tensor_tensor(y[:, :], ci[:, j:j + 512], cw(j), y[:, :], ALU.mult, ALU.add),
                           reads=[ci_r, prm_r, y_r], writes=[y_r])
                    op("act", lambda a: a.activation(y[:, :], y[:, :], AF.Silu), reads=[y_r], writes=[y_r])
                    if q == 0:
                        l2n_to(GQ[:, t0:t0 + tn], GQ_r, y, y_r, tn, HD ** -0.5)
                    elif q == 1:
                        l2n_to(GK[:, t0:t0 + tn], GK_r, y, y_r, tn, 1.0)
                    else:
                        op("dve", lambda v: v.tensor_copy(GV[:, t0:t0 + tn], y[:, :]), reads=[y_r], writes=[GV_r])
                else:
                    op("act", lambda a: a.activation(ZS[:, t0:t0 + tn], pt[:, 0:tn], AF.Silu), reads=[pr], writes=[ZS_r])

            proj_tile(gcol, 512, dst)
            pab, pab_r = next_ps()
            for k in range(4):
                for kc in range(KC):
                    op("pe", lambda pe: pe.matmul(pab[:, 2 * k:2 * k + 2], xn[:, kc, k * 128:(k + 1) * 128], w1[:, kc, 896:898],
                                                  start=(kc == 0), stop=(kc == KC - 1)),
                       reads=[w1_r, xn_rs[kc]], writes=[pab_r], inc=(kc == KC - 1))
            ab, ab_r = rt("ab", [128, 8])
            op("dve", lambda v: v.tensor_copy(ab[:, :], pab[:, 0:8]), reads=[pab_r], writes=[ab_r])
            abv = ab[:, :].rearrange("p (k t) -> p k t", t=2)
            sp4, sp4_r = rt("sp4", [128, 4, 1])
            eb4, eb4_r = rt("eb4", [128, 4, 1])
            op("act", lambda a: a.activation(sp4[:, :, :], abv[:, :, 0:1], AF.Exp, bias=prm[:, P_DTB:P_DTB + 1]), reads=[ab_r, prm_r], writes=[sp4_r])
            op("act", lambda a: a.activation(eb4[:, :, :], abv[:, :, 1:2], AF.Exp, scale=-1.0), reads=[ab_r], writes=[eb4_r])
            op("act", lambda a: a.activation(sp4[:, :, :], sp4[:, :, :], AF.Ln, bias=1.0), reads=[sp4_r], writes=[sp4_r])
            g4 = gtm[:, ti * 4:(ti + 1) * 4].rearrange("p (k o) -> p k o", o=1)
            b4 = btm[:, ti * 4:(ti + 1) * 4].rearrange("p (k o) -> p k o", o=1)
            op("dve", lambda v: v.tensor_scalar(g4, sp4[:, :, :], nexpA[:, 0:1], None, ALU.mult), reads=[sp4_r, nea_r, gtm_r], writes=[gtm_r])
            op("dve", lambda v: v.tensor_scalar(eb4[:, :, :], eb4[:, :, :], 1.0, None, ALU.add), reads=[eb4_r], writes=[eb4_r])
            op("dve", lambda v: v.reciprocal(b4, eb4[:, :, :]), reads=[eb4_r, btm_r], writes=[btm_r])

        _sc.__exit__(None, None, None); _sc = nc.named_scope('mix%d' % b); _sc.__enter__()
        fns = []
        if 'attn' in stages:
            fns.append(lambda: attn_batch(b))
        if 'gdn' in stages:
            gstate[b] = dict(wy_done=0, scan_done=0, blk={})
            fns.append(lambda: gdn_wy(b))
            fns.append(lambda: gdn_scan(b))
        if not fns:
            pass
        elif 'attn' not in stages:
            PAR.run([lambda: None] + fns)
        else:
            PAR.run(fns)
        _sc.__exit__(None, None, None)
    store(gcv_o[:, :, :, :], gcv_s[:, :, :, :], gcv_r, "gcv")
    if do_sample:
        with nc.named_scope('sample'):
            sample_part()
    C.finish()
    S0.close()
    return nc

_O_SB_Q, _O_SB_K, _O_SB_V = 0, 1024, 2048
_O_GDN_QKV = 3072
_O_GDN_Z = _O_GDN_QKV + 3072
_O_GDN_A = _O_GDN_Z + 1024
_O_GDN_B = _O_GDN_A + 8
_NC_CACHE = {}


def _fm(a, kc):
    return np.ascontiguousarray(a.T.reshape(kc, 128, a.shape[0]).transpose(1, 0, 2))


def kernel(x_prompt, x_sample, cache_sb_k, cache_sb_v, page_table, state_gdn_conv,
           state_gdn_rec, state_ffn_conv, p_prompt, p_sample, attn_norm, w_in,
           sb_logit_bias, sb_out_norm, gdn_conv_w, gdn_a_log, gdn_dt_bias, gdn_out_norm,
           w_out, ffn_norm, w_ffn_gate, w_ffn_up, ffn_conv_w, w_ffn_down, ple_norm,
           w_ple_gate, w_ple_proj, final_norm):
    f32 = np.float32
    A = lambda v: np.asarray(v)
    x_prompt, x_sample = A(x_prompt).astype(f32), A(x_sample).astype(f32)
    B, T, _ = x_prompt.shape
    NS = x_sample.shape[0]
    NP = B * T
    w_in0 = A(w_in)[0]
    ck, cv = A(cache_sb_k)[0], A(cache_sb_v)[0]
    NPOOL = ck.shape[0]
    sgc, sgr, sfc = A(state_gdn_conv)[0], A(state_gdn_rec)[0], A(state_ffn_conv)[0]
    xall = np.concatenate([x_prompt.reshape(NP, D), x_sample.reshape(NS, D)], 0)
    xT = _fm(xall, KC)
    cst = make_consts()
    ptabT = np.ascontiguousarray(A(page_table).T.astype(np.int32))
    an = A(attn_norm)[0]
    gcw = A(gdn_conv_w)[0]
    in_maps = []
    for c in range(8):
        hc = slice(c * 128, (c + 1) * 128)
        cols = [w_in0[:, _O_SB_Q:][:, hc], w_in0[:, _O_SB_K:][:, hc], w_in0[:, _O_SB_V:][:, hc],
                w_in0[:, _O_GDN_QKV:][:, hc], w_in0[:, _O_GDN_QKV + 1024:][:, hc], w_in0[:, _O_GDN_QKV + 2048:][:, hc],
                w_in0[:, _O_GDN_Z:][:, hc], w_in0[:, _O_GDN_A + c:_O_GDN_A + c + 1], w_in0[:, _O_GDN_B + c:_O_GDN_B + c + 1]]
        w1 = np.ascontiguousarray(np.concatenate(cols, 1)).astype(f32)
        prm = np.zeros((128, NPRM), f32)
        prm[:, P_CB] = A(sb_logit_bias)[0, c]
        prm[:, P_SBN] = A(sb_out_norm)[0]
        prm[:, P_ALOG] = A(gdn_a_log)[0, c]
        prm[:, P_DTB] = A(gdn_dt_bias)[0, c]
        for q in range(3):
            for j in range(4):
                prm[:, P_CW + q * 4 + j] = gcw[j, q * 1024 + c * 128:q * 1024 + (c + 1) * 128]
        prm[:, P_GNC] = A(gdn_out_norm)[0]
        prm[:, P_GNR:P_GNR + 128] = A(gdn_out_norm)[0][None, :]
        prm[:, P_AN:P_AN + KC] = an.reshape(KC, 128).T
        gh = np.stack([sgc[:, :, q * 1024 + c * 128:q * 1024 + (c + 1) * 128] for q in range(3)], 0)
        in_maps.append(dict(
            xT=xT, w1=w1, prm=prm, cst=cst,
            poolk=np.ascontiguousarray(ck[:, :, c, :].reshape(NPOOL, -1)),
            poolv=np.ascontiguousarray(cv[:, :, c, :].reshape(NPOOL, -1)),
            ptab=ptabT, ghist=np.ascontiguousarray(gh.transpose(3, 0, 2, 1)),
            grec=np.ascontiguousarray(sgr[:, c])))
    key1 = ("l1", T, B, NS, NPOOL)
    if key1 not in _NC_CACHE:
        _NC_CACHE[key1] = build_l1(T=T, NB=B, NS=NS, NPOOL=NPOOL, do_sample=True)
    res1 = run_bass_kernel_spmd(_NC_CACHE[key1], in_maps, core_ids=list(range(8))).results
    del in_maps

    sb_k_p = np.zeros((1, B, T, 8, 128), f32)
    sb_v_p = np.zeros((1, B, T, 8, 128), f32)
    sb_k_s = np.zeros((1, NS, 1, 8, 128), f32)
    sb_v_s = np.zeros((1, NS, 1, 8, 128), f32)
    gconv_p = np.zeros((1, B, 3, 3072), f32)
    gconv_s = np.zeros((1, NS, 3, 3072), f32)
    grec_p = np.zeros((1, B, 8, 128, 128), f32)
    grec_s = np.zeros((1, NS, 8, 128, 128), f32)
    mixT = np.zeros((D, NP + NS), f32)
    for c in range(8):
        r = res1[c]
        kT, vT = np.asarray(r["kT_o"]), np.asarray(r["vT_o"])
        sb_k_p[0, :, :, c, :] = kT[:, :NP].T.reshape(B, T, 128)
        sb_v_p[0, :, :, c, :] = vT[:, :NP].T.reshape(B, T, 128)
        sb_k_s[0, :, 0, c, :] = kT[:, NP:].T
        sb_v_s[0, :, 0, c, :] = vT[:, NP:].T
        gcvo = np.asarray(r["gcv_o"])
        gcso = np.asarray(r["gcs_o"])
        for q in range(3):
            gconv_p[0, :, :, q * 1024 + c * 128:q * 1024 + (c + 1) * 128] = gcvo[:, q].transpose(1, 2, 0)
            gconv_s[0, :, :, q * 1024 + c * 128:q * 1024 + (c + 1) * 128] = gcso[:, q].transpose(2, 1, 0)
        grec_p[0, :, c] = np.asarray(r["grp_o"])
        grec_s[0, :, c] = np.asarray(r["grs_o"])
        mixT[c * 128:(c + 1) * 128] = np.asarray(r["msb_o"])
        mixT[1024 + c * 128:1024 + (c + 1) * 128] = np.asarray(r["mgd_o"])

    NM = (B * T) // 8
    per_b = T // NM
    pall = np.concatenate([A(p_prompt)[0].reshape(NP, -1), A(p_sample)[0].reshape(NS, -1)], 0).astype(f32)
    xallT = np.ascontiguousarray(xall.T)
    pallT = np.ascontiguousarray(pall.T)
    nrm3 = np.ascontiguousarray(np.stack([A(ffn_norm)[0], A(ple_norm)[0], A(final_norm)]).reshape(3, KC, 128).transpose(2, 0, 1)).astype(f32)
    fcw = _fm(A(ffn_conv_w)[0].astype(f32), NJ)
    wts = l2_weight_layouts(A(w_out)[0], A(w_ffn_gate)[0], A(w_ffn_up)[0], A(w_ffn_down)[0], A(w_ple_gate)[0], A(w_ple_proj)[0])
    in_maps = []
    for j in range(8):
        b, s = j // per_b, j % per_b
        g0 = b * T + s * NM
        cols = np.concatenate([np.arange(g0 - 2, g0 + NM), [NP + j]])
        valid = np.ones(NM + 3, bool)
        if s == 0:
            valid[0:2] = False
            cols[0:2] = g0

        def take(MT, kc):
            t = MT[:, cols].copy()
            t[:, ~valid] = 0.0
            return np.ascontiguousarray(t.reshape(kc, 128, NM + 3).transpose(1, 0, 2))
        m = dict(xT=take(xallT, KC), mixT=take(mixT, KC), pT=take(pallT, 2), fhist=_fm(sfc[j].astype(f32), NJ), nrm=nrm3, fcw=fcw)
        m.update(wts)
        in_maps.append(m)
    key2 = ("l2", NM)
    if key2 not in _NC_CACHE:
        _NC_CACHE[key2] = build_l2(NM)
    res2 = run_bass_kernel_spmd(_NC_CACHE[key2], in_maps, core_ids=list(range(8))).results
    y_p = np.zeros((B, T, D), f32)
    y_s = np.zeros((NS, 1, D), f32)
    fconv_p = np.zeros((1, B, 2, DFF), f32)
    fconv_s = np.zeros((1, NS, 2, DFF), f32)
    for j in range(8):
        b, s = j // per_b, j % per_b
        r = res2[j]
        yk = np.asarray(r["yT"]).transpose(2, 1, 0).reshape(NM + 1, D)
        y_p[b, s * NM:(s + 1) * NM] = yk[:NM]
        y_s[j, 0] = yk[NM]
        fc = np.asarray(r["fcnew"]).transpose(2, 1, 0).reshape(4, DFF)
        if s == per_b - 1:
            fconv_p[0, b] = fc[0:2]
        fconv_s[0, j, 0] = fc[3]
        fconv_s[0, j, 1] = fc[2]
    return (y_p, y_s, sb_k_p, sb_v_p, gconv_p, grec_p, fconv_p, sb_k_s, sb_v_s, gconv_s, grec_s, fconv_s)
```

```python
from concourse.bass_utils import run_bass_kernel_spmd
import contextlib
import numpy as np
import concourse.bass as bass
import concourse.mybir as mybir

F32 = mybir.dt.float32
BF16 = mybir.dt.bfloat16
I32 = mybir.dt.int32
AF = mybir.ActivationFunctionType
ALU = mybir.AluOpType
AX = mybir.AxisListType


class Res:
    __slots__ = ("name", "w", "r", "dsem", "dval", "excl", "dq")

    def __init__(self, name, inherit=None):
        self.name = name
        self.w = {}
        self.r = dict(inherit) if inherit else {}
        self.dsem = None
        self.dval = 0
        self.excl = False
        self.dq = None


class Ctx:
    def __init__(self, nc):
        self.nc = nc
        self.eng = {}
        for name, h in (("pe", nc.tensor), ("act", nc.scalar), ("dve", nc.vector),
                        ("pool", nc.gpsimd), ("sp", nc.sync)):
            sem = nc.alloc_semaphore("sem_" + name)
            self.eng[name] = dict(h=h, sem=sem, cnt=0, seen={})
        self.freed = {}
        self.out_toks = {}
        self.nres = 0
        self.dsem_pool = {}
        self.par = None

    def res(self, name=None):
        self.nres += 1
        return Res(name or f"r{self.nres}", inherit=self.freed)

    def sb(self, stack, name, shape, dt):
        t = stack.enter_context(self.nc.sbuf_tensor("sb_" + name, list(shape), dt))
        return t, self.res(name)

    def ps(self, stack, name, shape, dt):
        t = stack.enter_context(self.nc.psum_tensor("pp_" + name, list(shape), dt))
        r = self.res(name)
        r.excl = True
        return t, r

    def free(self, *ress):
        for r in ress:
            for d in (r.w, r.r):
                for s, v in d.items():
                    if self.freed.get(s, 0) < v:
                        self.freed[s] = v
            if r.dsem is not None:
                self.dsem_pool.setdefault(r.dq, []).append((r.dsem, r.dval))
                r.dsem = None

    def _wait(self, e, toks):
        E = self.eng[e]
        for sem, val in toks.items():
            if E["seen"].get(sem, 0) < val:
                E["h"].wait_ge(sem, val)
                E["seen"][sem] = val

    @staticmethod
    def _merge(dst, src):
        for s, v in src.items():
            if dst.get(s, 0) < v:
                dst[s] = v

    def _deps(self, e, reads, writes, skip_sem=None):
        toks = {}
        own = self.eng[e]["sem"]
        for r in reads:
            self._merge(toks, r.w)
            if r.excl:
                self._merge(toks, {s_: v_ for s_, v_ in r.r.items() if s_ != own})
        for w in writes:
            self._merge(toks, w.w)
            self._merge(toks, w.r)
        if skip_sem is not None:
            toks.pop(skip_sem, None)
        return toks

    def op(self, e, fn, reads=(), writes=(), inc=True):
        E = self.eng[e]
        if self.par is not None:
            inc = True
        toks = self._deps(e, reads, writes, skip_sem=E["sem"] if e == "pe" else None)
        self._wait(e, toks)
        ins = fn(E["h"])
        if inc:
            E["cnt"] += 1
            ins.then_inc(E["sem"], 1)
            val = E["cnt"]
        else:
            val = E["cnt"] + 1
        for r in reads:
            if r.r.get(E["sem"], 0) < val:
                r.r[E["sem"]] = val
        for w in writes:
            w.w = {E["sem"]: val}
            w.r = {}
        if self.par is not None:
            self.par.switch()
        return ins

    def dma(self, q, out, in_, reads=(), writes=(), part=False, is_out=False, owner=None, **kw):
        E = self.eng[q]
        W = owner if owner is not None else writes[0]
        if W.dsem is None:
            W.dq = "pool" if q == "pool" else "hw"
            if self.dsem_pool.get(W.dq):
                W.dsem, W.dval = self.dsem_pool[W.dq].pop()
                self._wait(q, {W.dsem: W.dval})
            else:
                W.dsem = self.nc.alloc_semaphore("d_" + W.name)
                W.dval = 0
        toks = self._deps(q, reads, writes, skip_sem=W.dsem if part else None)
        self._wait(q, toks)
        ins = E["h"].dma_start(out=out, in_=in_, **kw)
        W.dval += 16
        ins.then_inc(W.dsem, 16)
        for r in reads:
            if r.r.get(W.dsem, 0) < W.dval:
                r.r[W.dsem] = W.dval
        for w in writes:
            if part:
                w.w[W.dsem] = W.dval
            else:
                w.w = {W.dsem: W.dval}
                w.r = {}
        if is_out:
            if self.out_toks.get(W.dsem, 0) < W.dval:
                self.out_toks[W.dsem] = W.dval
        return ins

    def indirect(self, out, in_, in_offset, reads=(), writes=(), part=False):
        E = self.eng["pool"]
        W = writes[0]
        if W.dsem is None:
            W.dq = "pool"
            if self.dsem_pool.get("pool"):
                W.dsem, W.dval = self.dsem_pool["pool"].pop()
                self._wait("pool", {W.dsem: W.dval})
            else:
                W.dsem = self.nc.alloc_semaphore("d_" + W.name)
                W.dval = 0
        toks = self._deps("pool", reads, writes, skip_sem=W.dsem if part else None)
        self._wait("pool", toks)
        ins = E["h"].indirect_dma_start(out=out, out_offset=None, in_=in_, in_offset=in_offset)
        W.dval += 16
        ins.then_inc(W.dsem, 16)
        for r in reads:
            if r.r.get(W.dsem, 0) < W.dval:
                r.r[W.dsem] = W.dval
        for w in writes:
            if part:
                w.w[W.dsem] = W.dval
            else:
                w.w = {W.dsem: W.dval}
                w.r = {}
        return ins

    def finish(self):
        toks = dict(self.out_toks)
        for name, E in self.eng.items():
            if name != "sp" and E["cnt"] > 0:
                toks[E["sem"]] = E["cnt"]
        self._wait("sp", toks)


class Par:
    def __init__(self, C):
        import threading
        self.C = C
        self.th = threading
        self.cv = threading.Condition()
        self.turn = 0
        self.alive = []
        self.ids = {}
        self.err = None
        self.weights = {}
        self.credit = {}

    def cur(self):
        return self.ids.get(self.th.get_ident(), None)

    def _next(self, i):
        n = len(self.alive)
        for k in range(1, n + 1):
            j = (i + k) % n
            if self.alive[j]:
                return j
        return -1

    def switch(self):
        i = self.cur()
        if i is None:
            return
        w = self.weights.get(i, 1)
        if w > 1:
            self.credit[i] = self.credit.get(i, 0) + 1
            if self.credit[i] % w != 0:
                return
        with self.cv:
            self.turn = self._next(i)
            self.cv.notify_all()
            while self.turn != i and self.err is None:
                self.cv.wait()
            if self.err is not None and self.turn != i:
                raise RuntimeError("peer failed")

    def idle(self):
        self.switch()

    def run(self, fns):
        self.alive = [True] * len(fns)
        self.turn = 0

        def body(i, fn):
            self.ids[self.th.get_ident()] = i
            try:
                with self.cv:
                    while self.turn != i and self.err is None:
                        self.cv.wait()
                if self.err is None:
                    fn()
            except BaseException as e:
                if self.err is None:
                    self.err = e
            finally:
                with self.cv:
                    self.alive[i] = False
                    self.turn = self._next(i)
                    self.cv.notify_all()
        ts = [self.th.Thread(target=body, args=(i, f)) for i, f in enumerate(fns)]
        self.C.par = self
        for t in ts:
            t.start()
        for t in ts:
            t.join()
        self.C.par = None
        if self.err is not None:
            raise self.err

D = 2048
KC = 16
DFF = 5504
NJ = 43
PLE = 256
EPS = 1e-6


def tiles_of(n, step=512):
    return [(s, min(step, n - s)) for s in range(0, n, step)]


def build_l2(NM=1024):
    NT = NM + 3
    TL = tiles_of(NT)
    nc = bass.Bass("TRN2", target_bir_lowering=False)
    dt_in = lambda name, shape, dt=F32: nc.dram_tensor(name, list(shape), dt, kind="ExternalInput")
    dt_out = lambda name, shape, dt=F32: nc.dram_tensor(name, list(shape), dt, kind="ExternalOutput")
    xT_d = dt_in("xT", [128, KC, NT])
    mixT_d = dt_in("mixT", [128, KC, NT])
    pT_d = dt_in("pT", [128, 2, NT])
    fh_d = dt_in("fhist", [128, NJ, 2])
    w_out_d = dt_in("w_out", [4, 128, KC, 512])
    w_gate_d = dt_in("w_gate", [22, 128, KC, 256])
    w_up_d = dt_in("w_up", [22, 128, KC, 256])
    w_down_d = dt_in("w_down", [KC, 128, NJ, 128])
    w_pg_d = dt_in("w_pg", [4, 128, KC, 512])
    w_pp_d = dt_in("w_pp", [128, 2, D])
    nrm_d = dt_in("nrm", [128, 3, KC])
    fcw_d = dt_in("fcw", [128, NJ, 3])
    yT_d = dt_out("yT", [128, KC, NM + 1])
    fc_d = dt_out("fcnew", [128, NJ, 4])
    hs_d = nc.dram_tensor("h_scratch", [128, KC, NT], F32)
    h2s_d = nc.dram_tensor("h2_scratch", [128, KC, NT], F32)

    C = Ctx(nc)
    hs_res = C.res("hs")
    with contextlib.ExitStack() as S0:
        S0.enter_context(nc.allow_low_precision("bf16 matmul operands, fp32 accumulation"))
        ones_bf, ones_r = C.sb(S0, "ones_bf", [128, 128], BF16)
        C.op("pool", lambda g: g.memset(ones_bf[:, :], 1.0), writes=[ones_r])
        nrm, nrm_r = C.sb(S0, "nrm", [128, 3, KC], F32)
        C.dma("sp", nrm[:, :, :], nrm_d[:, :, :], writes=[nrm_r])
        fcw, fcw_r = C.sb(S0, "fcw", [128, NJ, 3], F32)
        C.dma("sp", fcw[:, :, :], fcw_d[:, :, :], writes=[fcw_r])
        fh, fh_r = C.sb(S0, "fh", [128, NJ, 2], F32)
        C.dma("sp", fh[:, :, :], fh_d[:, :, :], writes=[fh_r])
        fcs, fcs_r = C.sb(S0, "fcs", [128, NJ, 4], F32)
        C.op("pool", lambda p: p.tensor_copy(fcs[:, :, 3], fh[:, :, 1]), reads=[fh_r], writes=[fcs_r])
        rstd, rstd_r = C.sb(S0, "rstd", [128, NT], F32)
        PS = [C.ps(S0, f"ps{i}", [128, 512], F32) for i in range(8)]
        psi = [0]

        def next_ps():
            t = PS[psi[0] % 8]
            psi[0] += 1
            return t

        def rms_stats(src, src_rs, sq_bufs):
            pss = [next_ps() for _ in TL]
            for kc in range(KC):
                sq, sq_r = sq_bufs[kc % len(sq_bufs)]
                C.op("act", lambda a: a.activation(sq[:, :], src[:, kc, :], AF.Square),
                     reads=[src_rs[kc]], writes=[sq_r])
                for ti, (t0, tn) in enumerate(TL):
                    pt, pr = pss[ti]
                    C.op("pe", lambda pe: pe.matmul(pt[:, 0:tn], ones_bf[:, :], sq[:, t0:t0 + tn],
                                                    start=(kc == 0), stop=(kc == KC - 1)),
                         reads=[ones_r, sq_r], writes=[pr], inc=(kc == KC - 1) or True)
            for ti, (t0, tn) in enumerate(TL):
                pt, pr = pss[ti]
                C.op("dve", lambda v: v.tensor_scalar(rstd[:, t0:t0 + tn], pt[:, 0:tn], 1.0 / D, EPS,
                                                      ALU.mult, ALU.add), reads=[pr], writes=[rstd_r])
            C.op("act", lambda a: a.activation(rstd[:, :], rstd[:, :], AF.Sqrt), reads=[rstd_r], writes=[rstd_r])
            C.op("dve", lambda v: v.reciprocal(rstd[:, :], rstd[:, :]), reads=[rstd_r], writes=[rstd_r])


        def make_loader(stack, name, src_d, wb):
            stg = [C.sb(stack, f"{name}_stg{i}", [128, 2, 512], F32) for i in range(4)]
            rs = [[C.res(f"{name}_{i}_{k}") for k in range(8)] for i in range(2)]
            cnt = [0]

            def load(g):
                wt, _ = wb[g % 2]
                for k in range(8):
                    st_, st_r = stg[cnt[0] % 4]
                    C.dma("sp", st_[:, :, :], src_d[g, :, 2 * k:2 * k + 2, :], writes=[st_r])
                    if cnt[0] % 2 == 0:
                        C.op("act", lambda a: a.copy(wt[:, 2 * k:2 * k + 2, :], st_[:, :, :]), reads=[st_r], writes=[rs[g % 2][k]])
                    else:
                        C.op("dve", lambda v: v.tensor_copy(wt[:, 2 * k:2 * k + 2, :], st_[:, :, :]), reads=[st_r], writes=[rs[g % 2][k]])
                    cnt[0] += 1
            return load, rs, stg

        _sc = nc.named_scope('st1'); _sc.__enter__()
        Sf = contextlib.ExitStack()
        fT, _ = C.sb(Sf, "fT", [128, KC, NT], BF16)
        fT_rs = [C.res(f"fT{k}") for k in range(KC)]
        S1 = contextlib.ExitStack()
        hT, _ = C.sb(S1, "hT", [128, KC, NT], F32)
        hT_rs = [C.res(f"hT{k}") for k in range(KC)]
        for kc in range(KC):
            C.dma("sp", hT[:, kc, :], xT_d[:, kc, :], writes=[hT_rs[kc]])
        Sm = contextlib.ExitStack()
        mixT, _ = C.sb(Sm, "mixT", [128, KC, NT], BF16)
        mix_rs = [C.res(f"mix{k}") for k in range(KC)]
        for kc in range(KC):
            C.dma("pool", mixT[:, kc, :], mixT_d[:, kc, :], writes=[mix_rs[kc]])
        wbufs = [C.sb(Sm, f"wo{i}", [128, KC, 512], BF16) for i in range(2)]
        pass
        ld1, wo_rs, wo_stg = make_loader(Sm, "wo", w_out_d, wbufs)
        ld1(0)
        for g in range(4):
            wt, wr = wbufs[g % 2]
            if g + 1 < 4:
                ld1(g + 1)
            for o in range(4):
                oc = g * 4 + o
                for (t0, tn) in TL:
                    pt, pr = next_ps()
                    for kc in range(KC):
                        C.op("pe", lambda pe: pe.matmul(pt[:, 0:tn], wt[:, kc, o * 128:(o + 1) * 128],
                                                        mixT[:, kc, t0:t0 + tn], start=(kc == 0), stop=(kc == KC - 1)),
                             reads=[wo_rs[g % 2][kc // 2], mix_rs[kc]], writes=[pr], inc=(kc == KC - 1))
                    C.op("dve", lambda v: v.tensor_tensor(hT[:, oc, t0:t0 + tn], hT[:, oc, t0:t0 + tn], pt[:, 0:tn], ALU.add),
                         reads=[pr, hT_rs[oc]], writes=[hT_rs[oc]])
        C.free(*mix_rs, *[r for _, r in wbufs + wo_stg], *[r for l in wo_rs for r in l])
        Sm.close()

        _sc.__exit__(None, None, None); _sc = nc.named_scope('st2'); _sc.__enter__()
        Sq = contextlib.ExitStack()
        sq_bufs = [C.sb(Sq, f"sq{i}", [128, NT], BF16) for i in range(2)]
        rms_stats(hT, hT_rs, sq_bufs)
        for kc in range(KC):
            C.op("dve", lambda v: v.scalar_tensor_tensor(fT[:, kc, :], hT[:, kc, :], nrm[:, 0, kc:kc + 1], rstd[:, :],
                                                         ALU.mult, ALU.mult),
                 reads=[hT_rs[kc], nrm_r, rstd_r], writes=[fT_rs[kc]])
            C.dma("sp", hs_d[:, kc, :], hT[:, kc, :], reads=[hT_rs[kc]], writes=[hs_res], part=True)
        C.free(*hT_rs, *[r for _, r in sq_bufs])
        Sq.close()
        S1.close()

        _sc.__exit__(None, None, None); _sc = nc.named_scope('st3'); _sc.__enter__()
        Sa = contextlib.ExitStack()
        actT, _ = C.sb(Sa, "actT", [128, NJ, NT], BF16)
        act_rs = [C.res(f"act{j}") for j in range(NJ)]
        Sg = contextlib.ExitStack()
        GW = 256
        wg_b = [C.sb(Sg, f"wg{i}", [128, KC, GW], BF16) for i in range(2)]
        wu_b = [C.sb(Sg, f"wu{i}", [128, KC, GW], BF16) for i in range(2)]
        gp_b = [C.sb(Sg, f"gp{i}", [128, NT], F32) for i in range(2)]
        cv_b = [C.sb(Sg, f"cv{i}", [128, NT], F32) for i in range(2)]
        sl_b = [C.sb(Sg, f"sl{i}", [128, NT], F32) for i in range(2)]
        ub_b = [C.sb(Sg, f"ub{i}", [128, NT], BF16) for i in range(2)]
        pass
        pass
        ngrp = (DFF + GW - 1) // GW
        pend = [None]
        stg_b = [C.sb(Sg, f"stg{i}", [128, 4, GW], F32) for i in range(4)]
        wg_rs = [[C.res(f"wg{i}_{k}") for k in range(4)] for i in range(2)]
        wu_rs = [[C.res(f"wu{i}_{k}") for k in range(4)] for i in range(2)]
        stgi = [0]

        def load_group(g):
            for (wb, rs, src) in ((wg_b, wg_rs, w_gate_d), (wu_b, wu_rs, w_up_d)):
                wt, _ = wb[g % 2]
                for k in range(4):
                    stg, stg_r = stg_b[stgi[0] % 4]
                    C.dma("sp", stg[:, :, :], src[g, :, 4 * k:4 * k + 4, :], writes=[stg_r])
                    if stgi[0] % 2 == 0:
                        C.op("act", lambda a: a.copy(wt[:, 4 * k:4 * k + 4, :], stg[:, :, :]), reads=[stg_r], writes=[rs[g % 2][k]])
                    else:
                        C.op("dve", lambda v: v.tensor_copy(wt[:, 4 * k:4 * k + 4, :], stg[:, :, :]), reads=[stg_r], writes=[rs[g % 2][k]])
                    stgi[0] += 1

        load_group(0)
        for g in range(ngrp):
            c0 = g * GW
            cw = min(GW, DFF - c0)
            wg, _ = wg_b[g % 2]
            wu, _ = wu_b[g % 2]
            if g + 1 < ngrp:
                load_group(g + 1)
            for o in range(cw // 128):
                j = (c0 // 128) + o
                gp, gp_r = gp_b[j % 2]
                cv, cv_r = cv_b[j % 2]
                sl, sl_r = sl_b[j % 2]
                pus = []
                for (t0, tn) in TL:
                    pg, pg_r = next_ps()
                    for kc in range(KC):
                        C.op("pe", lambda pe: pe.matmul(pg[:, 0:tn], wg[:, kc, o * 128:(o + 1) * 128],
                                                        fT[:, kc, t0:t0 + tn], start=(kc == 0), stop=(kc == KC - 1)),
                             reads=[wg_rs[g % 2][kc // 4], fT_rs[kc]], writes=[pg_r], inc=(kc == KC - 1))
                    C.op("act", lambda a: a.copy(gp[:, t0:t0 + tn], pg[:, 0:tn]), reads=[pg_r], writes=[gp_r])
                    pu, pu_r = next_ps()
                    for kc in range(KC):
                        C.op("pe", lambda pe: pe.matmul(pu[:, 0:tn], wu[:, kc, o * 128:(o + 1) * 128],
                                                        fT[:, kc, t0:t0 + tn], start=(kc == 0), stop=(kc == KC - 1)),
                             reads=[wu_rs[g % 2][kc // 4], fT_rs[kc]], writes=[pu_r], inc=(kc == KC - 1))
                    ub, ub_r = ub_b[j % 2]
                    C.op("act", lambda a: a.copy(ub[:, t0:t0 + tn], pu[:, 0:tn]), reads=[pu_r], writes=[ub_r])
                    pus.append((ub, ub_r, t0, tn))
                C.op("act", lambda a: a.activation(cv[:, 2:NM + 2], gp[:, 0:NM], AF.Copy, scale=fcw[:, j, 0:1]),
                     reads=[gp_r, fcw_r], writes=[cv_r])
                C.op("dve", lambda p: p.scalar_tensor_tensor(cv[:, 2:NM + 2], gp[:, 1:NM + 1], fcw[:, j, 1:2], cv[:, 2:NM + 2],
                                                              ALU.mult, ALU.add), reads=[gp_r, fcw_r, cv_r], writes=[cv_r])
                C.op("dve", lambda v: v.scalar_tensor_tensor(cv[:, 2:NM + 2], gp[:, 2:NM + 2], fcw[:, j, 2:3], cv[:, 2:NM + 2],
                                                             ALU.mult, ALU.add), reads=[gp_r, fcw_r, cv_r], writes=[cv_r])
                C.op("pool", lambda p: p.memset(cv[:, 0:2], 0.0), writes=[cv_r], reads=[cv_r])
                sc = NM + 2
                C.op("dve", lambda v: v.tensor_scalar(cv[:, sc:sc + 1], fh[:, j, 0:1], fcw[:, j, 0:1], None, ALU.mult),
                     reads=[fh_r, fcw_r, cv_r], writes=[cv_r])
                C.op("dve", lambda v: v.scalar_tensor_tensor(cv[:, sc:sc + 1], fh[:, j, 1:2], fcw[:, j, 1:2], cv[:, sc:sc + 1],
                                                             ALU.mult, ALU.add), reads=[fh_r, fcw_r, cv_r], writes=[cv_r])
                C.op("dve", lambda v: v.scalar_tensor_tensor(cv[:, sc:sc + 1], gp[:, sc:sc + 1], fcw[:, j, 2:3], cv[:, sc:sc + 1],
                                                             ALU.mult, ALU.add), reads=[gp_r, fcw_r, cv_r], writes=[cv_r])
                C.op("pool", lambda p: p.tensor_copy(fcs[:, j, 0:3], gp[:, NM:NM + 3]), reads=[gp_r, fcs_r], writes=[fcs_r])
                def tail(j=j, sl=sl, sl_r=sl_r, cv=cv, cv_r=cv_r, ub=ub, ub_r=ub_r):
                    C.op("act", lambda a: a.activation(sl[:, :], cv[:, :], AF.Silu), reads=[cv_r], writes=[sl_r])
                    C.op("dve", lambda v: v.tensor_tensor(actT[:, j, :], sl[:, :], ub[:, :], ALU.mult),
                         reads=[sl_r, ub_r], writes=[act_rs[j]])
                if pend[0] is not None:
                    pend[0]()
                pend[0] = tail
        pend[0]()
        C.dma("sp", fc_d[:, :, :], fcs[:, :, :], reads=[fcs_r], writes=[C.res("fc_out")], is_out=True)
        C.free(*[r for _, r in wg_b + wu_b + gp_b + cv_b + sl_b + ub_b + stg_b], *[r for l in wg_rs + wu_rs for r in l])
        Sg.close()

        _sc.__exit__(None, None, None); _sc = nc.named_scope('st4'); _sc.__enter__()
        Sd = contextlib.ExitStack()
        wd_b = [C.sb(Sd, f"wd{i}", [128, NJ, 128], BF16) for i in range(2)]
        hc_b = [C.sb(Sd, f"hc{i}", [128, NT], F32) for i in range(2)]
        h2s_res = C.res("h2s")
        pass
        JS = [(0, 11), (11, 11), (22, 11), (33, 10)]
        stg4 = [C.sb(Sd, f"stgd{i}", [128, 11, 128], F32) for i in range(4)]
        wd_rs = [[C.res(f"wd{i}_{k}") for k in range(4)] for i in range(2)]
        s4i = [0]

        def load_wd(oc):
            wd, _ = wd_b[oc % 2]
            for k, (j0, jn) in enumerate(JS):
                stg, stg_r = stg4[s4i[0] % 4]
                C.dma("sp", stg[:, 0:jn, :], w_down_d[oc, :, j0:j0 + jn, :], writes=[stg_r])
                if s4i[0] % 2 == 0:
                    C.op("act", lambda a: a.copy(wd[:, j0:j0 + jn, :], stg[:, 0:jn, :]), reads=[stg_r], writes=[wd_rs[oc % 2][k]])
                else:
                    C.op("dve", lambda v: v.tensor_copy(wd[:, j0:j0 + jn, :], stg[:, 0:jn, :]), reads=[stg_r], writes=[wd_rs[oc % 2][k]])
                s4i[0] += 1

        load_wd(0)
        for oc in range(KC):
            wd, _ = wd_b[oc % 2]
            hc, hc_r = hc_b[oc % 2]
            if oc + 1 < KC:
                load_wd(oc + 1)
            C.dma("sp", hc[:, :], hs_d[:, oc, :], reads=[hs_res], writes=[hc_r])
            for (t0, tn) in TL:
                pt, pr = next_ps()
                for j in range(NJ):
                    C.op("pe", lambda pe: pe.matmul(pt[:, 0:tn], wd[:, j, :], actT[:, j, t0:t0 + tn],
                                                    start=(j == 0), stop=(j == NJ - 1)),
                         reads=[wd_rs[oc % 2][min(j // 11, 3)], act_rs[j]], writes=[pr], inc=(j == NJ - 1))
                C.op("dve", lambda v: v.tensor_tensor(hc[:, t0:t0 + tn], hc[:, t0:t0 + tn], pt[:, 0:tn], ALU.add),
                     reads=[pr, hc_r], writes=[hc_r])
            C.dma("sp", h2s_d[:, oc, :], hc[:, :], reads=[hc_r], writes=[h2s_res], part=True, owner=hc_r)
        C.free(*[r for _, r in wd_b + hc_b + stg4], *[r for l in wd_rs for r in l], *act_rs, *fT_rs)
        Sd.close()
        Sa.close()
        Sf.close()
        Sh = contextlib.ExitStack()
        h2T, _ = C.sb(Sh, "h2T", [128, KC, NT], F32)
        h2_rs = [C.res(f"h2{k}") for k in range(KC)]
        for kc in range(KC):
            C.dma("sp", h2T[:, kc, :], h2s_d[:, kc, :], reads=[h2s_res], writes=[h2_rs[kc]])

        _sc.__exit__(None, None, None); _sc = nc.named_scope('st5'); _sc.__enter__()
        Sp = contextlib.ExitStack()
        gT, _ = C.sb(Sp, "gT", [128, KC, NT], BF16)
        gT_rs = [C.res(f"gT{k}") for k in range(KC)]
        sq_bufs = [C.sb(Sp, f"sqb{i}", [128, NT], BF16) for i in range(2)]
        rms_stats(h2T, h2_rs, sq_bufs)
        for kc in range(KC):
            C.op("dve", lambda v: v.scalar_tensor_tensor(gT[:, kc, :], h2T[:, kc, :], nrm[:, 1, kc:kc + 1], rstd[:, :],
                                                         ALU.mult, ALU.mult),
                 reads=[h2_rs[kc], nrm_r, rstd_r], writes=[gT_rs[kc]])
        pT, pT_r = C.sb(Sp, "pT", [128, 2, NT], BF16)
        C.dma("pool", pT[:, :, :], pT_d[:, :, :], writes=[pT_r])
        wpp, wpp_r = C.sb(Sp, "wpp", [128, 2, D], BF16)
        C.dma("pool", wpp[:, :, :], w_pp_d[:, :, :], writes=[wpp_r])
        wpg_b = [C.sb(Sp, f"wpg{i}", [128, KC, 512], BF16) for i in range(2)]
        sg_b = [C.sb(Sp, f"sg{i}", [128, 512], F32) for i in range(2)]
        pass
        cnt = 0
        ld5, wp_rs, wp_stg = make_loader(Sp, "wpg", w_pg_d, wpg_b)
        ld5(0)
        for g in range(4):
            wt, wr = wpg_b[g % 2]
            if g + 1 < 4:
                ld5(g + 1)
            for o in range(4):
                oc = g * 4 + o
                for (t0, tn) in TL:
                    pt, pr = next_ps()
                    for kc in range(KC):
                        C.op("pe", lambda pe: pe.matmul(pt[:, 0:tn], wt[:, kc, o * 128:(o + 1) * 128],
                                                        gT[:, kc, t0:t0 + tn], start=(kc == 0), stop=(kc == KC - 1)),
                             reads=[wp_rs[g % 2][kc // 2], gT_rs[kc]], writes=[pr], inc=(kc == KC - 1))
                    sg, sg_r = sg_b[cnt % 2]
                    cnt += 1
                    C.op("act", lambda a: a.activation(sg[:, 0:tn], pt[:, 0:tn], AF.Sigmoid), reads=[pr], writes=[sg_r])
                    p2, p2r = next_ps()
                    for kc in range(2):
                        C.op("pe", lambda pe: pe.matmul(p2[:, 0:tn], wpp[:, kc, oc * 128:(oc + 1) * 128],
                                                        pT[:, kc, t0:t0 + tn], start=(kc == 0), stop=(kc == 1)),
                             reads=[wpp_r, pT_r], writes=[p2r], inc=(kc == 1))
                    C.op("dve", lambda v: v.tensor_tensor(sg[:, 0:tn], sg[:, 0:tn], p2[:, 0:tn], ALU.mult),
                         reads=[sg_r, p2r], writes=[sg_r])
                    C.op("dve", lambda p: p.tensor_tensor(h2T[:, oc, t0:t0 + tn], h2T[:, oc, t0:t0 + tn], sg[:, 0:tn], ALU.add),
                         reads=[sg_r, h2_rs[oc]], writes=[h2_rs[oc]])
        _sc.__exit__(None, None, None); _sc = nc.named_scope('st6'); _sc.__enter__()
        rms_stats(h2T, h2_rs, sq_bufs)
        yb = [C.sb(Sp, f"yb{i}", [128, NM + 1], F32) for i in range(2)]
        y_res = C.res("y_out")
        for kc in range(KC):
            yt, yr = yb[kc % 2]
            C.op("dve", lambda v: v.scalar_tensor_tensor(yt[:, :], h2T[:, kc, 2:NT], nrm[:, 2, kc:kc + 1], rstd[:, 2:NT],
                                                         ALU.mult, ALU.mult),
                 reads=[h2_rs[kc], nrm_r, rstd_r], writes=[yr])
            C.dma("sp", yT_d[:, kc, :], yt[:, :], reads=[yr], writes=[y_res], part=True, is_out=True, owner=yr)
        _sc.__exit__(None, None, None)
        C.finish()
        Sp.close()
        Sh.close()
    return nc


def l2_weight_layouts(w_out, w_gate, w_up, w_down, w_pg, w_pp):
    f32 = np.float32

    def grp(w, gw):
        n = w.shape[1]
        ng = (n + gw - 1) // gw
        wp = np.zeros((w.shape[0], ng * gw), f32)
        wp[:, :n] = w
        return np.ascontiguousarray(wp.reshape(KC, 128, ng, gw).transpose(2, 1, 0, 3))
    return dict(w_out=grp(w_out, 512), w_gate=grp(w_gate, 256), w_up=grp(w_up, 256),
                w_down=np.ascontiguousarray(w_down.astype(f32).reshape(NJ, 128, KC, 128).transpose(2, 1, 0, 3)),
                w_pg=grp(w_pg, 512),
                w_pp=np.ascontiguousarray(w_pp.astype(f32).reshape(2, 128, D).transpose(1, 0, 2)))


HD = 128
SB_SCALE = HD ** -0.5
NW = 898
C_ID, C_TRIL, C_MST, C_UBD, C_BONES, C_MPOS, C_MNEG, C_SEL0, C_SEL1, C_SU = range(10)
NCST = 10 * 128
P_CB, P_SBN, P_ALOG, P_DTB, P_CW, P_GNC, P_GNR, P_AN = 0, 1, 2, 3, 4, 16, 17, 17 + 128
NPRM = P_AN + KC


def make_consts():
    c = np.zeros((128, 10, 128), np.float32)
    i = np.arange(128)
    same = (i[:, None] // 64) == (i[None, :] // 64)
    c[:, C_ID] = np.eye(128)
    c[:, C_TRIL] = (i[:, None] >= i[None, :])
    c[:, C_MST] = (i[None, :] > i[:, None])
    c[:, C_UBD] = (i[:, None] <= i[None, :]) & same
    c[:, C_BONES] = same
    c[:, C_MPOS] = np.where((i[:, None] > i[None, :]) & same, 0.0, 30000.0)
    c[:, C_MNEG] = np.where((i[None, :] >= i[:, None]) & same, 0.0, -30000.0)
    c[:, C_SEL0] = (i[:, None] < 64) * np.ones((1, 128))
    c[:, C_SEL1] = (i[:, None] >= 64) * np.ones((1, 128))
    c[:, C_SU] = (i[:, None] > i[None, :])
    return c.reshape(128, NCST)


def build_l1(T=4096, NB=2, NS=8, NPOOL=1280, do_sample=True, stages=('attn', 'gdn'), par=True):
    NTOK = NB * T + NS
    NBLK = T // 128
    NTILE = T // 512
    nc = bass.Bass("TRN2", target_bir_lowering=False)
    din = lambda name, shape, dt=F32: nc.dram_tensor(name, list(shape), dt, kind="ExternalInput")
    dout = lambda name, shape, dt=F32: nc.dram_tensor(name, list(shape), dt, kind="ExternalOutput")
    xT_d = din("xT", [128, KC, NTOK])
    w1_d = din("w1", [D, NW])
    prm_d = din("prm", [128, NPRM])
    cst_d = din("cst", [128, NCST])
    poolk_d = din("poolk", [NPOOL, 128 * HD])
    poolv_d = din("poolv", [NPOOL, 128 * HD])
    ptab_d = din("ptab", [128, NS], I32)
    ghist_d = din("ghist", [128, 3, 3, NS])
    grec_d = din("grec", [NS, 128, 128])
    kT_o = dout("kT_o", [128, NTOK])
    vT_o = dout("vT_o", [128, NTOK])
    msb_o = dout("msb_o", [128, NTOK])
    mgd_o = dout("mgd_o", [128, NTOK])
    gcv_o = dout("gcv_o", [128, 3, NB, 3])
    gcs_o = dout("gcs_o", [128, 3, 3, NS])
    grp_o = dout("grp_o", [NB, 128, 128])
    grs_o = dout("grs_o", [NS, 128, 128])

    C = Ctx(nc)
    S0 = contextlib.ExitStack()
    S0.enter_context(nc.allow_low_precision("bf16 matmul operands, fp32 accumulation"))
    sbt = lambda name, shape, dt=F32: C.sb(S0, name, shape, dt)
    op = C.op

    cst, cst_r = sbt("cst", [128, NCST])
    C.dma("sp", cst[:, :], cst_d[:, :], writes=[cst_r])
    prm, prm_r = sbt("prm", [128, NPRM])
    C.dma("sp", prm[:, :], prm_d[:, :], writes=[prm_r])
    cb = lambda k: cst[:, k * 128:(k + 1) * 128]
    ident_f = cb(C_ID)
    w1, w1_r = sbt("w1", [128, KC, NW], BF16)
    C.dma("pool", w1[:, :, :], w1_d.ap().rearrange("(kc p) n -> p kc n", p=128), writes=[w1_r])
    ident_b, idb_r = sbt("ident_b", [128, 128], BF16)
    op("dve", lambda v: v.tensor_copy(ident_b[:, :], ident_f), reads=[cst_r], writes=[idb_r])
    tril_b, trb_r = sbt("tril_b", [128, 128], BF16)
    op("dve", lambda v: v.tensor_copy(tril_b[:, :], cb(C_TRIL)), reads=[cst_r], writes=[trb_r])
    ones_b, onb_r = sbt("ones_b", [128, 128], BF16)
    op("pool", lambda g: g.memset(ones_b[:, :], 1.0), writes=[onb_r])
    ones_f, onf_r = sbt("ones_f", [128, 128])
    op("pool", lambda g: g.memset(ones_f[:, :], 1.0), writes=[onf_r])
    zeros_b, zb_r = sbt("zeros_b", [128, 512], BF16)
    op("pool", lambda g: g.memset(zeros_b[:, :], 0.0), writes=[zb_r])
    nexpA, nea_r = sbt("nexpA", [128, 1])
    op("act", lambda a: a.activation(nexpA[:, :], prm[:, P_ALOG:P_ALOG + 1], AF.Exp), reads=[prm_r], writes=[nea_r])
    op("dve", lambda v: v.tensor_scalar(nexpA[:, :], nexpA[:, :], -1.0, None, ALU.mult), reads=[nea_r], writes=[nea_r])

    PS = [C.ps(S0, f"ps{i}", [128, 512], F32) for i in range(6)]
    PSB = [C.ps(S0, f"psb{i}", [128, 1024], BF16) for i in range(1)]
    PSG = [C.ps(S0, f"psg{i}", [128, 512], F32) for i in range(1)] + PS[1:3]
    psbi = [0]
    psi = [0]

    def next_psb():
        t = PSB[0]
        psbi[0] += 1
        return t

    PAR = Par(C)
    import os as _os
    PAR.weights = {0: int(_os.environ.get('ATTW', '1')), 1: int(_os.environ.get('WYW', '2')), 2: int(_os.environ.get('SCW', '1'))}
    ps_cnt = {}

    def next_ps():
        who = PAR.cur() if C.par is not None else None
        if who == 0:
            lst = PS[3:5]
        elif who == 1:
            lst = PS[1:3]
        elif who == 2:
            lst = [PSG[0], PS[5]]
        else:
            lst = PS[1:6]
        k = ps_cnt.get(who, 0)
        ps_cnt[who] = k + 1
        return lst[k % len(lst)]

    def next_acc():
        return PS[0]

    rot = {}

    def rt(name, shape, dt=F32, n=2):
        if name not in rot:
            rot[name] = [[sbt(f"{name}_{i}", shape, dt) for i in range(n)], 0]
        lst = rot[name]
        t = lst[0][lst[1] % n]
        lst[1] += 1
        return t

    def rsqrt_to(out_ap, out_r, src_ap, src_rs, scale, n):
        tmp, tmp_r = rt(f"rsq{n}", [128, n], F32, 1)
        w = out_ap.shape[-1]
        op("dve", lambda v: v.tensor_scalar(tmp[:, 0:w], src_ap, scale, EPS, ALU.mult, ALU.add), reads=src_rs, writes=[tmp_r])
        op("act", lambda a: a.activation(tmp[:, 0:w], tmp[:, 0:w], AF.Ln), reads=[tmp_r], writes=[tmp_r])
        op("act", lambda a: a.activation(out_ap, tmp[:, 0:w], AF.Exp, scale=-0.5), reads=[tmp_r], writes=[out_r])

    Sm = S0
    BIGW = max(8 * T, 2 * 128 * HD)
    BIG, _ = C.sb(Sm, "BIG", [128, BIGW], BF16)
    bigi = [0]

    def big(name):
        i = bigi[0]
        bigi[0] += 1
        return BIG[:, i * T:(i + 1) * T], C.res(name)
    QT, QT_r = big("QT")
    KT, KT_r = big("KT")
    VTMf, VTM_r = big("VTM")
    VTM = VTMf.rearrange("p (b d) -> p b d", d=128)
    GQ, GQ_r = big("GQ")
    GK, GK_r = big("GK")
    GV, GV_r = big("GV")
    ZS, ZS_r = big("ZS")
    gtm, gtm_r = C.sb(Sm, "gtm", [128, NBLK], F32)
    btm, btm_r = C.sb(Sm, "btm", [128, NBLK], F32)
    xt, _ = C.sb(Sm, "xt", [128, KC, 256], F32)
    xt_rs = [C.res(f"xt{k}") for k in range(KC)]
    xn, _ = C.sb(Sm, "xn", [128, KC, 512], BF16)
    xn_rs = [C.res(f"xn{k}") for k in range(KC)]
    cin = [C.sb(Sm, f"cin{i}", [128, 3 + 512], F32) for i in range(3)]
    gcv_s, gcv_r = sbt("gcv_s", [128, 3, NB, 3])

    def proj_tile(col0, tn, dst):
        for h0 in range(0, tn, 256):
            hn = min(256, tn - h0)
            for kc in range(KC):
                C.dma("sp", xt[:, kc, 0:hn], xT_d[:, kc, col0 + h0:col0 + h0 + hn], writes=[xt_rs[kc]])
            pss, pss_r = next_ps()
            for kc in range(KC):
                sq, sq_r = rt("sq", [128, 512], BF16)
                op("act", lambda a: a.activation(sq[:, 0:hn], xt[:, kc, 0:hn], AF.Square), reads=[xt_rs[kc]], writes=[sq_r])
                op("pe", lambda pe: pe.matmul(pss[:, 0:hn], ones_b[:, :], sq[:, 0:hn], start=(kc == 0), stop=(kc == KC - 1)),
                   reads=[onb_r, sq_r], writes=[pss_r], inc=True)
            rstd, rstd_r = rt("rstd", [128, 512], F32, 1)
            rsqrt_to(rstd[:, 0:hn], rstd_r, pss[:, 0:hn], [pss_r], 1.0 / D, 512)
            for kc in range(KC):
                op("dve", lambda v: v.scalar_tensor_tensor(xn[:, kc, h0:h0 + hn], xt[:, kc, 0:hn], prm[:, P_AN + kc:P_AN + kc + 1],
                                                           rstd[:, 0:hn], ALU.mult, ALU.mult),
                   reads=[xt_rs[kc], prm_r, rstd_r], writes=[xn_rs[kc]])
        for cc in range(7):
            pt, pr = next_ps()
            for kc in range(KC):
                op("pe", lambda pe: pe.matmul(pt[:, 0:tn], w1[:, kc, cc * 128:(cc + 1) * 128], xn[:, kc, 0:tn],
                                              start=(kc == 0), stop=(kc == KC - 1)),
                   reads=[w1_r, xn_rs[kc]], writes=[pr], inc=(kc == KC - 1))
            dst(cc, pt, pr)

    def l2n_to(out_ap, out_r, y, y_r, tn, scale):
        sq, sq_r = rt("sq", [128, 512], BF16)
        op("act", lambda a: a.activation(sq[:, 0:tn], y[:, 0:tn], AF.Square), reads=[y_r], writes=[sq_r])
        pt, pr = next_ps()
        op("pe", lambda pe: pe.matmul(pt[:, 0:tn], ones_b[:, :], sq[:, 0:tn], start=True, stop=True),
           reads=[onb_r, sq_r], writes=[pr])
        rs, rs_r = rt("rstd", [128, 512], F32, 1)
        rsqrt_to(rs[:, 0:tn], rs_r, pt[:, 0:tn], [pr], 1.0, 512)
        op("dve", lambda v: v.scalar_tensor_tensor(out_ap, y[:, 0:tn], scale, rs[:, 0:tn], ALU.mult, ALU.mult),
           reads=[y_r, rs_r], writes=[out_r])

    def softplus_to(out_ap, out_r, src_ap, src_rs, bias_ap, scale, shape):
        tmp, tmp_r = rt("spl%d_%d" % tuple(shape), shape)
        kw = dict(scale=scale)
        if bias_ap is not None:
            kw["bias"] = bias_ap
        op("act", lambda a: a.activation(tmp[:, :], src_ap, AF.Exp, **kw), reads=src_rs + [prm_r], writes=[tmp_r])
        op("act", lambda a: a.activation(out_ap, tmp[:, :], AF.Ln, bias=1.0), reads=[tmp_r], writes=[out_r])

    def headnorm_out(po, po_r, tn, out_d, col0, wcol):
        osb, osb_r = rt("osb", [128, 512], F32, 1)
        op("act", lambda a: a.copy(osb[:, 0:tn], po[:, 0:tn]), reads=[po_r], writes=[osb_r])
        sq, sq_r = rt("sq", [128, 512], BF16)
        op("act", lambda a: a.activation(sq[:, 0:tn], osb[:, 0:tn], AF.Square), reads=[osb_r], writes=[sq_r])
        pt, pr = next_ps()
        op("pe", lambda pe: pe.matmul(pt[:, 0:tn], ones_b[:, :], sq[:, 0:tn], start=True, stop=True),
           reads=[onb_r, sq_r], writes=[pr])
        rs, rs_r = rt("rstd", [128, 512], F32, 1)
        rsqrt_to(rs[:, 0:tn], rs_r, pt[:, 0:tn], [pr], 1.0 / HD, 512)
        op("dve", lambda v: v.scalar_tensor_tensor(osb[:, 0:tn], osb[:, 0:tn], prm[:, wcol:wcol + 1], rs[:, 0:tn],
                                                   ALU.mult, ALU.mult), reads=[osb_r, prm_r, rs_r], writes=[osb_r])
        return osb, osb_r

    out_res = {k: C.res("o_" + k) for k in ("kT", "vT", "msb", "mgd", "gcv", "gcs", "grp", "grs")}

    def store(out_ap, src_ap, src_r, key):
        C.dma("sp", out_ap, src_ap, reads=[src_r], writes=[out_res[key]], part=True, is_out=True, owner=src_r)


    def zt(name, dt):
        t, r = sbt(name, [128, 128], dt)
        op("pool", lambda g: g.memset(t[:, :], 0.0), writes=[r])
        return t, r
    WT0 = [zt("wTm0_%d" % i, BF16) for i in range(2)]
    WT1 = [zt("wTm1_%d" % i, BF16) for i in range(2)]
    QM0 = [zt("qm0_%d" % i, BF16) for i in range(2)]
    QM1 = [zt("qm1_%d" % i, BF16) for i in range(2)]
    KD0 = [zt("kd0_%d" % i, BF16) for i in range(2)]
    KD1 = [zt("kd1_%d" % i, BF16) for i in range(2)]
    gnr = prm[:, P_GNR:P_GNR + 128]

    def mmf(lhsT, rhs, reads, n=128):
        pt, pr = next_ps()
        op("pe", lambda pe: pe.matmul(pt[:, 0:n], lhsT, rhs, start=True, stop=True), reads=reads, writes=[pr])
        return pt, pr

    gstate = {}

    def gdn_wy(b):
        H = gstate[b]
        for nb in range(NBLK):
            while nb - H['scan_done'] > 1:
                PAR.idle()
            cs = slice(nb * 128, (nb + 1) * 128)
            pp_ = nb % 2
            (wTm0, wTm0_r), (wTm1, wTm1_r) = WT0[pp_], WT1[pp_]
            (qm0, qm0_r), (qm1, qm1_r) = QM0[pp_], QM1[pp_]
            (kd0, kd0_r), (kd1, kd1_r) = KD0[pp_], KD1[pp_]
            gcol = gtm[:, nb:nb + 1]
            bcol = btm[:, nb:nb + 1]
            p1, p1_r = next_ps()
            for i, blkc in enumerate((C_UBD, C_BONES, C_SEL0, C_SEL1)):
                op("pe", lambda pe: pe.matmul(p1[:, i:i + 1], cb(blkc), gcol, start=True, stop=True),
                   reads=[cst_r, gtm_r], writes=[p1_r])
            sc, sc_r = rt("sc", [128, 8])
            op("dve", lambda v: v.tensor_copy(sc[:, 0:4], p1[:, 0:4]), reads=[p1_r], writes=[sc_r])
            op("act", lambda a: a.activation(sc[:, 4:5], sc[:, 0:1], AF.Exp), reads=[sc_r], writes=[sc_r])
            op("dve", lambda v: v.tensor_tensor(sc[:, 5:6], sc[:, 1:2], sc[:, 0:1], ALU.subtract), reads=[sc_r], writes=[sc_r])
            op("act", lambda a: a.activation(sc[:, 5:6], sc[:, 5:6], AF.Exp), reads=[sc_r], writes=[sc_r])
            op("dve", lambda v: v.tensor_tensor(sc[:, 6:7], sc[:, 4:5], bcol, ALU.mult), reads=[sc_r, btm_r], writes=[sc_r])
            op("act", lambda a: a.activation(sc[:, 2:4], sc[:, 2:4], AF.Exp), reads=[sc_r], writes=[sc_r])
            Dg, Dg_r = rt("Dg", [128, 128])
            op("dve", lambda v: v.tensor_scalar(Dg[:, :], ident_f, sc[:, 0:1], None, ALU.mult), reads=[cst_r, sc_r], writes=[Dg_r])
            pG, pG_r = mmf(ones_f[:, :], Dg[:, :], [onf_r, Dg_r])
            E1, E1_r = rt("E1", [128, 128])
            op("dve", lambda v: v.scalar_tensor_tensor(E1[:, :], pG[:, 0:128], sc[:, 0:1], cb(C_MPOS), ALU.subtract, ALU.add),
               reads=[pG_r, sc_r, cst_r], writes=[E1_r])
            op("act", lambda a: a.activation(E1[:, :], E1[:, :], AF.Exp, scale=-1.0), reads=[E1_r], writes=[E1_r])
            E2, E2_r = rt("E2", [128, 128])
            op("dve", lambda v: v.scalar_tensor_tensor(E2[:, :], pG[:, 0:128], sc[:, 0:1], cb(C_MNEG), ALU.subtract, ALU.add),
               reads=[pG_r, sc_r, cst_r], writes=[E2_r])
            op("act", lambda a: a.activation(E2[:, :], E2[:, :], AF.Exp), reads=[E2_r], writes=[E2_r])
            pK, pK_r = next_ps()
            op("pe", lambda pe: pe.matmul(pK[:, 0:128], GK[:, cs], GK[:, cs], start=True, stop=True), reads=[GK_r], writes=[pK_r])
            op("pe", lambda pe: pe.matmul(pK[:, 128:256], GK[:, cs], GQ[:, cs], start=True, stop=True), reads=[GK_r, GQ_r], writes=[pK_r])
            Nl, Nl_r = rt("Nl", [128, 128], F32, n=3)
            op("dve", lambda v: v.scalar_tensor_tensor(Nl[:, :], pK[:, 0:128], bcol, E1[:, :], ALU.mult, ALU.mult),
               reads=[pK_r, btm_r, E1_r], writes=[Nl_r])
            QKd, QKd_r = rt("QKd", [128, 128], BF16)
            op("dve", lambda v: v.tensor_tensor(QKd[:, :], pK[:, 128:256], E2[:, :], ALU.mult), reads=[pK_r, E2_r], writes=[QKd_r])
            pM, pM_r = mmf(Nl[:, :], ident_f, [Nl_r, cst_r])
            Ml, Ml_r = rt("Ml", [128, 128], F32, n=3)
            op("act", lambda a: a.copy(Ml[:, :], pM[:, 0:128]), reads=[pM_r], writes=[Ml_r])
            W, W_r = rt("W", [128, 128], F32, n=3)
            op("dve", lambda v: v.tensor_tensor(W[:, :], ident_f, pM[:, 0:128], ALU.subtract), reads=[pM_r, cst_r], writes=[W_r])
            for k in range(1, 6):
                pN, pN_r = mmf(Ml[:, :], Nl[:, :], [Ml_r, Nl_r])
                if k < 5:
                    pM2, pM2_r = mmf(Nl[:, :], Ml[:, :], [Ml_r, Nl_r])
                Nl2, Nl2_r = rt("Nl", [128, 128], F32, n=3)
                op("act", lambda a: a.copy(Nl2[:, :], pN[:, 0:128]), reads=[pN_r], writes=[Nl2_r])
                if k < 5:
                    Ml2, Ml2_r = rt("Ml", [128, 128], F32, n=3)
                    op("dve", lambda v: v.tensor_copy(Ml2[:, :], pM2[:, 0:128]), reads=[pM2_r], writes=[Ml2_r])
                pW, pW_r = mmf(Nl2[:, :], W[:, :], [Nl2_r, W_r])
                W2, W2_r = rt("W", [128, 128], F32, n=3)
                op("dve", lambda v: v.tensor_tensor(W2[:, :], W[:, :], pW[:, 0:128], ALU.add), reads=[W_r, pW_r], writes=[W2_r])
                Nl, Nl_r, W, W_r = Nl2, Nl2_r, W2, W2_r
                if k < 5:
                    Ml, Ml_r = Ml2, Ml2_r
            tpk, tpk_r = next_psb()
            op("pe", lambda pe: pe.transpose(tpk[:, 0:128], GK[:, cs], ident_b[:, :]), reads=[GK_r, idb_r], writes=[tpk_r])
            kbg, kbg_r = rt("kbg", [128, 128])
            op("dve", lambda v: v.tensor_scalar(kbg[:, :], tpk[:, 0:128], sc[:, 6:7], None, ALU.mult), reads=[tpk_r, sc_r], writes=[kbg_r])
            op("act", lambda a: a.activation(kd0[0:64, :], tpk[0:64, 0:128], AF.Copy, scale=sc[0:64, 5:6]), reads=[tpk_r, sc_r, kd0_r], writes=[kd0_r])
            op("act", lambda a: a.activation(kd1[64:128, :], tpk[64:128, 0:128], AF.Copy, scale=sc[64:128, 5:6]), reads=[tpk_r, sc_r, kd1_r], writes=[kd1_r])
            tpv, tpv_r = next_psb()
            op("pe", lambda pe: pe.transpose(tpv[:, 0:128], GV[:, cs], ident_b[:, :]), reads=[GV_r, idb_r], writes=[tpv_r])
            vb_, vb_r = rt("vbt", [128, 128])
            op("dve", lambda v: v.tensor_scalar(vb_[:, :], tpv[:, 0:128], bcol, None, ALU.mult), reads=[tpv_r, btm_r], writes=[vb_r])
            pu, pu_r = mmf(W[:, :], vb_[:, :], [W_r, vb_r])
            u, u_r = rt("u", [128, 128])
            op("act", lambda a: a.copy(u[:, :], pu[:, 0:128]), reads=[pu_r], writes=[u_r])
            pw, pw_r = mmf(kbg[:, :], W[:, :], [kbg_r, W_r])
            op("act", lambda a: a.mul(wTm0[:, 0:64], pw[:, 0:64], -1.0), reads=[pw_r, wTm0_r], writes=[wTm0_r])
            op("dve", lambda v: v.tensor_scalar(wTm1[:, 64:128], pw[:, 64:128], -1.0, None, ALU.mult), reads=[pw_r, wTm1_r], writes=[wTm1_r])
            op("pool", lambda g: g.tensor_copy(qm0[:, 0:64], GQ[:, nb * 128:nb * 128 + 64]), reads=[GQ_r, qm0_r], writes=[qm0_r])
            op("pool", lambda g: g.tensor_copy(qm1[:, 64:128], GQ[:, nb * 128 + 64:nb * 128 + 128]), reads=[GQ_r, qm1_r], writes=[qm1_r])
            H['blk'][nb] = dict(u=(u, u_r), QKd=(QKd, QKd_r), sc=(sc, sc_r), par=nb % 2)
            H['wy_done'] = nb + 1

    def gdn_scan(b):
        H = gstate[b]
        S, S_r = rt("S", [128, 128], F32, n=3)
        Sb, Sb_r = rt("Sb", [128, 128], BF16, n=3)
        op("pool", lambda g: g.memset(S[:, :], 0.0), reads=[S_r], writes=[S_r])
        op("pool", lambda g: g.memset(Sb[:, :], 0.0), reads=[Sb_r], writes=[Sb_r])
        mg, mg_r = None, None
        for nb in range(NBLK):
            while H['wy_done'] <= nb:
                PAR.idle()
            B_ = H['blk'].pop(nb)
            (u, u_r), (QKd, QKd_r), (sc, sc_r) = B_['u'], B_['QKd'], B_['sc']
            pp_ = B_['par']
            (wTm0, wTm0_r), (wTm1, wTm1_r) = WT0[pp_], WT1[pp_]
            (qm0, qm0_r), (qm1, qm1_r) = QM0[pp_], QM1[pp_]
            (kd0, kd0_r), (kd1, kd1_r) = KD0[pp_], KD1[pp_]
            cs = slice(nb * 128, (nb + 1) * 128)
            pa, pa_r = mmf(wTm0[:, :], Sb[:, :], [wTm0_r, Sb_r])
            vn, vn_r = rt("vn", [128, 128])
            op("dve", lambda v: v.tensor_tensor(vn[:, :], u[:, :], pa[:, 0:128], ALU.add), reads=[u_r, pa_r], writes=[vn_r])
            vnb, vnb_r = rt("vnb", [128, 128], BF16)
            op("act", lambda a: a.copy(vnb[:, :], vn[:, :]), reads=[vn_r], writes=[vnb_r])
            pq, pq_r = mmf(qm0[:, :], Sb[:, :], [qm0_r, Sb_r])
            o1, o1_r = rt("o1", [128, 128])
            op("act", lambda a: a.activation(o1[:, :], pq[:, 0:128], AF.Copy, scale=sc[:, 4:5]), reads=[pq_r, sc_r], writes=[o1_r])
            pS1, pS1_r = mmf(kd0[:, :], vnb[:, :], [kd0_r, vnb_r])
            S1, S1_r = rt("S", [128, 128], F32, n=3)
            op("dve", lambda v: v.scalar_tensor_tensor(S1[:, :], S[:, :], sc[:, 2:3], pS1[:, 0:128], ALU.mult, ALU.add),
               reads=[S_r, sc_r, pS1_r], writes=[S1_r])
            S1b, S1b_r = rt("Sb", [128, 128], BF16, n=3)
            op("act", lambda a: a.copy(S1b[:, :], S1[:, :]), reads=[S1_r], writes=[S1b_r])
            pb, pb_r = mmf(wTm1[:, :], S1b[:, :], [wTm1_r, S1b_r])
            vn2, vn2_r = rt("vn", [128, 128])
            op("dve", lambda v: v.tensor_tensor(vn2[:, :], vn[:, :], pb[:, 0:128], ALU.add), reads=[vn_r, pb_r], writes=[vn2_r])
            vn2b, vn2b_r = rt("vnb", [128, 128], BF16)
            op("act", lambda a: a.copy(vn2b[:, :], vn2[:, :]), reads=[vn2_r], writes=[vn2b_r])
            pq2, pq2_r = mmf(qm1[:, :], S1b[:, :], [qm1_r, S1b_r])
            op("act", lambda a: a.activation(o1[64:128, :], pq2[64:128, 0:128], AF.Copy, scale=sc[64:128, 4:5]), reads=[pq2_r, sc_r, o1_r], writes=[o1_r])
            pS2, pS2_r = mmf(kd1[:, :], vn2b[:, :], [kd1_r, vn2b_r])
            S2, S2_r = rt("S", [128, 128], F32, n=3)
            op("dve", lambda v: v.scalar_tensor_tensor(S2[:, :], S1[:, :], sc[:, 3:4], pS2[:, 0:128], ALU.mult, ALU.add),
               reads=[S1_r, sc_r, pS2_r], writes=[S2_r])
            S2b, S2b_r = rt("Sb", [128, 128], BF16, n=3)
            op("act", lambda a: a.copy(S2b[:, :], S2[:, :]), reads=[S2_r], writes=[S2b_r])
            pqk, pqk_r = mmf(QKd[:, :], vn2b[:, :], [QKd_r, vn2b_r])
            op("dve", lambda v: v.tensor_tensor(o1[:, :], o1[:, :], pqk[:, 0:128], ALU.add), reads=[o1_r, pqk_r], writes=[o1_r])
            S, S_r, Sb, Sb_r = S2, S2_r, S2b, S2b_r
            jk, jk_r = rt("jk", [128, 128])
            ssc, ssc_r = rt("ssc", [128, 1])
            op("act", lambda a: a.activation(jk[:, :], o1[:, :], AF.Square, accum_out=ssc[:, :]), reads=[o1_r], writes=[jk_r, ssc_r])
            rs1, rs1_r = rt("rs1", [128, 1])
            rsqrt_to(rs1[:, :], rs1_r, ssc[:, :], [ssc_r], 1.0 / HD, 1)
            on, on_r = rt("on", [128, 128], F32)
            op("dve", lambda v: v.scalar_tensor_tensor(on[:, :], o1[:, :], rs1[:, 0:1], gnr, ALU.mult, ALU.mult),
               reads=[o1_r, rs1_r, prm_r], writes=[on_r])
            tpo, tpo_r = mmf(on[:, :], ident_f, [on_r, cst_r])
            k4 = nb % 4
            if k4 == 0:
                mg, mg_r = rt("mg", [128, 512], F32, 1)
            op("dve", lambda v: v.tensor_tensor(mg[:, k4 * 128:(k4 + 1) * 128], tpo[:, 0:128], ZS[:, cs], ALU.mult),
               reads=[tpo_r, ZS_r, mg_r], writes=[mg_r])
            if k4 == 3:
                store(mgd_o[:, b * T + (nb - 3) * 128:b * T + (nb + 1) * 128], mg[:, :], mg_r, "mgd")
            H['scan_done'] = nb + 1
        store(grp_o[b, :, :], S[:, :], S_r, "grp")


    def sample_part():
        col0 = NB * T
        C.free(QT_r, KT_r, VTM_r, GQ_r, GK_r, GV_r, ZS_r)
        Kg = BIG[:, 0:128 * HD]
        Vg = BIG[:, 128 * HD:2 * 128 * HD]
        Kg_r, Vg_r = C.res("Kg"), C.res("Vg")
        idx, idx_r = sbt("idx", [128, NS], I32)
        C.dma("sp", idx[:, :], ptab_d[:, :], writes=[idx_r])
        gh, gh_r = sbt("gh", [128, 3, 3, NS])
        C.dma("sp", gh[:, :, :, :], ghist_d[:, :, :, :], writes=[gh_r])
        Sall, Sall_r = sbt("Sall", [128, NS, 128])
        for s_ in range(NS):
            C.dma("sp", Sall[:, s_, :], grec_d[s_, :, :], writes=[Sall_r], part=True)
        sm = {}
        t8 = lambda name: sbt(name, [128, NS])

        def dst(cc, pt, pr):
            if cc == 0:
                t_, r_ = t8("s_q")
                op("act", lambda a: a.copy(t_[:, :], pt[:, 0:NS]), reads=[pr], writes=[r_])
                sm["q"] = (t_, r_)
            elif cc in (1, 2):
                t_, r_ = t8("s_kv%d" % cc)
                op("act", lambda a: a.copy(t_[:, :], pt[:, 0:NS]), reads=[pr], writes=[r_])
                store((kT_o if cc == 1 else vT_o)[:, col0:col0 + NS], t_[:, :], r_, "kT" if cc == 1 else "vT")
            elif cc in (3, 4, 5):
                q = cc - 3
                cn, cn_r = t8("s_cn%d" % q)
                op("act", lambda a: a.copy(cn[:, :], pt[:, 0:NS]), reads=[pr], writes=[cn_r])
                store(gcs_o[:, q, 2, :], cn[:, :], cn_r, "gcs")
                C.dma("sp", gcs_o[:, q, 0:2, :], gh[:, q, 1:3, :], reads=[gh_r], writes=[out_res["gcs"]], part=True, is_out=True, owner=C.res("ghst%d" % q))
                y, y_r = t8("s_y%d" % q)
                cw = lambda j: prm[:, P_CW + q * 4 + j:P_CW + q * 4 + j + 1]
                op("dve", lambda v: v.tensor_scalar(y[:, :], gh[:, q, 0, :], cw(0), None, ALU.mult), reads=[gh_r, prm_r], writes=[y_r])
                for j in (1, 2):
                    op("dve", lambda v: v.scalar_tensor_tensor(y[:, :], gh[:, q, j, :], cw(j), y[:, :], ALU.mult, ALU.add),
                       reads=[gh_r, prm_r, y_r], writes=[y_r])
                op("dve", lambda v: v.scalar_tensor_tensor(y[:, :], cn[:, :], cw(3), y[:, :], ALU.mult, ALU.add),
                   reads=[cn_r, prm_r, y_r], writes=[y_r])
                op("act", lambda a: a.activation(y[:, :], y[:, :], AF.Silu), reads=[y_r], writes=[y_r])
                if q == 2:
                    sm["gv"] = (y, y_r)
                else:
                    o_, r_ = t8("s_g%d" % q)
                    l2n_to(o_[:, :], r_, y, y_r, NS, HD ** -0.5 if q == 0 else 1.0)
                    sm["gq" if q == 0 else "gk"] = (o_, r_)
            else:
                t_, r_ = t8("s_zs")
                op("act", lambda a: a.activation(t_[:, :], pt[:, 0:NS], AF.Silu), reads=[pr], writes=[r_])
                sm["zs"] = (t_, r_)

        proj_tile(col0, NS, dst)
        pA, pA_r = next_ps()
        for kc in range(KC):
            op("pe", lambda pe: pe.matmul(pA[0:1, 0:NS], w1[:, kc, 896:897], xn[:, kc, 0:NS], start=(kc == 0), stop=(kc == KC - 1)),
               reads=[w1_r, xn_rs[kc]], writes=[pA_r], inc=(kc == KC - 1))
        row, row_r = sbt("s_row", [1, 2, NS])
        tmp1, tmp1_r = sbt("s_tmp1", [1, NS])
        op("act", lambda a: a.activation(tmp1[:, :], pA[0:1, 0:NS], AF.Exp, bias=prm[0:1, P_DTB:P_DTB + 1]), reads=[pA_r, prm_r], writes=[tmp1_r])
        op("act", lambda a: a.activation(tmp1[:, :], tmp1[:, :], AF.Ln, bias=1.0), reads=[tmp1_r], writes=[tmp1_r])
        op("dve", lambda v: v.tensor_scalar(row[:, 0, :], tmp1[:, :], nexpA[0:1, 0:1], None, ALU.mult), reads=[tmp1_r, nea_r], writes=[row_r])
        pB, pB_r = next_ps()
        for kc in range(KC):
            op("pe", lambda pe: pe.matmul(pB[0:1, 0:NS], w1[:, kc, 897:898], xn[:, kc, 0:NS], start=(kc == 0), stop=(kc == KC - 1)),
               reads=[w1_r, xn_rs[kc]], writes=[pB_r], inc=(kc == KC - 1))
        op("act", lambda a: a.activation(tmp1[:, :], pB[0:1, 0:NS], AF.Exp, scale=-1.0), reads=[pB_r, tmp1_r], writes=[tmp1_r])
        op("dve", lambda v: v.tensor_scalar(tmp1[:, :], tmp1[:, :], 1.0, None, ALU.add), reads=[tmp1_r], writes=[tmp1_r])
        op("dve", lambda v: v.reciprocal(row[:, 1, :], tmp1[:, :]), reads=[tmp1_r, row_r], writes=[row_r])
        pbc, pbc_r = mmf(ones_f[0:1, :], row[0:1, :, :].rearrange("p a n -> p (a n)"), [onf_r, row_r], n=2 * NS)
        bc, bc_r = sbt("s_bc", [128, 3, NS])
        op("dve", lambda v: v.tensor_copy(bc[:, 0:2, :].rearrange("p a n -> p (a n)"), pbc[:, 0:2 * NS]), reads=[pbc_r], writes=[bc_r])
        op("act", lambda a: a.activation(bc[:, 2, :], bc[:, 0, :], AF.Exp), reads=[bc_r], writes=[bc_r])
        egb, beb = bc[:, 2, :], bc[:, 1, :]

        pos, pos_r = next_acc()
        q_, q_r = sm["q"]
        for s_ in range(NS):
            C.indirect(Kg, poolk_d[:, :], bass.IndirectOffsetOnAxis(ap=idx[:, s_:s_ + 1], axis=0), reads=[idx_r], writes=[Kg_r])
            C.indirect(Vg, poolv_d[:, :], bass.IndirectOffsetOnAxis(ap=idx[:, s_:s_ + 1], axis=0), reads=[idx_r], writes=[Vg_r])
            Qrep, Qrep_r = rt("Qrep", [128, 128])
            op("dve", lambda v: v.tensor_scalar(Qrep[:, :], ones_f[:, :], q_[:, s_:s_ + 1], None, ALU.mult), reads=[onf_r, q_r], writes=[Qrep_r])
            pqb, pqb_r = mmf(Qrep[:, :], ident_f, [Qrep_r, cst_r])
            qbc, qbc_r = rt("qbc", [128, 128], BF16)
            op("act", lambda a: a.copy(qbc[:, :], pqb[:, 0:128]), reads=[pqb_r], writes=[qbc_r])
            z, z_r = rt("zs_", [128, 128])
            for r0 in range(0, 128, 8):
                prod, prod_r = rt("prod", [128, 8, 128], F32, 1)
                op("dve", lambda v: v.tensor_tensor(prod[:, :, :], Kg[:, r0 * 128:(r0 + 8) * 128].rearrange("p (r d) -> p r d", d=128),
                                                    qbc[:, :].unsqueeze(1).to_broadcast([128, 8, 128]), ALU.mult),
                   reads=[Kg_r, qbc_r], writes=[prod_r])
                op("dve", lambda v: v.reduce_sum(z[:, r0:r0 + 8], prod[:, :, :], AX.X), reads=[prod_r, z_r], writes=[z_r])
            pzT, pzT_r = mmf(z[:, :], ident_f, [z_r, cst_r])
            e, e_r = rt("se", [128, 128])
            op("act", lambda a: a.activation(e[:, :], pzT[:, 0:128], AF.Exp, bias=prm[:, P_CB:P_CB + 1], scale=SB_SCALE),
               reads=[pzT_r, prm_r], writes=[e_r])
            L, L_r = rt("sL", [128, 128])
            op("act", lambda a: a.activation(L[:, :], e[:, :], AF.Ln, bias=1.0), reads=[e_r], writes=[L_r])
            pS, pS_r = next_ps()
            op("pe", lambda pe: pe.matmul(pS[:, 0:128], cb(C_TRIL), L[:, :], start=True, stop=False), reads=[cst_r, L_r], writes=[pS_r], inc=False)
            pT, pT_r = mmf(L[:, :], ones_f[:, :], [L_r, onf_r])
            Tbt, Tbt_r = rt("Tbt", [128, 128])
            op("act", lambda a: a.copy(Tbt[:, :], pT[:, 0:128]), reads=[pT_r], writes=[Tbt_r])
            op("pe", lambda pe: pe.matmul(pS[:, 0:128], Tbt[:, :], cb(C_SU), start=False, stop=True), reads=[Tbt_r, cst_r], writes=[pS_r])
            w_, w_r = rt("sw", [128, 128])
            op("act", lambda a: a.activation(w_[:, :], pS[:, 0:128], AF.Exp, scale=-1.0), reads=[pS_r], writes=[w_r])
            op("dve", lambda v: v.tensor_tensor(w_[:, :], w_[:, :], e[:, :], ALU.mult), reads=[w_r, e_r], writes=[w_r])
            paT, paT_r = mmf(w_[:, :], ident_f, [w_r, cst_r])
            aT, aT_r = rt("aT", [128, 128], BF16)
            op("act", lambda a: a.copy(aT[:, :], paT[:, 0:128]), reads=[paT_r], writes=[aT_r])
            for r in range(128):
                op("pe", lambda pe: pe.matmul(pos[:, s_:s_ + 1], Vg[:, r * 128:(r + 1) * 128], aT[:, r:r + 1], start=(r == 0), stop=(r == 127)),
                   reads=[Vg_r, aT_r], writes=[pos_r], inc=(r == 127))
        osb, osb_r = headnorm_out(pos, pos_r, NS, msb_o, col0, P_SBN)
        store(msb_o[:, col0:col0 + NS], osb[:, 0:NS], osb_r, "msb")

        gq_, gq_r = sm["gq"]
        gk_, gk_r = sm["gk"]
        gv_, gv_r = sm["gv"]
        zs_, zs_r = sm["zs"]
        pks, pks_r = next_ps()
        for s_ in range(NS):
            op("pe", lambda pe: pe.matmul(pks[:, s_:s_ + 1], Sall[:, s_, :], gk_[:, s_:s_ + 1], start=True, stop=True), reads=[Sall_r, gk_r], writes=[pks_r])
            op("pe", lambda pe: pe.matmul(pks[:, NS + s_:NS + s_ + 1], Sall[:, s_, :], gq_[:, s_:s_ + 1], start=True, stop=True), reads=[Sall_r, gq_r], writes=[pks_r])
        vn, vn_r = t8("s_vn")
        op("dve", lambda v: v.tensor_tensor(vn[:, :], pks[:, 0:NS], egb, ALU.mult), reads=[pks_r, bc_r], writes=[vn_r])
        op("dve", lambda v: v.tensor_tensor(vn[:, :], gv_[:, :], vn[:, :], ALU.subtract), reads=[gv_r, vn_r], writes=[vn_r])
        op("dve", lambda v: v.tensor_tensor(vn[:, :], vn[:, :], beb, ALU.mult), reads=[vn_r, bc_r], writes=[vn_r])
        so, so_r = t8("s_o")
        op("dve", lambda v: v.tensor_tensor(so[:, :], pks[:, NS:2 * NS], egb, ALU.mult), reads=[pks_r, bc_r], writes=[so_r])
        pr_, pr_r = t8("s_pr")
        op("dve", lambda v: v.tensor_tensor(pr_[:, :], gq_[:, :], gk_[:, :], ALU.mult), reads=[gq_r, gk_r], writes=[pr_r])
        pqk, pqk_r = mmf(ones_f[:, :], pr_[:, :], [onf_r, pr_r], n=NS)
        op("dve", lambda v: v.tensor_tensor(pr_[:, :], pqk[:, 0:NS], vn[:, :], ALU.mult), reads=[pqk_r, vn_r, pr_r], writes=[pr_r])
        op("dve", lambda v: v.tensor_tensor(so[:, :], so[:, :], pr_[:, :], ALU.add), reads=[so_r, pr_r], writes=[so_r])
        og, og_r = headnorm_out(so, so_r, NS, mgd_o, col0, P_GNC)
        op("dve", lambda v: v.tensor_tensor(og[:, 0:NS], og[:, 0:NS], zs_[:, :], ALU.mult), reads=[og_r, zs_r], writes=[og_r])
        store(mgd_o[:, col0:col0 + NS], og[:, 0:NS], og_r, "mgd")
        pkr, pkr_r = next_ps()
        op("pe", lambda pe: pe.matmul(pkr[0:NS, 0:128], gk_[:, :], ident_f, start=True, stop=True), reads=[gk_r, cst_r], writes=[pkr_r])
        op("pe", lambda pe: pe.matmul(pkr[0:NS, 128:256], vn[:, :], ident_f, start=True, stop=True), reads=[vn_r, cst_r], writes=[pkr_r])
        kvr, kvr_r = sbt("s_kvr", [NS, 256])
        op("act", lambda a: a.copy(kvr[:, :], pkr[0:NS, 0:256]), reads=[pkr_r], writes=[kvr_r])
        for s_ in range(NS):
            vm, vm_r = rt("s_vm", [NS, 128])
            op("dve", lambda v: v.tensor_scalar(vm[:, :], kvr[:, 128:256], cst[0:NS, C_ID * 128 + s_:C_ID * 128 + s_ + 1], None, ALU.mult),
               reads=[kvr_r, cst_r], writes=[vm_r])
            pO, pO_r = mmf(kvr[:, 0:128], vm[:, :], [kvr_r, vm_r])
            Sn, Sn_r = rt("s_Sn", [128, 128])
            op("dve", lambda v: v.scalar_tensor_tensor(Sn[:, :], Sall[:, s_, :], bc[:, 2, s_:s_ + 1], pO[:, 0:128], ALU.mult, ALU.add),
               reads=[Sall_r, bc_r, pO_r], writes=[Sn_r])
            store(grs_o[s_, :, :], Sn[:, :], Sn_r, "grs")

    def attn_batch(b):
        for G in (range(NTILE) if 'attn' in stages else []):
            racc, racc_r = rt("racc", [128, 512], F32, 1)
            op("dve", lambda g: g.memset(racc[:, :], 0.0), reads=[racc_r], writes=[racc_r])
            po, po_r = next_acc()
            op("pe", lambda pe: pe.matmul(po[:, 0:512], zeros_b[:, 0:128], zeros_b[:, 0:512], start=True, stop=False),
               reads=[zb_r], writes=[po_r], inc=False)
            for jb in range(4 * G + 3, -1, -1):
                c0 = max(0, jb - 4 * G) * 128
                wd = 512 - c0
                pz, pz_r = next_ps()
                op("pe", lambda pe: pe.matmul(pz[:, c0:512], KT[:, jb * 128:(jb + 1) * 128], QT[:, G * 512 + c0:(G + 1) * 512],
                                              start=True, stop=True), reads=[KT_r, QT_r], writes=[pz_r])
                e, e_r = rt("e", [128, 512])
                op("act", lambda a: a.activation(e[:, c0:512], pz[:, c0:512], AF.Exp, bias=prm[:, P_CB:P_CB + 1], scale=SB_SCALE),
                   reads=[pz_r, prm_r], writes=[e_r])
                if jb >= 4 * G:
                    op("dve", lambda g: g.tensor_tensor(e[:, c0:c0 + 128], e[:, c0:c0 + 128], cb(C_MST), ALU.mult),
                       reads=[e_r, cst_r], writes=[e_r])
                L, L_r = rt("L", [128, 512], BF16)
                op("act", lambda a: a.activation(L[:, c0:512], e[:, c0:512], AF.Ln, bias=1.0), reads=[e_r], writes=[L_r])
                pS, pS_r = next_ps()
                op("pe", lambda pe: pe.matmul(pS[:, c0:512], tril_b[:, :], L[:, c0:512], start=True, stop=True),
                   reads=[trb_r, L_r], writes=[pS_r])
                if jb > 0:
                    pR, pR_r = next_ps()
                    op("pe", lambda pe: pe.matmul(pR[:, c0:512], ones_b[:, :], L[:, c0:512], start=True, stop=True),
                       reads=[onb_r, L_r], writes=[pR_r])
                t, t_r = rt("t", [128, 512], F32, 1)
                op("dve", lambda v: v.tensor_tensor(t[:, c0:512], pS[:, c0:512], racc[:, c0:512], ALU.add),
                   reads=[pS_r, racc_r], writes=[t_r])
                op("act", lambda a: a.activation(t[:, c0:512], t[:, c0:512], AF.Exp, scale=-1.0), reads=[t_r], writes=[t_r])
                av, av_r = rt("a", [128, 512], BF16)
                op("dve", lambda g: g.tensor_tensor(av[:, c0:512], e[:, c0:512], t[:, c0:512], ALU.mult),
                   reads=[e_r, t_r], writes=[av_r])
                if jb > 0:
                    op("dve", lambda v: v.tensor_tensor(racc[:, c0:512], racc[:, c0:512], pR[:, c0:512], ALU.add),
                       reads=[pR_r, racc_r], writes=[racc_r])
                op("pe", lambda pe: pe.matmul(po[:, c0:512], VTM[:, jb, :], av[:, c0:512], start=False, stop=(jb == 0)),
                   reads=[VTM_r, av_r], writes=[po_r], inc=True)
            osb, osb_r = headnorm_out(po, po_r, 512, msb_o, 0, P_SBN)
            store(msb_o[:, b * T + G * 512:b * T + (G + 1) * 512], osb[:, :], osb_r, "msb")


    for b in range(NB):
        _sc = nc.named_scope('proj%d' % b); _sc.__enter__()
        for ti in range(NTILE):
            t0 = ti * 512
            gcol = b * T + t0

            def dst(cc, pt, pr, t0=t0, gcol=gcol, ti=ti):
                tn = 512
                if cc == 0:
                    op("act", lambda a: a.copy(QT[:, t0:t0 + tn], pt[:, 0:tn]), reads=[pr], writes=[QT_r])
                elif cc in (1, 2):
                    st, st_r = rt("kvst", [128, 512], F32, 1)
                    op("act", lambda a: a.copy(st[:, :], pt[:, 0:tn]), reads=[pr], writes=[st_r])
                    store((kT_o if cc == 1 else vT_o)[:, gcol:gcol + tn], st[:, :], st_r, "kT" if cc == 1 else "vT")
                    if cc == 1:
                        op("dve", lambda v: v.tensor_copy(KT[:, t0:t0 + tn], pt[:, 0:tn]), reads=[pr], writes=[KT_r])
                    else:
                        vb, vb_r = rt("vbf", [128, 512], BF16, 1)
                        op("dve", lambda v: v.tensor_copy(vb[:, :], pt[:, 0:tn]), reads=[pr], writes=[vb_r])
                        for k in range(4):
                            tp, tp_r = next_psb()
                            op("pe", lambda pe: pe.transpose(tp[:, 0:128], vb[:, k * 128:(k + 1) * 128], ident_b[:, :]),
                               reads=[vb_r, idb_r], writes=[tp_r])
                            op("act", lambda a: a.copy(VTM[:, ti * 4 + k, :], tp[:, 0:128]), reads=[tp_r], writes=[VTM_r])
                elif cc in (3, 4, 5):
                    ci, ci_r = cin[cc - 3]
                    q = cc - 3
                    if ti == 0:
                        op("pool", lambda g: g.memset(ci[:, 0:3], 0.0), reads=[ci_r], writes=[ci_r])
                    else:
                        op("pool", lambda g: g.tensor_copy(ci[:, 0:3], ci[:, 512:515]), reads=[ci_r], writes=[ci_r])
                    op("act", lambda a: a.copy(ci[:, 3:515], pt[:, 0:tn]), reads=[pr], writes=[ci_r])
                    if ti == NTILE - 1:
                        op("pool", lambda g: g.tensor_copy(gcv_s[:, q, b, :], ci[:, 512:515]), reads=[ci_r], writes=[gcv_r])
                    y, y_r = rt("cvy", [128, 512], F32, 1)
                    cw = lambda j: prm[:, P_CW + q * 4 + j:P_CW + q * 4 + j + 1]
                    op("dve", lambda v: v.tensor_scalar(y[:, :], ci[:, 0:512], cw(0), None, ALU.mult), reads=[ci_r, prm_r], writes=[y_r])
                    for j in (1, 2, 3):
                        op("dve", lambda v: v.scalar_tensor_tensor(y[:, :], ci[:, j:j + 512], cw(j), y[:, :], ALU.mult, ALU.add),
                           reads=[ci_r, prm_r, y_r], writes=[y_r])
                    op("act", lambda a: a.activation(y[:, :], y[:, :], AF.Silu), reads=[y_r], writes=[y_r])
                    if q == 0:
                        l2n_to(GQ[:, t0:t0 + tn], GQ_r, y, y_r, tn, HD ** -0.5)
                    elif q == 1:
                        l2n_to(GK[:, t0:t0 + tn], GK_r, y, y_r, tn, 1.0)
                    else:
                        op("dve", lambda v: v.tensor_copy(GV[:, t0:t0 + tn], y[:, :]), reads=[y_r], writes=[GV_r])
                else:
                    op("act", lambda a: a.activation(ZS[:, t0:t0 + tn], pt[:, 0:tn], AF.Silu), reads=[pr], writes=[ZS_r])

            proj_tile(gcol, 512, dst)
            pab, pab_r = next_ps()
            for k in range(4):
                for kc in range(KC):
                    op("pe", lambda pe: pe.matmul(pab[:, 2 * k:2 * k + 2], xn[:, kc, k * 128:(k + 1) * 128], w1[:, kc, 896:898],
                                                  start=(kc == 0), stop=(kc == KC - 1)),
                       reads=[w1_r, xn_rs[kc]], writes=[pab_r], inc=(kc == KC - 1))
            ab, ab_r = rt("ab", [128, 8])
            op("dve", lambda v: v.tensor_copy(ab[:, :], pab[:, 0:8]), reads=[pab_r], writes=[ab_r])
            abv = ab[:, :].rearrange("p (k t) -> p k t", t=2)
            sp4, sp4_r = rt("sp4", [128, 4, 1])
            eb4, eb4_r = rt("eb4", [128, 4, 1])
            op("act", lambda a: a.activation(sp4[:, :, :], abv[:, :, 0:1], AF.Exp, bias=prm[:, P_DTB:P_DTB + 1]), reads=[ab_r, prm_r], writes=[sp4_r])
            op("act", lambda a: a.activation(eb4[:, :, :], abv[:, :, 1:2], AF.Exp, scale=-1.0), reads=[ab_r], writes=[eb4_r])
            op("act", lambda a: a.activation(sp4[:, :, :], sp4[:, :, :], AF.Ln, bias=1.0), reads=[sp4_r], writes=[sp4_r])
            g4 = gtm[:, ti * 4:(ti + 1) * 4].rearrange("p (k o) -> p k o", o=1)
            b4 = btm[:, ti * 4:(ti + 1) * 4].rearrange("p (k o) -> p k o", o=1)
            op("dve", lambda v: v.tensor_scalar(g4, sp4[:, :, :], nexpA[:, 0:1], None, ALU.mult), reads=[sp4_r, nea_r, gtm_r], writes=[gtm_r])
            op("dve", lambda v: v.tensor_scalar(eb4[:, :, :], eb4[:, :, :], 1.0, None, ALU.add), reads=[eb4_r], writes=[eb4_r])
            op("dve", lambda v: v.reciprocal(b4, eb4[:, :, :]), reads=[eb4_r, btm_r], writes=[btm_r])

        _sc.__exit__(None, None, None); _sc = nc.named_scope('mix%d' % b); _sc.__enter__()
        fns = []
        if 'attn' in stages:
            fns.append(lambda: attn_batch(b))
        if 'gdn' in stages:
            gstate[b] = dict(wy_done=0, scan_done=0, blk={})
            fns.append(lambda: gdn_wy(b))
            fns.append(lambda: gdn_scan(b))
        if not fns:
            pass
        elif 'attn' not in stages:
            PAR.run([lambda: None] + fns)
        else:
            PAR.run(fns)
        _sc.__exit__(None, None, None)
    store(gcv_o[:, :, :, :], gcv_s[:, :, :, :], gcv_r, "gcv")
    if do_sample:
        with nc.named_scope('sample'):
            sample_part()
    C.finish()
    S0.close()
    return nc

_O_SB_Q, _O_SB_K, _O_SB_V = 0, 1024, 2048
_O_GDN_QKV = 3072
_O_GDN_Z = _O_GDN_QKV + 3072
_O_GDN_A = _O_GDN_Z + 1024
_O_GDN_B = _O_GDN_A + 8
_NC_CACHE = {}


def _fm(a, kc):
    return np.ascontiguousarray(a.T.reshape(kc, 128, a.shape[0]).transpose(1, 0, 2))


def kernel(x_prompt, x_sample, cache_sb_k, cache_sb_v, page_table, state_gdn_conv,
           state_gdn_rec, state_ffn_conv, p_prompt, p_sample, attn_norm, w_in,
           sb_logit_bias, sb_out_norm, gdn_conv_w, gdn_a_log, gdn_dt_bias, gdn_out_norm,
           w_out, ffn_norm, w_ffn_gate, w_ffn_up, ffn_conv_w, w_ffn_down, ple_norm,
           w_ple_gate, w_ple_proj, final_norm):
    f32 = np.float32
    A = lambda v: np.asarray(v)
    x_prompt, x_sample = A(x_prompt).astype(f32), A(x_sample).astype(f32)
    B, T, _ = x_prompt.shape
    NS = x_sample.shape[0]
    NP = B * T
    w_in0 = A(w_in)[0]
    ck, cv = A(cache_sb_k)[0], A(cache_sb_v)[0]
    NPOOL = ck.shape[0]
    sgc, sgr, sfc = A(state_gdn_conv)[0], A(state_gdn_rec)[0], A(state_ffn_conv)[0]
    xall = np.concatenate([x_prompt.reshape(NP, D), x_sample.reshape(NS, D)], 0)
    xT = _fm(xall, KC)
    cst = make_consts()
    ptabT = np.ascontiguousarray(A(page_table).T.astype(np.int32))
    an = A(attn_norm)[0]
    gcw = A(gdn_conv_w)[0]
    in_maps = []
    for c in range(8):
        hc = slice(c * 128, (c + 1) * 128)
        cols = [w_in0[:, _O_SB_Q:][:, hc], w_in0[:, _O_SB_K:][:, hc], w_in0[:, _O_SB_V:][:, hc],
                w_in0[:, _O_GDN_QKV:][:, hc], w_in0[:, _O_GDN_QKV + 1024:][:, hc], w_in0[:, _O_GDN_QKV + 2048:][:, hc],
                w_in0[:, _O_GDN_Z:][:, hc], w_in0[:, _O_GDN_A + c:_O_GDN_A + c + 1], w_in0[:, _O_GDN_B + c:_O_GDN_B + c + 1]]
        w1 = np.ascontiguousarray(np.concatenate(cols, 1)).astype(f32)
        prm = np.zeros((128, NPRM), f32)
        prm[:, P_CB] = A(sb_logit_bias)[0, c]
        prm[:, P_SBN] = A(sb_out_norm)[0]
        prm[:, P_ALOG] = A(gdn_a_log)[0, c]
        prm[:, P_DTB] = A(gdn_dt_bias)[0, c]
        for q in range(3):
            for j in range(4):
                prm[:, P_CW + q * 4 + j] = gcw[j, q * 1024 + c * 128:q * 1024 + (c + 1) * 128]
        prm[:, P_GNC] = A(gdn_out_norm)[0]
        prm[:, P_GNR:P_GNR + 128] = A(gdn_out_norm)[0][None, :]
        prm[:, P_AN:P_AN + KC] = an.reshape(KC, 128).T
        gh = np.stack([sgc[:, :, q * 1024 + c * 128:q * 1024 + (c + 1) * 128] for q in range(3)], 0)
        in_maps.append(dict(
            xT=xT, w1=w1, prm=prm, cst=cst,
            poolk=np.ascontiguousarray(ck[:, :, c, :].reshape(NPOOL, -1)),
            poolv=np.ascontiguousarray(cv[:, :, c, :].reshape(NPOOL, -1)),
            ptab=ptabT, ghist=np.ascontiguousarray(gh.transpose(3, 0, 2, 1)),
            grec=np.ascontiguousarray(sgr[:, c])))
    key1 = ("l1", T, B, NS, NPOOL)
    if key1 not in _NC_CACHE:
        _NC_CACHE[key1] = build_l1(T=T, NB=B, NS=NS, NPOOL=NPOOL, do_sample=True)
    res1 = run_bass_kernel_spmd(_NC_CACHE[key1], in_maps, core_ids=list(range(8))).results
    del in_maps

    sb_k_p = np.zeros((1, B, T, 8, 128), f32)
    sb_v_p = np.zeros((1, B, T, 8, 128), f32)
    sb_k_s = np.zeros((1, NS, 1, 8, 128), f32)
    sb_v_s = np.zeros((1, NS, 1, 8, 128), f32)
    gconv_p = np.zeros((1, B, 3, 3072), f32)
    gconv_s = np.zeros((1, NS, 3, 3072), f32)
    grec_p = np.zeros((1, B, 8, 128, 128), f32)
    grec_s = np.zeros((1, NS, 8, 128, 128), f32)
    mixT = np.zeros((D, NP + NS), f32)
    for c in range(8):
        r = res1[c]
        kT, vT = np.asarray(r["kT_o"]), np.asarray(r["vT_o"])
        sb_k_p[0, :, :, c, :] = kT[:, :NP].T.reshape(B, T, 128)
        sb_v_p[0, :, :, c, :] = vT[:, :NP].T.reshape(B, T, 128)
        sb_k_s[0, :, 0, c, :] = kT[:, NP:].T
        sb_v_s[0, :, 0, c, :] = vT[:, NP:].T
        gcvo = np.asarray(r["gcv_o"])
        gcso = np.asarray(r["gcs_o"])
        for q in range(3):
            gconv_p[0, :, :, q * 1024 + c * 128:q * 1024 + (c + 1) * 128] = gcvo[:, q].transpose(1, 2, 0)
            gconv_s[0, :, :, q * 1024 + c * 128:q * 1024 + (c + 1) * 128] = gcso[:, q].transpose(2, 1, 0)
        grec_p[0, :, c] = np.asarray(r["grp_o"])
        grec_s[0, :, c] = np.asarray(r["grs_o"])
        mixT[c * 128:(c + 1) * 128] = np.asarray(r["msb_o"])
        mixT[1024 + c * 128:1024 + (c + 1) * 128] = np.asarray(r["mgd_o"])

    NM = (B * T) // 8
    per_b = T // NM
    pall = np.concatenate([A(p_prompt)[0].reshape(NP, -1), A(p_sample)[0].reshape(NS, -1)], 0).astype(f32)
    xallT = np.ascontiguousarray(xall.T)
    pallT = np.ascontiguousarray(pall.T)
    nrm3 = np.ascontiguousarray(np.stack([A(ffn_norm)[0], A(ple_norm)[0], A(final_norm)]).reshape(3, KC, 128).transpose(2, 0, 1)).astype(f32)
    fcw = _fm(A(ffn_conv_w)[0].astype(f32), NJ)
    wts = l2_weight_layouts(A(w_out)[0], A(w_ffn_gate)[0], A(w_ffn_up)[0], A(w_ffn_down)[0], A(w_ple_gate)[0], A(w_ple_proj)[0])
    in_maps = []
    for j in range(8):
        b, s = j // per_b, j % per_b
        g0 = b * T + s * NM
        cols = np.concatenate([np.arange(g0 - 2, g0 + NM), [NP + j]])
        valid = np.ones(NM + 3, bool)
        if s == 0:
            valid[0:2] = False
            cols[0:2] = g0

        def take(MT, kc):
            t = MT[:, cols].copy()
            t[:, ~valid] = 0.0
            return np.ascontiguousarray(t.reshape(kc, 128, NM + 3).transpose(1, 0, 2))
        m = dict(xT=take(xallT, KC), mixT=take(mixT, KC), pT=take(pallT, 2), fhist=_fm(sfc[j].astype(f32), NJ), nrm=nrm3, fcw=fcw)
        m.update(wts)
        in_maps.append(m)
    key2 = ("l2", NM)
    if key2 not in _NC_CACHE:
        _NC_CACHE[key2] = build_l2(NM)
    res2 = run_bass_kernel_spmd(_NC_CACHE[key2], in_maps, core_ids=list(range(8))).results
    y_p = np.zeros((B, T, D), f32)
    y_s = np.zeros((NS, 1, D), f32)
    fconv_p = np.zeros((1, B, 2, DFF), f32)
    fconv_s = np.zeros((1, NS, 2, DFF), f32)
    for j in range(8):
        b, s = j // per_b, j % per_b
        r = res2[j]
        yk = np.asarray(r["yT"]).transpose(2, 1, 0).reshape(NM + 1, D)
        y_p[b, s * NM:(s + 1) * NM] = yk[:NM]
        y_s[j, 0] = yk[NM]
        fc = np.asarray(r["fcnew"]).transpose(2, 1, 0).reshape(4, DFF)
        if s == per_b - 1:
            fconv_p[0, b] = fc[0:2]
        fconv_s[0, j, 0] = fc[3]
        fconv_s[0, j, 1] = fc[2]
    return (y_p, y_s, sb_k_p, sb_v_p, gconv_p, grec_p, fconv_p, sb_k_s, sb_v_s, gconv_s, grec_s, fconv_s)
```
